# Optimizing a Trainium2 kernel written in Bass

```python
import math
import jax
import jax.numpy as jnp
from jax import lax
import numpy as np

D_MODEL = 2048
BATCH = 2
SEQ = 4096
DEPTH = 2
DEC_BATCH = 8
DEC_SEQ = 4
PAST_LEN = 16384
PAGE_SIZE = 128

A_HEADS = 8
A_HEAD_DIM = 128
A_WIDTH = A_HEADS * A_HEAD_DIM
IDX_HEADS = 16
IDX_DIM = 64
TOPK_MAX = 256
QBLK = 64
REL_BUCKETS = 32
REL_MAX_DIST = 128
GLA_HEADS = 4
GLA_DK = 64
GLA_DV = 128
GLA_WIDTH = GLA_HEADS * GLA_DV
GLA_RANK = 16
GLA_TAU = 16.0
RET_HEADS = 4
RET_DK = 64
RET_DV = 128
RET_WIDTH = RET_HEADS * RET_DV
ROPE_BASE = 10000.0

CHUNK = 64
MIX_WIDTH = A_WIDTH + GLA_WIDTH + RET_WIDTH
D_FF = -(-8 * D_MODEL // (3 * 256)) * 256
EPS = 1e-6
IN_SIZES = (A_WIDTH, A_WIDTH, A_WIDTH, IDX_HEADS * IDX_DIM, IDX_DIM, IDX_HEADS,
            GLA_HEADS * GLA_DK, GLA_HEADS * GLA_DK, GLA_WIDTH, GLA_RANK, GLA_WIDTH,
            RET_HEADS * RET_DK, RET_HEADS * RET_DK, RET_WIDTH, RET_WIDTH, 3 * D_MODEL)
IN_TOTAL = sum(IN_SIZES)

kernel_name = 'hybrid_dsa_gla_retnet_step'


def rmsnorm(x, g):
    xf = x.astype(jnp.float32)
    y = xf * lax.rsqrt(jnp.mean(xf * xf, axis=-1, keepdims=True) + EPS)
    return (y * g.astype(jnp.float32)).astype(x.dtype)


def groupnorm_heads(x, g):
    xf = x.astype(jnp.float32)
    xc = xf - jnp.mean(xf, axis=-1, keepdims=True)
    y = xc * lax.rsqrt(jnp.mean(xc * xc, axis=-1, keepdims=True) + EPS)
    return (y * g.astype(jnp.float32)).astype(x.dtype)


def rope(x, pos):
    half = x.shape[-1] // 2
    freqs = ROPE_BASE ** (-jnp.arange(half, dtype=jnp.float32) / half)
    ang = pos.astype(jnp.float32)[:, None] * freqs[None, :]
    cos = jnp.cos(ang)[None, :, None, :]
    sin = jnp.sin(ang)[None, :, None, :]
    xf = x.astype(jnp.float32)
    x1, x2 = xf[..., :half], xf[..., half:]
    return jnp.concatenate([x1 * cos - x2 * sin, x1 * sin + x2 * cos], axis=-1).astype(x.dtype)


def rel_bucket(dist):
    n = jnp.maximum(dist, 0)
    max_exact = REL_BUCKETS // 2
    nf = jnp.maximum(n, 1).astype(jnp.float32)
    large = max_exact + (jnp.log(nf / max_exact) / math.log(REL_MAX_DIST / max_exact)
                         * (REL_BUCKETS - max_exact)).astype(jnp.int32)
    large = jnp.minimum(large, REL_BUCKETS - 1)
    return jnp.where(n < max_exact, n, large)


def split_columns(z):
    points = []
    acc = 0
    for s in IN_SIZES[:-1]:
        acc += s
        points.append(acc)
    return jnp.split(z, points, axis=-1)


def gather_rows(rows, ids):
    return jax.vmap(lambda r, i: r[i])(rows, ids)


def index_select(qi, wi, ki, q_pos, topk):
    s = jax.nn.relu(jnp.einsum('bqhd,bkd->bqhk', qi, ki))
    score = jnp.einsum('bqhk,bqh->bqk', s, wi).astype(jnp.float32)
    k_pos = jnp.arange(ki.shape[1])
    admissible = k_pos[None, :] <= q_pos[:, None]
    score = jnp.where(admissible[None], score, -jnp.inf)
    _, idx = lax.top_k(score, topk)
    valid = idx <= q_pos[None, :, None]
    return idx, valid


def sparse_attend(q, k_sel, v_sel, idx, valid, q_pos, rel_table):
    hd = q.shape[-1]
    logits = jnp.einsum('bqhd,bqkhd->bhqk', q, k_sel).astype(jnp.float32) * (hd ** -0.5)
    bias = rel_table[rel_bucket(q_pos[None, :, None] - idx)]
    logits = logits + bias.astype(jnp.float32).transpose(0, 3, 1, 2)
    logits = jnp.where(valid[:, None], logits, -jnp.inf)
    p = jax.nn.softmax(logits, axis=-1).astype(v_sel.dtype)
    return jnp.einsum('bhqk,bqkhd->bqhd', p, v_sel)


def dsa_prompt(q, k, v, qi, wi, ki, rel_table):
    B, T, H, hd = q.shape
    topk = min(TOPK_MAX, T // 4)

    def block(i):
        s0 = i * QBLK
        q_pos = s0 + jnp.arange(QBLK)
        sl = lambda a: lax.dynamic_slice_in_dim(a, s0, QBLK, axis=1)
        idx, valid = index_select(sl(qi), sl(wi), ki, q_pos, topk)
        return sparse_attend(sl(q), gather_rows(k, idx), gather_rows(v, idx), idx, valid, q_pos, rel_table)

    out = lax.map(block, jnp.arange(T // QBLK))
    return out.transpose(1, 0, 2, 3, 4).reshape(B, T, H, hd)


def dsa_sample(q, k_new, v_new, qi, wi, ki_new, pool_k, pool_v, pool_ki, page_table, rel_table):
    B, Tn = q.shape[:2]
    past = page_table.shape[1] * PAGE_SIZE
    topk = min(TOPK_MAX, (past + Tn) // 4)
    q_pos = past + jnp.arange(Tn)
    ki_past = pool_ki[page_table].reshape(B, past, IDX_DIM)
    ki_all = jnp.concatenate([ki_past, ki_new], axis=1)
    idx, valid = index_select(qi, wi, ki_all, q_pos, topk)
    in_past = (idx < past)[..., None, None]
    pidx = jnp.minimum(idx, past - 1)
    phys = jax.vmap(lambda pt, i: pt[i])(page_table, pidx // PAGE_SIZE)
    row = pidx % PAGE_SIZE
    nidx = jnp.clip(idx - past, 0, Tn - 1)
    k_sel = jnp.where(in_past, pool_k[phys, row], gather_rows(k_new, nidx))
    v_sel = jnp.where(in_past, pool_v[phys, row], gather_rows(v_new, nidx))
    return sparse_attend(q, k_sel, v_sel, idx, valid, q_pos, rel_table)


def chunked_linear_attention(q, k, v, log_a, s0):
    B, T, H, DK = q.shape
    DV = v.shape[-1]
    c = CHUNK if T % CHUNK == 0 else T
    n = T // c

    def to_chunks(a):
        return a.astype(jnp.float32).reshape(B, n, c, H, a.shape[-1]).transpose(1, 0, 3, 2, 4)

    causal = jnp.tril(jnp.ones((c, c), dtype=bool))

    def step(S, blk):
        qb, kb, vb, ab = blk
        b = jnp.cumsum(ab, axis=2)
        o_inter = jnp.einsum('bhtd,bhde->bhte', qb * jnp.exp(b), S)
        diff = jnp.where(causal[None, None, :, :, None],
                         b[:, :, :, None, :] - b[:, :, None, :, :], -jnp.inf)
        scores = jnp.einsum('bhtd,bhsd,bhtsd->bhts', qb, kb, jnp.exp(diff))
        o_intra = jnp.einsum('bhts,bhse->bhte', scores, vb)
        b_end = b[:, :, -1:, :]
        S_new = (jnp.exp(b_end[:, :, 0, :, None]) * S
                 + jnp.einsum('bhsd,bhse->bhde', kb * jnp.exp(b_end - b), vb))
        return S_new, o_inter + o_intra

    s_final, o = lax.scan(step, s0.astype(jnp.float32),
                          (to_chunks(q), to_chunks(k), to_chunks(v), to_chunks(log_a)))
    o = o.transpose(1, 0, 3, 2, 4).reshape(B, T, H, DV).astype(q.dtype)
    return o, s_final


def trunk_layer(x, pos, s_gla0, s_ret0, past_cache, rel_table,
                w_in, a_q_norm, a_k_norm, gla_wa, gla_ba, gla_norm, ret_norm,
                w_branch, w_out, norm_mix, norm_ffn, w_ffn_in, w_ffn_out):
    B, T, _ = x.shape
    heads = lambda a, nh: a.reshape(B, T, nh, -1)
    h = rmsnorm(x, norm_mix)
    z = jnp.einsum('btd,de->bte', h, w_in)
    (aq, ak, av, iq, ik, iw, gq, gk, gv, ga, gg, rq, rk, rv, rg, gates) = split_columns(z)

    aq = rmsnorm(heads(aq, A_HEADS), a_q_norm)
    ak = rmsnorm(heads(ak, A_HEADS), a_k_norm)
    av = heads(av, A_HEADS)
    iq = heads(iq, IDX_HEADS)
    iw = iw * ((IDX_HEADS * IDX_DIM) ** -0.5)
    if past_cache is None:
        o_a = dsa_prompt(aq, ak, av, iq, iw, ik, rel_table)
    else:
        o_a = dsa_sample(aq, ak, av, iq, iw, ik, *past_cache, rel_table)

    log_a_gla = jax.nn.log_sigmoid(jnp.einsum('btr,re->bte', ga, gla_wa) + gla_ba).astype(jnp.float32) / GLA_TAU
    o_b, s_gla = chunked_linear_attention(heads(gq, GLA_HEADS) * (GLA_DK ** -0.5), heads(gk, GLA_HEADS),
                                          heads(gv, GLA_HEADS), heads(log_a_gla, GLA_HEADS), s_gla0)
    o_b = rmsnorm(o_b, gla_norm) * jax.nn.silu(heads(gg, GLA_HEADS))

    rq_r = rope(heads(rq, RET_HEADS), pos)
    rk_r = rope(heads(rk, RET_HEADS), pos) * (RET_DK ** -0.5)
    log_gamma = jnp.log1p(-jnp.exp2(-5.0 - jnp.arange(RET_HEADS, dtype=jnp.float32)))
    log_a_ret = jnp.broadcast_to(log_gamma[None, None, :, None], (B, T, RET_HEADS, RET_DK))
    o_c, s_ret = chunked_linear_attention(rq_r, rk_r, heads(rv, RET_HEADS), log_a_ret, s_ret0)
    o_c = groupnorm_heads(o_c, ret_norm) * jax.nn.silu(heads(rg, RET_HEADS))

    y_a = jnp.einsum('bte,ed->btd', o_a.reshape(B, T, A_WIDTH), w_branch[:A_WIDTH])
    y_b = jnp.einsum('bte,ed->btd', o_b.reshape(B, T, GLA_WIDTH), w_branch[A_WIDTH:A_WIDTH + GLA_WIDTH])
    y_c = jnp.einsum('bte,ed->btd', o_c.reshape(B, T, RET_WIDTH), w_branch[A_WIDTH + GLA_WIDTH:])
    g_a, g_b, g_c = jnp.split(jax.nn.sigmoid(gates), 3, axis=-1)
    merged = g_a * y_a + g_b * y_b + g_c * y_c
    x = x + jnp.einsum('btd,de->bte', merged, w_out)

    h2 = rmsnorm(x, norm_ffn)
    gate, up = jnp.split(jnp.einsum('btd,df->btf', h2, w_ffn_in), 2, axis=-1)
    x = x + jnp.einsum('btf,fd->btd', jax.nn.silu(gate) * up, w_ffn_out)
    return x, (ak, av, ik, s_gla, s_ret)


def setup_inputs(seed: int = 0) -> dict:
    key = jax.random.key(seed)
    ks = jax.random.split(key, 24)
    f32 = jnp.float32
    n_pages = PAST_LEN // PAGE_SIZE
    n_used = DEC_BATCH * n_pages
    n_pool = n_used + max(1, n_used // 4)

    def normal(k, shape, scale=1.0):
        return jax.random.normal(k, shape, f32) * scale

    def gain(k, shape):
        return 1.0 + normal(k, shape, 0.05)

    page_table = jax.random.permutation(ks[7], n_pool)[:n_used].reshape(DEC_BATCH, n_pages).astype(jnp.int32)
    return {
        'x_prompt': normal(ks[0], (BATCH, SEQ, D_MODEL)),
        'x_sample': normal(ks[1], (DEC_BATCH, DEC_SEQ, D_MODEL)),
        'cache_k': normal(ks[2], (DEPTH, n_pool, PAGE_SIZE, A_HEADS, A_HEAD_DIM)),
        'cache_v': normal(ks[3], (DEPTH, n_pool, PAGE_SIZE, A_HEADS, A_HEAD_DIM)),
        'cache_kidx': normal(ks[4], (DEPTH, n_pool, PAGE_SIZE, IDX_DIM)),
        'state_gla': normal(ks[5], (DEPTH, DEC_BATCH, GLA_HEADS, GLA_DK, GLA_DV)),
        'state_ret': normal(ks[6], (DEPTH, DEC_BATCH, RET_HEADS, RET_DK, RET_DV)),
        'page_table': page_table,
        'rel_table': normal(ks[8], (REL_BUCKETS, A_HEADS), 0.1),
        'w_in': normal(ks[9], (DEPTH, D_MODEL, IN_TOTAL), D_MODEL ** -0.5),
        'a_q_norm': gain(ks[10], (DEPTH, A_HEAD_DIM)),
        'a_k_norm': gain(ks[11], (DEPTH, A_HEAD_DIM)),
        'gla_wa': normal(ks[12], (DEPTH, GLA_RANK, GLA_HEADS * GLA_DK), GLA_RANK ** -0.5),
        'gla_ba': normal(ks[13], (DEPTH, GLA_HEADS * GLA_DK), 0.1),
        'gla_norm': gain(ks[14], (DEPTH, GLA_DV)),
        'ret_norm': gain(ks[15], (DEPTH, RET_DV)),
        'w_branch': normal(ks[16], (DEPTH, MIX_WIDTH, D_MODEL), A_WIDTH ** -0.5),
        'w_out': normal(ks[17], (DEPTH, D_MODEL, D_MODEL), D_MODEL ** -0.5),
        'norm_mix': gain(ks[18], (DEPTH, D_MODEL)),
        'norm_ffn': gain(ks[19], (DEPTH, D_MODEL)),
        'w_ffn_in': normal(ks[20], (DEPTH, D_MODEL, 2 * D_FF), D_MODEL ** -0.5),
        'w_ffn_out': normal(ks[21], (DEPTH, D_FF, D_MODEL), D_FF ** -0.5),
    }


def reference(x_prompt, x_sample, cache_k, cache_v, cache_kidx, state_gla, state_ret, page_table,
              rel_table, w_in, a_q_norm, a_k_norm, gla_wa, gla_ba, gla_norm, ret_norm,
              w_branch, w_out, norm_mix, norm_ffn, w_ffn_in, w_ffn_out):
    bp, tp = x_prompt.shape[:2]
    ts = x_sample.shape[1]
    past = page_table.shape[1] * PAGE_SIZE
    pos_p = jnp.arange(tp)
    pos_s = past + jnp.arange(ts)
    zero_gla = jnp.zeros((bp, GLA_HEADS, GLA_DK, GLA_DV), jnp.float32)
    zero_ret = jnp.zeros((bp, RET_HEADS, RET_DK, RET_DV), jnp.float32)
    xp, xs = x_prompt, x_sample
    rows_p = [[], [], [], [], []]
    rows_s = [[], [], [], [], []]
    for l in range(DEPTH):
        weights = (w_in[l], a_q_norm[l], a_k_norm[l], gla_wa[l], gla_ba[l], gla_norm[l], ret_norm[l],
                   w_branch[l], w_out[l], norm_mix[l], norm_ffn[l], w_ffn_in[l], w_ffn_out[l])
        xp, new_p = trunk_layer(xp, pos_p, zero_gla, zero_ret, None, rel_table, *weights)
        xs, new_s = trunk_layer(xs, pos_s, state_gla[l], state_ret[l],
                                (cache_k[l], cache_v[l], cache_kidx[l], page_table), rel_table, *weights)
        for lst, r in zip(rows_p, new_p):
            lst.append(r)
        for lst, r in zip(rows_s, new_s):
            lst.append(r)
    k_p, v_p, kidx_p, gla_p, ret_p = [jnp.stack(r) for r in rows_p]
    k_s, v_s, kidx_s, gla_s, ret_s = [jnp.stack(r) for r in rows_s]
    return (xp, xs, k_p, v_p, kidx_p, gla_p, ret_p, k_s, v_s, kidx_s, gla_s, ret_s)
```

```python
import numpy as np
import ml_dtypes
import concourse.bass as bass
import concourse.mybir as mybir
from concourse.bass_utils import run_bass_kernel_spmd

F32 = mybir.dt.float32
BF16 = mybir.dt.bfloat16
I32 = mybir.dt.int32
U32 = mybir.dt.uint32
AF = mybir.ActivationFunctionType
ALU = mybir.AluOpType
AX = mybir.AxisListType

NCORES = 8
D = 2048
KC = 16
NTILE = 8
NTOK = 1024
NS = 4
NT = NTOK + NS
NTP = 1032
TB = 344
NTB = 3
DEPTH = 2
SEQ = 4096
BATCH = 2
PAST = 16384
NPAGE = 128
NPOOL = 1280
EPS = 1e-6
DFF = 5632
FC = 44
TOPK = 256
NEG = -1.0e30

_sizes = (1024, 1024, 1024, 1024, 64, 16, 256, 256, 512, 16, 512, 256, 256, 512, 512, 6144)
_names = ("aq", "ak", "av", "iq", "ik", "iw", "gq", "gk", "gv", "ga", "gg", "rq", "rk", "rv", "rg", "gates")
OFF = {}
_o = 0
for _n, _s in zip(_names, _sizes):
    OFF[_n] = _o
    _o += _s
IN_TOTAL = _o


def _win_blocks():
    blocks = []
    def add(name, off, n):
        for i in range(n // 128):
            blocks.append((name, i, np.arange(off + i * 128, off + (i + 1) * 128)))
    add("aq", OFF["aq"], 1024)
    add("ak", OFF["ak"], 1024)
    add("av", OFF["av"], 1024)
    add("iq", OFF["iq"], 1024)
    small = -np.ones(128, np.int64)
    small[0:16] = np.arange(OFF["ga"], OFF["ga"] + 16)
    small[32:48] = np.arange(OFF["iw"], OFF["iw"] + 16)
    small[64:128] = np.arange(OFF["ik"], OFF["ik"] + 64)
    blocks.append(("small", 0, small))
    add("gq", OFF["gq"], 256)
    add("gk", OFF["gk"], 256)
    add("gv", OFF["gv"], 512)
    add("gg", OFF["gg"], 512)
    add("rq", OFF["rq"], 256)
    add("rk", OFF["rk"], 256)
    add("rv", OFF["rv"], 512)
    add("rg", OFF["rg"], 512)
    add("gates", OFF["gates"], 6144)
    return blocks


WIN_BLOCKS = _win_blocks()
NWB = len(WIN_BLOCKS)
WB_INDEX = {}
for _i, (_n, _j, _c) in enumerate(WIN_BLOCKS):
    WB_INDEX[(_n, _j)] = _i


class Buf:
    __slots__ = ("name", "w", "r")

    def __init__(self, name):
        self.name = name
        self.w = None
        self.r = {}


ENGS = ("pe", "act", "dve", "pool", "sp")


class Prog:
    def __init__(self, nc, n_dma_sems=40):
        self.nc = nc
        self.q = {e: [] for e in ENGS}
        self.cnt = {}
        self.water = {e: {} for e in ENGS}
        self.semh = {}
        for e in ("pe", "act", "dve", "pool"):
            self.semh[e] = nc.alloc_semaphore("s_" + e)
            self.cnt[e] = 0
        self.dsems = {}
        self.dnext = {}
        for q, n in (("sp", 24), ("act", 12), ("pool", 24)):
            self.dsems[q] = []
            self.dnext[q] = 0
            for i in range(n):
                k = "d%s%d" % (q, i)
                self.semh[k] = nc.alloc_semaphore("s_" + k)
                self.cnt[k] = 0
                self.dsems[q].append(k)
        self.out_toks = []
        self.n_inst = 0

    def _need(self, eng, toks):
        best = {}
        for t in toks:
            if t is None:
                continue
            k, v = t
            if k == "pe" and eng == "pe":
                continue
            if self.water[eng].get(k, 0) >= v:
                continue
            if best.get(k, 0) < v:
                best[k] = v
        for k, v in best.items():
            self.q[eng].append(("wait", k, v))
            self.water[eng][k] = v
            self.n_inst += 1

    def _deps(self, reads, writes):
        toks = []
        for b in reads:
            toks.append(b.w)
        for b in writes:
            toks.append(b.w)
            for k, v in b.r.items():
                toks.append((k, v))
        return toks

    def _mark(self, tok, reads, writes):
        k, v = tok
        for b in reads:
            if b.r.get(k, 0) < v:
                b.r[k] = v
        for b in writes:
            b.w = tok
            b.r = {}

    def op(self, eng, fn, reads=(), writes=()):
        self._need(eng, self._deps(reads, writes))
        self.cnt[eng] += 1
        tok = (eng, self.cnt[eng])
        self.q[eng].append(("op", fn, eng, 1))
        self.n_inst += 1
        self._mark(tok, reads, writes)
        return tok

    def dma(self, queue, fn, reads=(), writes=(), is_output=False):
        k = self.dsems[queue][self.dnext[queue]]
        self.dnext[queue] = (self.dnext[queue] + 1) % len(self.dsems[queue])
        toks = self._deps(reads, writes)
        if self.cnt[k] > 0:
            toks.append((k, self.cnt[k]))
        self._need(queue, toks)
        self.cnt[k] += 16
        tok = (k, self.cnt[k])
        self.q[queue].append(("op", fn, k, 16))
        self.n_inst += 1
        self._mark(tok, reads, writes)
        if is_output:
            self.out_toks.append(tok)
        return tok

    def cc(self, fn, reads=(), writes=()):
        k = "cc%d" % len([s for s in self.semh if s.startswith("cc")])
        self.semh[k] = self.nc.alloc_semaphore("s_" + k)
        self.cnt[k] = 0
        self._need("pool", self._deps(reads, writes))
        self.cnt[k] += 1
        tok = (k, 1)
        self.q["pool"].append(("op", fn, k, 1))
        self.n_inst += 1
        self._mark(tok, reads, writes)
        return tok

    def barrier(self):
        toks = [(k, v) for k, v in self.cnt.items() if v > 0]
        for e in ENGS:
            self._need(e, toks)

    def finish(self):
        best = {}
        for k, v in self.out_toks:
            best[k] = max(best.get(k, 0), v)
        for k, v in best.items():
            self.q["sp"].append(("wait", k, v))

    def emit(self):
        nc = self.nc
        engmap = {"pe": "tensor", "act": "scalar", "dve": "vector", "pool": "gpsimd", "sp": "sync"}
        with nc.Block() as block:
            for e in ENGS:
                items = self.q[e]
                semh = self.semh

                def body(eng, items=items):
                    for it in items:
                        if it[0] == "wait":
                            eng.wait_ge(semh[it[1]], it[2])
                        else:
                            it[1](eng).then_inc(semh[it[2]], it[3])

                getattr(block, engmap[e])(body)


def _wblk(w, cols):
    K = w.shape[0]
    sel = np.zeros((K, 128), np.float32)
    m = cols >= 0
    sel[:, m] = w[:, cols[m]]
    return np.ascontiguousarray(sel.reshape(K // 128, 128, 128).transpose(1, 0, 2))


def _pvec(v):
    return np.ascontiguousarray(v.reshape(-1, 128).T)


XW = 6000


class Builder:
    def __init__(self, stages="all", use_cache=True, use_state=None):
        self.stages = stages
        self.use_cache = use_cache
        self.use_state = use_cache if use_state is None else use_state
        nc = bass.Bass("TRN2", target_bir_lowering=False)
        self.nc = nc
        self.P = Prog(nc)
        self.din = {}
        self.dout = {}
        self._decl()
        self._alloc()

    def inp(self, name, shape, dt=F32):
        t = self.nc.dram_tensor(name, list(shape), dt, kind="ExternalInput")
        self.din[name] = (list(shape), dt)
        return t.ap()

    def outp(self, name, shape, dt=F32):
        t = self.nc.dram_tensor(name, list(shape), dt, kind="ExternalOutput")
        self.dout[name] = (list(shape), dt)
        return t.ap()

    def _decl(self):
        nc = self.nc
        self.xT_d = self.inp("xT", [128, KC, NTP])
        self.win_d = self.inp("win", [DEPTH, NWB, 128, KC, 128])
        self.wbr_d = self.inp("wbr", [DEPTH, 16, 128, KC, 128])
        self.wout_d = self.inp("wout", [DEPTH, 16, 128, KC, 128])
        self.wfi_d = self.inp("wfi", [DEPTH, 2 * FC, 128, KC, 128])
        self.wfo_d = self.inp("wfo", [DEPTH, 16, 128, FC, 128])
        self.vec_d = self.inp("vecs", [128, DEPTH, 36])
        self.wa_d = self.inp("gla_wa", [DEPTH, 16, 256])
        self.ba_d = self.inp("gla_ba", [DEPTH, 1, 256])
        self.rel_d = self.inp("rel_bc", [128, 256])
        self.cm_d = self.inp("cmask", [128, 8, 128])
        self.coef_d = self.inp("coef", [128, 32])
        self.bk_d = self.inp("bkt", [128, 256])
        self.rtab_d = self.inp("rtab", [9, 128, 4, 2, 128])
        self.dret_d = self.inp("dret", [64, 2, 4])
        self.cst_d = self.inp("cst", [128, 4, 128])
        if self.use_cache:
            self.ck_d = [self.inp("cache_k%d" % l, [NPOOL * 128, 1024]) for l in range(DEPTH)]
            self.cv_d = [self.inp("cache_v%d" % l, [NPOOL * 128, 1024]) for l in range(DEPTH)]
            self.cki_d = [self.inp("cache_kidx%d" % l, [NPOOL, 128 * 64]) for l in range(DEPTH)]
            self.pt_d = self.inp("ptab", [128, 1], I32)
            self.bks_d = self.inp("bkt_s", [128, 4, 129])
            self.cms_d = self.inp("cmask_s", [128, 4])
        if self.use_state:
            self.sgla_d = self.inp("s_gla", [DEPTH, 4, 64, 128])
            self.sret_d = self.inp("s_ret", [DEPTH, 4, 64, 128])
        self.yT_o = self.outp("yT", [128, KC, NTP])
        self.kT_o = self.outp("kT", [DEPTH, 128, 8, NTP])
        self.vT_o = self.outp("vT", [DEPTH, 128, 8, NTP])
        self.kiT_o = self.outp("kiT", [DEPTH, 64, NTP])
        self.glap_o = self.outp("gla_p", [DEPTH, BATCH, 4, 64, 128])
        self.retp_o = self.outp("ret_p", [DEPTH, BATCH, 4, 64, 128])
        self.glas_o = self.outp("gla_s", [DEPTH, 4, 64, 128])
        self.rets_o = self.outp("ret_s", [DEPTH, 4, 64, 128])
        dt_ = nc.dram_tensor
        self.payK = [dt_("payK%d" % l, [1024, 1024], BF16) for l in range(DEPTH)]
        self.payV = [dt_("payV%d" % l, [1024, 1024], BF16) for l in range(DEPTH)]
        self.payI = [dt_("payI%d" % l, [64, 1024], BF16) for l in range(DEPTH)]
        self.gK = [dt_("gK%d" % l, [8 * 1024, 1024], BF16) for l in range(DEPTH)]
        self.gV = [dt_("gV%d" % l, [8 * 1024, 1024], BF16) for l in range(DEPTH)]
        self.gI = [dt_("gI%d" % l, [8 * 64, 1024], BF16) for l in range(DEPTH)]
        self.payU = [[dt_("payU%d_%d" % (l, t), [8 * 4 * 64, 128], F32) for t in range(2)] for l in range(DEPTH)]
        self.payD = [[dt_("payD%d_%d" % (l, t), [64, 32], F32) for t in range(2)] for l in range(DEPTH)]
        self.gU = [[dt_("gU%d_%d" % (l, t), [8 * 2048, 128], F32) for t in range(2)] for l in range(DEPTH)]
        self.gD = [[dt_("gD%d_%d" % (l, t), [8 * 64, 32], F32) for t in range(2)] for l in range(DEPTH)]
        self.Gd = dt_("Gd", [8, 128, 9 * 128], BF16)
        self.b_Gd = Buf("Gd")

    def _alloc(self):
        nc = self.nc
        sb = nc.alloc_sbuf_tensor
        self.xT = sb("xT_s", [128, KC, NTP], F32)
        self.b_xT = [Buf("xT%d" % c) for c in range(KC)]
        self.ABCD = sb("abcd", [128, 32, NTP], BF16)
        self.hT = sb("hT_s", [128, KC, NTP], BF16)
        self.b_hT = Buf("hT")
        self.X = sb("X_s", [128, XW], F32)
        f32v = lambda ap: ap.rearrange("p a b -> p (a b)").bitcast(F32)
        self.R_h = f32v(self.hT[:])
        self.R_A = f32v(self.ABCD[:, 0:8, :])
        self.R_B = f32v(self.ABCD[:, 8:16, :])
        self.R_AB = f32v(self.ABCD[:, 0:16, :])
        self.R_ABC = f32v(self.ABCD[:, 0:24, :])
        self.R_C = f32v(self.ABCD[:, 16:24, :])
        self.R_D = f32v(self.ABCD[:, 24:32, :])
        self.R_X = self.X[:]
        self.QT = self.ABCD[:, 0:8, :]
        self.iqT = self.ABCD[:, 8:16, :]
        self.oaT = self.ABCD[:, 16:24, :]
        self.obT = self.ABCD[:, 24:32, :]
        self.mgT = self.ABCD[:, 0:16, :]
        self.b_QT = [Buf("QT%d" % i) for i in range(8)]
        self.b_iqT = [Buf("iqT%d" % i) for i in range(8)]
        self.b_oaT = Buf("oaT")
        self.b_obT = [Buf("obT%d" % i) for i in range(8)]
        self.b_mgT = [Buf("mgT%d" % i) for i in range(16)]
        self.cst = sb("cst_s", [128, 4, 128], F32)
        self.cstb = sb("cstb", [128, 4, 128], BF16)
        self.b_cst = Buf("cst")
        self.vecs = sb("vecs_s", [128, DEPTH, 36], F32)
        self.b_vecs = Buf("vecs")
        self.coef = sb("coef_sb", [128, 32], F32)
        self.b_coef = Buf("coef")
        self.epsb = sb("epsb", [128, 1], F32)
        self.b_eps = Buf("eps")
        self.wbuf = [sb("wbuf%d" % i, [128, KC, 128], BF16) for i in range(3)]
        self.b_wbuf = [Buf("wbuf%d" % i) for i in range(3)]
        self.wnext = 0
        self.ps = [nc.alloc_psum_tensor("ps%d" % i, [128, 512], F32) for i in range(8)]
        self.psb = [p[:, 0:512].bitcast(BF16) for p in self.ps]
        self.b_ps = [Buf("ps%d" % i) for i in range(8)]
        self.iw_tok = sb("iw_tok", [128, 9, 16], F32)
        self.iw_abs = sb("iw_abs", [128, 9, 16], F32)
        self.iw_sgn = sb("iw_sgn", [128, 9, 16], F32)
        self.b_iw = Buf("iw")
        self.KTs = sb("KTs", [128, 8, NS], BF16)
        self.Vs = sb("Vs", [NS, 8, 129], BF16)
        self.b_KTs = Buf("KTs")
        self.b_Vs = Buf("Vs")
        self.b_sqh, self.b_rs = Buf("sqh"), Buf("rs")
        self.smallT, _ = self.carve(self.R_X, 0, [128, NTP], F32)
        self.b_small = Buf("smallT")

    def carve(self, base, off, shape, dt):
        n = int(np.prod(shape[1:]))
        words = n if dt in (F32, I32, U32) else (n + 1) // 2
        ap = base[:, off:off + words]
        if dt != F32:
            ap = ap.bitcast(dt)[:, 0:n]
        if len(shape) == 3:
            ap = ap.rearrange("p (a b) -> p a b", a=shape[1])
        elif len(shape) == 4:
            ap = ap.rearrange("p (a b c) -> p a b c", a=shape[1], b=shape[2])
        return ap, off + words

    def barrier(self):
        self.P.barrier()

    def O(self, eng, name, reads, writes, *args, **kw):
        return self.P.op(eng, lambda e: getattr(e, name)(*args, **kw), reads, writes)

    def DMA(self, q, reads, writes, out, in_, is_output=False):
        return self.P.dma(q, lambda e: e.dma_start(out=out, in_=in_), reads, writes, is_output=is_output)

    def mm(self, out, lhsT, rhs, start, stop, reads, writes):
        return self.P.op("pe", lambda e: e.matmul(out, lhsT=lhsT, rhs=rhs, start=start, stop=stop), reads, writes)

    def tr(self, out, in_, ident, reads, writes):
        return self.P.op("pe", lambda e: e.transpose(out, in_, ident), reads, writes)

    def next_w(self):
        i = self.wnext
        self.wnext = (self.wnext + 1) % len(self.wbuf)
        return self.wbuf[i], self.b_wbuf[i]

    def setup(self):
        self.DMA("sp", [], self.b_xT, self.xT[:], self.xT_d)
        self.DMA("sp", [], [self.b_cst], self.cst[:], self.cst_d)
        self.DMA("sp", [], [self.b_vecs], self.vecs[:], self.vec_d)
        self.DMA("sp", [], [self.b_coef], self.coef[:], self.coef_d)
        self.O("dve", "tensor_copy", [self.b_cst], [self.b_cst], out=self.cstb[:], in_=self.cst[:])
        self.O("pool", "memset", [], [self.b_eps], self.epsb[:], EPS)
        self.O("pool", "memset", [], [self.b_Vs], self.Vs[:], 1.0)
        self.O("pool", "memset", [], [self.b_iw], self.iw_tok[:], 0.0)
        self.ident_f = self.cst[:, 0, :]
        self.tri_f = self.cst[:, 1, :]
        self.perm_f = self.cst[:, 2, :]
        self.ones_f = self.cst[:, 3, :]
        self.ident_b = self.cstb[:, 0, :]
        self.tri_b = self.cstb[:, 1, :]
        self.ones_b = self.cstb[:, 3, :]

    def rmsnorm_x(self, l, which):
        sq, _ = self.carve(self.R_AB, 0, [128, KC, NTP], BF16)
        rstd, _ = self.carve(self.R_X, 1548, [128, NTB, TB], F32)
        b_sq = Buf("sq")
        b_rstd = Buf("rstd")
        self.O("act", "activation", self.b_xT, [b_sq], out=sq, in_=self.xT[:], func=AF.Square)
        for tb in range(NTB):
            for c in range(KC):
                self.mm(self.ps[tb][:, 0:TB], self.ones_b, sq[:, c, tb * TB:(tb + 1) * TB], c == 0, c == KC - 1,
                        [self.b_cst, b_sq], [self.b_ps[tb]])
        for tb in range(NTB):
            self.O("act", "activation", [self.b_ps[tb], self.b_eps], [b_rstd], out=rstd[:, tb, :], in_=self.ps[tb][:, 0:TB],
                   func=AF.Ln, scale=1.0 / D, bias=self.epsb[:])
        self.O("act", "activation", [b_rstd], [b_rstd], out=rstd, in_=rstd, func=AF.Exp, scale=-0.5)
        r2 = rstd.rearrange("p a b -> p (a b)")
        for c in range(KC):
            self.O("dve", "scalar_tensor_tensor", [self.b_xT[c], self.b_vecs, b_rstd], [self.b_hT],
                   out=self.hT[:, c, :], in0=self.xT[:, c, :], scalar=self.vecs[:, l, which * 16 + c:which * 16 + c + 1],
                   in1=r2, op0=ALU.mult, op1=ALU.mult)

    def dense(self, w_ap, nk, rhs_of, rhs_bufs, banks, ncols=NTB, width=TB, wtile=None):
        if wtile is None:
            wb, b_wb = self.next_w()
        else:
            wb, b_wb = wtile
        self.P.dma("pool", lambda e: e.dma_start(out=wb[:, 0:nk, :], in_=w_ap), [], [b_wb])
        for tb in range(ncols):
            for k in range(nk):
                self.mm(self.ps[banks[tb]][:, 0:width], wb[:, k, :], rhs_of(k, tb), k == 0, k == nk - 1,
                        [b_wb] + rhs_bufs, [self.b_ps[banks[tb]]])

    def h_rhs(self, k, tb):
        return self.hT[:, k, tb * TB:(tb + 1) * TB]

    def head_rms(self, banks, gcol, out_ap, b_out, l, statbanks):
        sqh, _ = self.carve(self.R_X, 1032, [128, TB], BF16)
        rs, _ = self.carve(self.R_X, 1204, [128, TB], F32)
        for tb in range(NTB):
            b_sq, b_rs = self.b_sqh, self.b_rs
            sbk = statbanks[tb % len(statbanks)]
            self.O("act", "activation", [self.b_ps[banks[tb]]], [b_sq], out=sqh, in_=self.ps[banks[tb]][:, 0:TB], func=AF.Square)
            self.mm(self.ps[sbk][:, 0:TB], self.ones_b, sqh, True, True, [self.b_cst, b_sq], [self.b_ps[sbk]])
            self.O("act", "activation", [self.b_ps[sbk], self.b_eps], [b_rs], out=rs, in_=self.ps[sbk][:, 0:TB],
                   func=AF.Ln, scale=1.0 / 128, bias=self.epsb[:])
            self.O("act", "activation", [b_rs], [b_rs], out=rs, in_=rs, func=AF.Exp, scale=-0.5)
            self.O("dve", "scalar_tensor_tensor", [self.b_ps[banks[tb]], self.b_vecs, b_rs], [b_out],
                   out=out_ap[:, tb * TB:(tb + 1) * TB], in0=self.ps[banks[tb]][:, 0:TB], scalar=self.vecs[:, l, gcol:gcol + 1],
                   in1=rs, op0=ALU.mult, op1=ALU.mult)

    def evac(self, banks, out_ap, b_out, engs=("dve", "act", "dve")):
        for tb in range(NTB):
            e = engs[tb]
            o = out_ap[:, tb * TB:(tb + 1) * TB]
            i = self.ps[banks[tb]][:, 0:TB]
            if e == "act":
                self.O("act", "copy", [self.b_ps[banks[tb]]], [b_out], out=o, in_=i)
            else:
                self.O(e, "tensor_copy", [self.b_ps[banks[tb]]], [b_out], out=o, in_=i)

    def phase_kv(self, l):
        P = self.P
        knf, knb, vf, vb, vtok = [], [], [], [], []
        o = 0
        for i in range(2):
            a, o = self.carve(self.R_C, o, [128, NTP], F32); knf.append(a)
        for i in range(2):
            a, o = self.carve(self.R_C, o, [128, NTP], BF16); knb.append(a)
        o = 0
        for i in range(2):
            a, o = self.carve(self.R_D, o, [128, NTP], F32); vf.append(a)
        for i in range(2):
            a, o = self.carve(self.R_D, o, [128, NTP], BF16); vb.append(a)
        for i in range(2):
            a, o = self.carve(self.R_D, o, [128, 4, 128], BF16); vtok.append(a)
        b_knf = [Buf("knf%d" % i) for i in range(2)]
        b_knb = [Buf("knb%d" % i) for i in range(2)]
        b_vf = [Buf("vf%d" % i) for i in range(2)]
        b_vb = [Buf("vb%d" % i) for i in range(2)]
        b_vtok = [Buf("vtok%d" % i) for i in range(2)]
        self.b_payK = [Buf("payK%d" % h) for h in range(8)]
        self.b_payV = [Buf("payV%d" % i) for i in range(16)]
        self.b_payI = Buf("payI")
        payVv = self.payV[l].ap().rearrange("(t p) c -> p t c", p=128)
        for h in range(8):
            s = h % 2
            banks = [0, 1, 2] if s == 0 else [3, 4, 5]
            self.dense(self.win_d[l, WB_INDEX[("ak", h)]], KC, self.h_rhs, [self.b_hT], banks)
            self.head_rms(banks, 33, knf[s], b_knf[s], l, [6, 7])
            self.DMA("sp", [b_knf[s]], [], self.kT_o[l, :, h, :], knf[s], is_output=True)
            self.O("act", "copy", [b_knf[s]], [b_knb[s]], out=knb[s], in_=knf[s])
            self.DMA("sp", [b_knb[s]], [self.b_payK[h]], self.payK[l].ap()[h * 128:(h + 1) * 128, :], knb[s][:, 0:NTOK])
            self.O("pool", "tensor_copy", [b_knb[s]], [self.b_KTs], out=self.KTs[:, h, :], in_=knb[s][:, NTOK:NT])
        for h in range(8):
            s = h % 2
            banks = [0, 1, 2] if s == 0 else [3, 4, 5]
            self.dense(self.win_d[l, WB_INDEX[("av", h)]], KC, self.h_rhs, [self.b_hT], banks)
            self.evac(banks, vf[s], b_vf[s])
            self.DMA("sp", [b_vf[s]], [], self.vT_o[l, :, h, :], vf[s], is_output=True)
            self.O("pool", "tensor_copy", [b_vf[s]], [b_vb[s]], out=vb[s], in_=vf[s])
            for g in range(2):
                pb = 6 + g
                for jj in range(4):
                    lt = g * 4 + jj
                    self.tr(self.psb[pb][:, jj * 128:(jj + 1) * 128], vb[s][:, lt * 128:(lt + 1) * 128], self.ident_b,
                            [b_vb[s], self.b_cst], [self.b_ps[pb]])
                self.O("dve", "tensor_copy", [self.b_ps[pb]], [b_vtok[g]], out=vtok[g],
                       in_=self.psb[pb][:, 0:512].rearrange("p (a b) -> p a b", a=4))
                self.DMA("sp", [b_vtok[g]], [self.b_payV[h * 2 + g]], payVv[:, g * 4:(g + 1) * 4, h * 128:(h + 1) * 128], vtok[g])
            self.tr(self.psb[6][0:NS, 512:640], vb[s][:, NTOK:NT], self.ident_b, [b_vb[s], self.b_cst], [self.b_ps[6]])
            self.O("dve", "tensor_copy", [self.b_ps[6]], [self.b_Vs], out=self.Vs[:, h, 0:128], in_=self.psb[6][0:NS, 512:640])
        banks = [0, 1, 2]
        self.dense(self.win_d[l, WB_INDEX[("small", 0)]], KC, self.h_rhs, [self.b_hT], banks)
        self.evac(banks, self.smallT, self.b_small)
        self.DMA("sp", [self.b_small], [], self.kiT_o[l], self.smallT[64:128, :], is_output=True)
        ikb, _ = self.carve(self.R_C, 3096, [128, NTP], BF16)
        b_ikb = Buf("ikb")
        P.op("act", lambda e: e.copy(out=ikb[64:128, :], in_=self.smallT[64:128, :]), [self.b_small], [b_ikb])
        self.DMA("sp", [b_ikb], [self.b_payI], self.payI[l].ap(), ikb[64:128, 0:NTOK])
        for lt in range(9):
            n = 128 if lt < 8 else NS
            self.tr(self.ps[7][0:n, lt * 16:(lt + 1) * 16], self.smallT[32:48, lt * 128:lt * 128 + n], self.ident_f[32:48, 32:48],
                    [self.b_small, self.b_cst], [self.b_ps[7]])
        self.O("dve", "tensor_copy", [self.b_ps[7]], [self.b_iw], out=self.iw_tok[:, 0:8, :],
               in_=self.ps[7][:, 0:128].rearrange("p (a b) -> p a b", a=8))
        self.O("dve", "tensor_copy", [self.b_ps[7]], [self.b_iw], out=self.iw_tok[0:NS, 8, :], in_=self.ps[7][0:NS, 128:144])
        self.O("act", "activation", [self.b_iw], [self.b_iw], out=self.iw_abs[:], in_=self.iw_tok[:], func=AF.Abs)
        self.O("act", "activation", [self.b_iw], [self.b_iw], out=self.iw_sgn[:], in_=self.iw_tok[:], func=AF.Sign)
        self.b_gK, self.b_gV, self.b_gI = Buf("gK"), Buf("gV"), Buf("gI")
        rg = [list(range(NCORES))]
        for pay, g, br, bw in ((self.payK[l], self.gK[l], self.b_payK, self.b_gK), (self.payV[l], self.gV[l], self.b_payV, self.b_gV),
                               (self.payI[l], self.gI[l], [self.b_payI], self.b_gI)):
            P.cc(lambda e, pay=pay, g=g: e.collective_compute("AllGather", ALU.bypass, replica_groups=rg,
                                                              ins=[pay.ap().opt()], outs=[g.ap().opt()]), br, [bw])

    def phase_q(self, l):
        for h in range(8):
            banks = [0, 1, 2] if h % 2 == 0 else [3, 4, 5]
            self.dense(self.win_d[l, WB_INDEX[("aq", h)]], KC, self.h_rhs, [self.b_hT], banks)
            self.head_rms(banks, 32, self.QT[:, h, :], self.b_QT[h], l, [6, 7])
        for i in range(8):
            banks = [0, 1, 2] if i % 2 == 0 else [3, 4, 5]
            self.dense(self.win_d[l, WB_INDEX[("iq", i)]], KC, self.h_rhs, [self.b_hT], banks)
            self.evac(banks, self.iqT[:, i, :], self.b_iqT[i], engs=("dve", "act", "pool") if False else ("dve", "act", "dve"))

    def setup_bias(self):
        o = 0
        acc, o = self.carve(self.R_h, o, [128, 8, 256], F32)
        bk, o = self.carve(self.R_h, o, [128, 256], F32)
        tb_, o = self.carve(self.R_h, o, [128, 256], F32)
        mb, o = self.carve(self.R_h, o, [128, 256], F32)
        neg, o = self.carve(self.R_h, o, [128, 8], F32)
        tmp, o = self.carve(self.R_h, o, [128, 128], F32)
        gs = []
        for i in range(2):
            a, o = self.carve(self.R_h, o, [128, 9, 128], BF16)
            gs.append(a)
        b_acc, b_bk, b_tb, b_mb, b_neg, b_tmp = Buf("acc"), Buf("bk"), Buf("tb"), Buf("mb"), Buf("neg"), Buf("tmp")
        b_gs = [Buf("gs0"), Buf("gs1")]
        self.DMA("sp", [], [b_bk], bk, self.bk_d)
        self.DMA("sp", [], [b_tb], tb_, self.rel_d)
        for b in range(32):
            self.O("dve", "tensor_scalar", [b_bk], [b_mb], out=mb, in0=bk, scalar1=float(b), scalar2=None, op0=ALU.is_equal)
            for h in range(8):
                col = tb_[:, b * 8 + h:b * 8 + h + 1]
                if b == 0:
                    self.O("dve", "tensor_scalar", [b_mb, b_tb], [b_acc], out=acc[:, h, :], in0=mb, scalar1=col, scalar2=None, op0=ALU.mult)
                else:
                    self.O("dve", "scalar_tensor_tensor", [b_mb, b_tb, b_acc], [b_acc], out=acc[:, h, :], in0=mb, scalar=col,
                           in1=acc[:, h, :], op0=ALU.mult, op1=ALU.add)
        self.O("dve", "tensor_scalar", [b_tb], [b_neg], out=neg, in0=tb_[:, 248:256], scalar1=-1.0, scalar2=None, op0=ALU.mult)
        for h in range(8):
            self.O("act", "activation", [b_acc, b_neg], [b_acc], out=acc[:, h, :], in_=acc[:, h, :], func=AF.Exp, bias=neg[:, h:h + 1], scale=1.0)
        self.O("dve", "tensor_scalar", [b_acc], [b_acc], out=acc, in0=acc, scalar1=-1.0, scalar2=None, op0=ALU.add)
        for h in range(8):
            g = gs[h % 2]
            for idx in range(9):
                self.O("dve", "tensor_scalar", [b_acc, self.b_coef], [b_tmp], out=tmp, in0=acc[:, h, 0:128], scalar1=self.coef[:, idx:idx + 1],
                       scalar2=1.0, op0=ALU.mult, op1=ALU.add)
                self.O("dve", "scalar_tensor_tensor", [b_acc, self.b_coef, b_tmp], [b_gs[h % 2]], out=g[:, idx, :], in0=acc[:, h, 128:256],
                       scalar=self.coef[:, 9 + idx:10 + idx], in1=tmp, op0=ALU.mult, op1=ALU.add)
            self.DMA("sp", [b_gs[h % 2]], [self.b_Gd], self.Gd.ap()[h], g.rearrange("p a b -> p (a b)"))

    def attn_prompt(self, l, n_iter=16):
        P = self.P
        gIv = self.gI[l].ap().rearrange("(r d) t -> d r t", d=64)
        gKv = self.gK[l].ap().rearrange("(r hd) t -> hd r t", hd=1024)
        gVv = self.gV[l].ap().rearrange("(r t p) c -> p r t c", r=8, t=8, p=128)
        cm, _ = self.carve(self.R_X, 1548, [128, 8 * 128], F32)
        b_cm = Buf("cm")
        self.DMA("sp", [], [b_cm], cm, self.cm_d.rearrange("p a b -> p (a b)"))
        maskT, _ = self.carve(self.R_D, 0, [128, 32, 128], BF16)
        for b in range(BATCH):
            for m in range(4):
                lt = b * 4 + m
                cols = slice(lt * 128, (lt + 1) * 128)
                nkt = 8 * m + 8
                L = nkt * 128
                sc, _ = self.carve(self.R_h, 0, [128, 4096], F32)
                junk, _ = self.carve(self.R_h, 4096, [128, 4096], BF16)
                ik, _ = self.carve(self.R_h, 6144, [128, 32, 128], BF16)
                tmp = [self.carve(self.R_X, 2580 + 512 * i, [128, 512], F32)[0] for i in range(2)]
                tiny, _ = self.carve(self.R_X, 3604, [128, 8], F32)
                lo, hi, mid, cnt, ge, t1, t2 = [tiny[:, i:i + 1] for i in range(7)]
                b_sc = [Buf("sc%d" % i) for i in range(8)]
                b_junk, b_ik, b_tiny = Buf("junk"), Buf("ik"), Buf("tiny")
                b_tmp = [Buf("tmp0"), Buf("tmp1")]
                b_maskT = Buf("maskT")
                for mm in range(m + 1):
                    for dup in range(2):
                        self.DMA("sp" if dup == 0 else "act", [self.b_gI], [b_ik], ik[dup * 64:(dup + 1) * 64, mm * 8:(mm + 1) * 8, :],
                                 gIv[:, :, (4 * b + mm) * 128:(4 * b + mm + 1) * 128])
                k = 0
                for kb in range(nkt // 4):
                    for h in range(16):
                        bank = k % 4
                        tb_ = tmp[k % 2]
                        k += 1
                        hp = (h % 2) * 64
                        self.mm(self.ps[bank][:, 0:512], self.iqT[hp:hp + 64, h // 2, cols], ik[hp:hp + 64, kb * 4:(kb + 1) * 4, :], True, True,
                                [self.b_iqT[h // 2], b_ik], [self.b_ps[bank]])
                        self.O("act", "activation", [self.b_ps[bank], self.b_iw], [b_tmp[k % 2]], out=tb_, in_=self.ps[bank][:, 0:512],
                               func=AF.Relu, scale=self.iw_abs[:, lt, h:h + 1])
                        dst = sc[:, kb * 512:(kb + 1) * 512]
                        if h == 0:
                            self.O("dve", "tensor_scalar", [b_tmp[k % 2], self.b_iw], [b_sc[kb]], out=dst, in0=tb_, scalar1=self.iw_sgn[:, lt, h:h + 1],
                                   scalar2=None, op0=ALU.mult)
                        else:
                            self.O("dve", "scalar_tensor_tensor", [b_tmp[k % 2], self.b_iw, b_sc[kb]], [b_sc[kb]], out=dst, in0=tb_,
                                   scalar=self.iw_sgn[:, lt, h:h + 1], in1=dst, op0=ALU.mult, op1=ALU.add)
                allsc = b_sc[0:nkt // 4]
                self.O("dve", "tensor_reduce", allsc, [b_tiny], out=lo, in_=sc[:, 0:L], axis=AX.X, op=ALU.min)
                last = sc[:, L - 1024:L]
                self.O("dve", "tensor_tensor", allsc + [b_cm], allsc, out=last, in0=last, in1=cm, op=ALU.add)
                self.O("dve", "tensor_reduce", allsc, [b_tiny], out=hi, in_=sc[:, 0:L], axis=AX.X, op=ALU.max)
                for it in range(n_iter):
                    self.O("dve", "tensor_scalar", [b_tiny], [b_tiny], out=mid, in0=lo, scalar1=0.5, scalar2=None, op0=ALU.mult)
                    self.O("dve", "scalar_tensor_tensor", [b_tiny], [b_tiny], out=mid, in0=hi, scalar=0.5, in1=mid, op0=ALU.mult, op1=ALU.add)
                    self.O("dve", "tensor_scalar", allsc + [b_tiny], [b_junk, b_tiny], out=junk[:, 0:L], in0=sc[:, 0:L], scalar1=mid, scalar2=None,
                           op0=ALU.is_ge, op1=ALU.add, accum_out=cnt)
                    self.O("dve", "tensor_scalar", [b_tiny], [b_tiny], out=ge, in0=cnt, scalar1=float(TOPK) - 0.5, scalar2=None, op0=ALU.is_ge)
                    self.O("dve", "tensor_tensor", [b_tiny], [b_tiny], out=t1, in0=mid, in1=lo, op=ALU.subtract)
                    self.O("dve", "tensor_tensor", [b_tiny], [b_tiny], out=t2, in0=hi, in1=mid, op=ALU.subtract)
                    self.O("dve", "scalar_tensor_tensor", [b_tiny], [b_tiny], out=lo, in0=t1, scalar=ge, in1=lo, op0=ALU.mult, op1=ALU.add)
                    self.O("dve", "scalar_tensor_tensor", [b_tiny], [b_tiny], out=hi, in0=t2, scalar=ge, in1=mid, op0=ALU.mult, op1=ALU.add)
                self.O("dve", "tensor_scalar", allsc + [b_tiny], [b_junk], out=junk[:, 0:L], in0=sc[:, 0:L], scalar1=lo, scalar2=None, op0=ALU.is_ge)
                for g in range(nkt // 8):
                    bank = 4 + (g % 2)
                    for jj in range(8):
                        j = g * 8 + jj
                        self.tr(self.psb[bank][:, jj * 128:(jj + 1) * 128], junk[:, j * 128:(j + 1) * 128], self.ident_b, [b_junk, self.b_cst], [self.b_ps[bank]])
                    src = self.psb[bank][:, 0:1024].rearrange("p (a b) -> p a b", a=8)
                    if g % 2 == 0:
                        self.O("act", "copy", [self.b_ps[bank]], [b_maskT], out=maskT[:, g * 8:(g + 1) * 8, :], in_=src)
                    else:
                        self.O("pool", "tensor_copy", [self.b_ps[bank]], [b_maskT], out=maskT[:, g * 8:(g + 1) * 8, :], in_=src) if False else \
                            self.O("dve", "tensor_copy", [self.b_ps[bank]], [b_maskT], out=maskT[:, g * 8:(g + 1) * 8, :], in_=src)
                self.barrier()
                KTb = [self.carve(self.R_h, 2048 * i, [128, 32, 128], BF16)[0] for i in range(2)]
                Vb = [self.carve(self.R_h, 4096 + 2064 * i, [128, 32, 129], BF16)[0] for i in range(2)]
                pt = [self.carve(self.R_D, 2048 + 256 * i, [128, 4, 128], BF16)[0] for i in range(2)]
                oa, _ = self.carve(self.R_D, 2560, [128, 1024], BF16)
                mE, _ = self.carve(self.R_D, 3072, [128, 12, 128], BF16)
                Gb = [self.carve(self.R_X, 2580 + 576 * i, [128, 9, 128], BF16)[0] for i in range(2)]
                rec, _ = self.carve(self.R_X, 3740, [128, 2], F32)
                b_KT, b_V, b_pt, b_G = [Buf("KT0"), Buf("KT1")], [Buf("V0"), Buf("V1")], [Buf("pt0"), Buf("pt1")], [Buf("G0"), Buf("G1")]
                b_oa, b_mE, b_rec = Buf("oa"), Buf("mE"), Buf("rec")
                for i in range(2):
                    self.O("pool", "memset", [], [b_V[i]], Vb[i][:, :, 128:129], 1.0)
                for h in range(8):
                    s = h % 2
                    for mm in range(m + 1):
                        self.DMA("sp", [self.b_gK], [b_KT[s]], KTb[s][:, mm * 8:(mm + 1) * 8, :],
                                 gKv[h * 128:(h + 1) * 128, :, (4 * b + mm) * 128:(4 * b + mm + 1) * 128])
                        self.DMA("act", [self.b_gV], [b_V[s]], Vb[s][:, mm * 8:(mm + 1) * 8, 0:128], gVv[:, :, 4 * b + mm, h * 128:(h + 1) * 128])
                    self.DMA("sp", [self.b_Gd], [b_G[s]], Gb[s].rearrange("p a b -> p (a b)"), self.Gd.ap()[h])
                    if m >= 1:
                        self.O("pool", "tensor_copy", [b_maskT], [b_mE], out=mE[:, 0:3, :], in_=maskT[:, 8 * m - 4:8 * m - 1, :])
                        self.O("pool", "tensor_tensor", [b_maskT, b_G[s]], [b_mE], out=mE[:, 3:12, :], in0=maskT[:, 8 * m - 1:8 * m + 8, :], in1=Gb[s], op=ALU.mult)
                    else:
                        self.O("pool", "tensor_tensor", [b_maskT, b_G[s]], [b_mE], out=mE[:, 4:12, :], in0=maskT[:, 0:8, :], in1=Gb[s][:, 1:9, :], op=ALU.mult)
                    pob = 2 + s
                    ngr = nkt // 4
                    for kg in range(ngr):
                        bank = kg % 2
                        for jj in range(4):
                            self.mm(self.ps[bank][:, jj * 128:(jj + 1) * 128], KTb[s][:, kg * 4 + jj, :], self.QT[:, h, cols], True, True,
                                    [b_KT[s], self.b_QT[h]], [self.b_ps[bank]])
                        p_ = pt[kg % 2]
                        self.O("act", "activation", [self.b_ps[bank]], [b_pt[kg % 2]], out=p_, in_=self.ps[bank][:, 0:512].rearrange("p (a b) -> p a b", a=4),
                               func=AF.Exp, scale=128.0 ** -0.5)
                        if kg < 2 * m - 1:
                            mk, bm = maskT[:, kg * 4:(kg + 1) * 4, :], b_maskT
                        else:
                            o4 = (kg - (2 * m - 1)) * 4
                            mk, bm = mE[:, o4:o4 + 4, :], b_mE
                        self.O("dve", "tensor_tensor", [b_pt[kg % 2], bm], [b_pt[kg % 2]], out=p_, in0=p_, in1=mk, op=ALU.mult)
                        for jj in range(4):
                            j = kg * 4 + jj
                            self.mm(self.ps[pob][:, 0:129], p_[:, jj, :], Vb[s][:, j, :], j == 0, j == nkt - 1, [b_pt[kg % 2], b_V[s]], [self.b_ps[pob]])
                    self.O("dve", "reciprocal", [self.b_ps[pob]], [b_rec], out=rec[:, s:s + 1], in_=self.ps[pob][:, 128:129])
                    self.O("dve", "tensor_scalar", [self.b_ps[pob], b_rec], [b_oa], out=oa[:, h * 128:(h + 1) * 128], in0=self.ps[pob][:, 0:128],
                           scalar1=rec[:, s:s + 1], scalar2=None, op0=ALU.mult)
                for ch in range(8):
                    self.tr(self.psb[4][:, ch * 128:(ch + 1) * 128], oa[:, ch * 128:(ch + 1) * 128], self.ident_b, [b_oa, self.b_cst], [self.b_ps[4]])
                self.O("act", "copy", [self.b_ps[4]], [self.b_oaT], out=self.oaT[:, :, cols], in_=self.psb[4][:, 0:1024].rearrange("p (a b) -> p a b", a=8))
                self.barrier()

    def lin_attn(self, l, typ):
        P = self.P
        names = ("gq", "gk", "gv", "gg") if typ == 0 else ("rq", "rk", "rv", "rg")
        o = 0
        qf, o = self.carve(self.R_AB, o, [128, 2, NTP], F32)
        kf, o = self.carve(self.R_AB, o, [128, 2, NTP], F32)
        vb, o = self.carve(self.R_AB, o, [128, 4, NTP], BF16)
        sg, o = self.carve(self.R_AB, o, [128, 4, NTP], BF16)
        b_qf = [Buf("qf%d" % i) for i in range(2)]
        b_kf = [Buf("kf%d" % i) for i in range(2)]
        b_vb = [Buf("vb%d" % i) for i in range(4)]
        b_sg = [Buf("sg%d" % i) for i in range(4)]
        X = self.R_X
        qt, _ = self.carve(X, 1548, [128, 2, NTP], BF16)
        b_qt = [Buf("qt0"), Buf("qt1")]
        T1 = 2580
        la, _ = self.carve(X, T1, [128, 256], F32)
        ebq, _ = self.carve(X, T1 + 256, [128, 2, 128], F32)
        ebk, _ = self.carve(X, T1 + 512, [128, 2, 128], F32)
        rtab, _ = self.carve(X, T1, [128, 4, 2, 128], F32)
        kt, _ = self.carve(X, 3604, [128, 2, 128], BF16)
        AT = [self.carve(X, 3732 + 64 * i, [128, 128], BF16)[0] for i in range(2)]
        vtok = [self.carve(X, 3860 + 64 * i, [128, 128], BF16)[0] for i in range(2)]
        ktok = [self.carve(X, 3988 + 32 * i, [128, 64], BF16)[0] for i in range(2)]
        dvec, _ = self.carve(X, 4052, [128, 4], F32)
        tmpf, _ = self.carve(X, 4056, [128, 128], F32)
        Ust, _ = self.carve(X, 4184, [128, 4, 128], F32)
        gab, _ = self.carve(X, 4696, [128, NTP], BF16)
        wab, _ = self.carve(X, 5212, [128, 256], BF16)
        bab, _ = self.carve(X, 5340, [128, 256], BF16)
        Us_s, _ = self.carve(X, 5468, [128, 4, 128], F32)
        dv_s, _ = self.carve(X, 5980, [128, 4], F32)
        dret, _ = self.carve(X, 5984, [128, 2, 4], F32)
        b_la, b_eb, b_rtab, b_kt = Buf("la"), Buf("eb"), Buf("rtab"), Buf("kt")
        b_AT, b_vtok, b_ktok = [Buf("AT0"), Buf("AT1")], [Buf("vt0"), Buf("vt1")], [Buf("kk0"), Buf("kk1")]
        b_dvec, b_tmpf, b_Ust, b_gab, b_wab, b_Us_s, b_dret = Buf("dvec"), Buf("tmpf"), Buf("Ust"), Buf("gab"), Buf("wab"), Buf("Us_s"), Buf("dret")
        for i in range(2):
            banks = [0, 1, 2] if i % 2 == 0 else [3, 4, 5]
            self.dense(self.win_d[l, WB_INDEX[(names[0], i)]], KC, self.h_rhs, [self.b_hT], banks)
            self.evac(banks, qf[:, i, :], b_qf[i])
        for i in range(2):
            banks = [0, 1, 2] if i % 2 == 0 else [3, 4, 5]
            self.dense(self.win_d[l, WB_INDEX[(names[1], i)]], KC, self.h_rhs, [self.b_hT], banks)
            self.evac(banks, kf[:, i, :], b_kf[i])
        for i in range(4):
            banks = [0, 1, 2] if i % 2 == 0 else [3, 4, 5]
            self.dense(self.win_d[l, WB_INDEX[(names[2], i)]], KC, self.h_rhs, [self.b_hT], banks)
            self.evac(banks, vb[:, i, :], b_vb[i])
        for i in range(4):
            banks = [0, 1, 2] if i % 2 == 0 else [3, 4, 5]
            self.dense(self.win_d[l, WB_INDEX[(names[3], i)]], KC, self.h_rhs, [self.b_hT], banks)
            for tb in range(NTB):
                self.O("act", "activation", [self.b_ps[banks[tb]]], [b_sg[i]], out=sg[:, i, tb * TB:(tb + 1) * TB], in_=self.ps[banks[tb]][:, 0:TB], func=AF.Silu)
        if typ == 0:
            self.O("act", "copy", [self.b_small], [b_gab], out=gab[0:16, :], in_=self.smallT[0:16, :])
            P.dma("pool", lambda e: e.dma_start(out=wab[0:16, :], in_=self.wa_d[l]), [], [b_wab])
            P.dma("pool", lambda e: e.dma_start(out=bab[0:1, :], in_=self.ba_d[l]), [], [b_wab])
        else:
            self.DMA("sp", [], [b_dret], dret[0:64, :, :], self.dret_d)
        payUv = self.payU[l][typ].ap().rearrange("(t h k) v -> k t h v", t=8, h=4, k=64)
        b_payU = [Buf("payU%d" % i) for i in range(8)]
        b_payD = [Buf("payD%d" % i) for i in range(8)]
        for lt in range(9):
            n = 128 if lt < 8 else NS
            c0 = lt * 128
            cs = slice(c0, c0 + n)
            if typ == 0:
                self.mm(self.ps[6][0:n, 0:256], gab[0:16, cs], wab[0:16, :], True, False, [b_gab, b_wab], [self.b_ps[6]])
                self.mm(self.ps[6][0:n, 0:256], self.ones_b[0:1, 0:n], bab[0:1, :], False, True, [self.b_cst, b_wab], [self.b_ps[6]])
                self.O("act", "activation", [self.b_ps[6]], [b_la], out=la[0:n, :], in_=self.ps[6][0:n, 0:256], func=AF.Exp, scale=-1.0)
                self.O("act", "activation", [b_la], [b_la], out=la[0:n, :], in_=la[0:n, :], func=AF.Ln, bias=1.0, scale=1.0)
                for i in range(2):
                    self.mm(self.ps[7][:, i * 128:i * 128 + n], la[0:n, i * 128:(i + 1) * 128], self.tri_f[0:n, 0:n], True, True,
                            [b_la, self.b_cst], [self.b_ps[7]])
                for hh in range(4):
                    self.mm(self.ps[7][0:64, 256 + 2 * hh:256 + 2 * hh + 1], la[0:n, hh * 64:(hh + 1) * 64], self.ones_f[0:n, 0:1], True, True,
                            [b_la, self.b_cst], [self.b_ps[7]])
                for i in range(2):
                    self.O("act", "activation", [self.b_ps[7]], [b_eb], out=ebq[:, i, 0:n], in_=self.ps[7][:, i * 128:i * 128 + n], func=AF.Exp, scale=-1.0 / 16)
                    self.O("act", "activation", [self.b_ps[7]], [b_eb], out=ebk[:, i, 0:n], in_=self.ps[7][:, i * 128:i * 128 + n], func=AF.Exp, scale=1.0 / 16)
                    self.O("dve", "scalar_tensor_tensor", [b_qf[i], b_eb], [b_qt[i]], out=qt[:, i, cs], in0=qf[:, i, cs], scalar=0.125, in1=ebq[:, i, 0:n],
                           op0=ALU.mult, op1=ALU.mult)
                    self.O("dve", "tensor_tensor", [b_kf[i], b_eb], [b_kt], out=kt[:, i, 0:n], in0=kf[:, i, cs], in1=ebk[:, i, 0:n], op=ALU.mult)
                self.O("act", "activation", [self.b_ps[7]], [b_dvec], out=dvec[0:64, :], in_=self.ps[7][0:64, 256:264].rearrange("p (a b) -> p a b", b=2)[:, :, 0],
                       func=AF.Exp, scale=-1.0 / 16)
            else:
                self.DMA("sp", [], [b_rtab], rtab, self.rtab_d[lt])
                for i in range(2):
                    for src, bsrc, which, dst, bdst, dcs in ((qf, b_qf[i], 0, qt, b_qt[i], cs), (kf, b_kf[i], 2, kt, b_kt, slice(0, n))):
                        self.mm(self.ps[7][:, 0:n], self.perm_f, src[:, i, cs], True, True, [self.b_cst, bsrc], [self.b_ps[7]])
                        self.O("dve", "tensor_tensor", [bsrc, b_rtab], [b_tmpf], out=tmpf[:, 0:n], in0=src[:, i, cs], in1=rtab[:, which, i, 0:n], op=ALU.mult)
                        self.O("dve", "tensor_tensor", [self.b_ps[7], b_rtab], [self.b_ps[7]], out=self.ps[7][:, 128:128 + n], in0=self.ps[7][:, 0:n],
                               in1=rtab[:, which + 1, i, 0:n], op=ALU.mult)
                        self.O("dve", "tensor_tensor", [self.b_ps[7], b_tmpf], [bdst], out=dst[:, i, dcs], in0=self.ps[7][:, 128:128 + n], in1=tmpf[:, 0:n], op=ALU.add)
                self.O("dve", "tensor_copy", [b_dret], [b_dvec], out=dvec[0:64, :], in_=dret[0:64, 0 if lt < 8 else 1, :])
            for hh in range(4):
                i = hh // 2
                hp = (hh % 2) * 64
                s = hh % 2
                self.mm(self.ps[s][0:n, 0:n], kt[hp:hp + 64, i, 0:n], qt[hp:hp + 64, i, cs], True, True, [b_kt, b_qt[i]], [self.b_ps[s]])
                self.O("dve", "tensor_tensor", [self.b_ps[s], self.b_cst], [b_AT[s]], out=AT[s][0:n, 0:n], in0=self.ps[s][0:n, 0:n], in1=self.tri_f[0:n, 0:n], op=ALU.mult)
                self.tr(self.psb[2][0:n, s * 128:(s + 1) * 128], vb[:, hh, cs], self.ident_b, [b_vb[hh], self.b_cst], [self.b_ps[2]])
                self.O("act", "copy", [self.b_ps[2]], [b_vtok[s]], out=vtok[s][0:n, :], in_=self.psb[2][0:n, s * 128:(s + 1) * 128])
                self.tr(self.psb[3][0:n, s * 64:(s + 1) * 64], kt[hp:hp + 64, i, 0:n], self.ident_b[hp:hp + 64, hp:hp + 64], [b_kt, self.b_cst], [self.b_ps[3]])
                self.O("act", "copy", [self.b_ps[3]], [b_ktok[s]], out=ktok[s][0:n, :], in_=self.psb[3][0:n, s * 64:(s + 1) * 64])
                self.mm(self.ps[4 + s][:, 0:n], vtok[s][0:n, :], AT[s][0:n, 0:n], True, True, [b_vtok[s], b_AT[s]], [self.b_ps[4 + s]])
                self.O("act", "copy", [self.b_ps[4 + s]], [self.b_obT[typ * 4 + hh]], out=self.obT[:, typ * 4 + hh, cs], in_=self.ps[4 + s][:, 0:n])
                self.mm(self.ps[6][0:64, 256 + s * 128:256 + (s + 1) * 128], ktok[s][0:n, :], vtok[s][0:n, :], True, True, [b_ktok[s], b_vtok[s]], [self.b_ps[6]])
                dstU = Ust if lt < 8 else Us_s
                self.O("dve", "tensor_scalar", [self.b_ps[6], b_dvec], [b_Ust if lt < 8 else b_Us_s], out=dstU[0:64, hh, :],
                       in0=self.ps[6][0:64, 256 + s * 128:256 + (s + 1) * 128], scalar1=dvec[0:64, hh:hh + 1], scalar2=None, op0=ALU.mult)
            if lt < 8:
                self.DMA("sp", [b_Ust], [b_payU[lt]], payUv[:, lt, :, :], Ust[0:64, :, :])
                self.DMA("sp", [b_dvec], [b_payD[lt]], self.payD[l][typ].ap()[:, lt * 4:(lt + 1) * 4], dvec[0:64, :])
            else:
                self.O("dve", "tensor_copy", [b_dvec], [b_Us_s], out=dv_s[0:64, :], in_=dvec[0:64, :])
        b_gU, b_gD = Buf("gU"), Buf("gD")
        rg = [list(range(NCORES))]
        for pay, g, br, bw in ((self.payU[l][typ], self.gU[l][typ], b_payU, b_gU), (self.payD[l][typ], self.gD[l][typ], b_payD, b_gD)):
            P.cc(lambda e, pay=pay, g=g: e.collective_compute("AllGather", ALU.bypass, replica_groups=rg,
                                                              ins=[pay.ap().opt()], outs=[g.ap().opt()]), br, [bw])
        self.barrier()
        SU, _ = self.carve(self.R_AB, 0, [128, 32, 128], F32)
        SDall, _ = self.carve(X, T1, [128, 8, 32], F32)
        SD, _ = self.carve(X, T1 + 256, [128, 32], F32)
        S, _ = self.carve(X, T1 + 288, [128, 128], F32)
        Sin, _ = self.carve(X, T1 + 416, [128, 4, 128], F32)
        SinB, _ = self.carve(X, T1 + 928, [128, 9, 128], BF16)
        S0h, _ = self.carve(X, T1 + 1504, [128, 128], F32)
        S0p, _ = self.carve(X, T1 + 1632, [128, 128], F32)
        b_S0p = Buf("S0p")
        of, _ = self.carve(self.R_AB, 4128, [128, NTP], F32)
        b_SU, b_SDall, b_SD, b_S, b_Sin, b_SinB, b_S0h, b_of = Buf("SU"), Buf("SDall"), Buf("SD"), Buf("S"), Buf("Sin"), Buf("SinB"), Buf("S0h"), Buf("of")
        gUv = self.gU[l][typ].ap().rearrange("(r t h k) v -> k r t h v", r=8, t=8, h=4, k=64)
        gDv = self.gD[l][typ].ap().rearrange("(r k) c -> k r c", k=64)
        for half in range(2):
            self.DMA("sp", [b_gD], [b_SDall], SDall[half * 64:(half + 1) * 64, :, :], gDv)
        st_in = (self.sgla_d if typ == 0 else self.sret_d) if self.use_state else None
        st_out_p = self.glap_o if typ == 0 else self.retp_o
        st_out_s = self.glas_o if typ == 0 else self.rets_o
        SDv = SDall.rearrange("p r (t h) -> p t r h", h=4)
        for pr in range(2):
            for b in range(BATCH):
                for mm in range(4):
                    for half in range(2):
                        self.DMA("sp" if half == 0 else "act", [b_gU], [b_SU], SU[half * 64:(half + 1) * 64, mm * 8:(mm + 1) * 8, :],
                                 gUv[:, :, 4 * b + mm, 2 * pr + half, :])
                for half in range(2):
                    hs = slice(half * 64, (half + 1) * 64)
                    self.O("dve", "tensor_copy", [b_SDall], [b_SD], out=SD[hs, :].rearrange("p (t r) -> p t r", r=8),
                           in_=SDv[hs, 4 * b:4 * b + 4, :, 2 * pr + half])
                self.O("pool", "memset", [], [b_S], S, 0.0)
                self.O("pool", "memset", [], [b_Sin], Sin, 0.0)
                for j in range(32):
                    mm, r = j // 8, j % 8
                    self.O("dve", "scalar_tensor_tensor", [b_S, self.b_coef, b_Sin], [b_Sin], out=Sin[:, mm, :], in0=S, scalar=self.coef[:, 18 + r:19 + r],
                           in1=Sin[:, mm, :], op0=ALU.mult, op1=ALU.add)
                    self.O("dve", "scalar_tensor_tensor", [b_S, b_SD, b_SU], [b_S], out=S, in0=S, scalar=SD[:, j:j + 1], in1=SU[:, j, :],
                           op0=ALU.mult, op1=ALU.add)
                for half in range(2):
                    self.DMA("sp", [b_S], [], st_out_p[l, b, 2 * pr + half], S[half * 64:(half + 1) * 64, :], is_output=True)
                self.O("act", "copy", [b_Sin], [b_SinB], out=SinB[:, 4 * b:4 * b + 4, :], in_=Sin)
            if self.use_state:
                for half in range(2):
                    hh = 2 * pr + half
                    hs_ = slice(half * 64, (half + 1) * 64)
                    self.DMA("act", [], [b_S0p], S0p[hs_, :], st_in[l, hh])
                    P.op("act", lambda e, hs_=hs_: e.copy(out=SinB[hs_, 8, :], in_=S0p[hs_, :]), [b_S0p], [b_SinB])
                    self.DMA("sp", [], [b_S0h], S0h[0:64, :], st_in[l, hh])
                    self.O("dve", "scalar_tensor_tensor", [b_S0h, b_Us_s], [b_S0h], out=S0h[0:64, :], in0=S0h[0:64, :], scalar=dv_s[0:64, hh:hh + 1],
                           in1=Us_s[0:64, hh, :], op0=ALU.mult, op1=ALU.add)
                    self.DMA("sp", [b_S0h], [], st_out_s[l, hh], S0h[0:64, :], is_output=True)
            else:
                self.O("pool", "memset", [], [b_SinB], SinB[:, 8, :], 0.0)
            for half in range(2):
                hh = 2 * pr + half
                hp = half * 64
                ch = typ * 4 + hh
                for lt in range(9):
                    n = 128 if lt < 8 else NS
                    cs = slice(lt * 128, lt * 128 + n)
                    bank = lt % 2
                    self.mm(self.ps[bank][:, 0:n], SinB[hp:hp + 64, lt, :], qt[hp:hp + 64, pr, cs], True, True, [b_SinB, b_qt[pr]], [self.b_ps[bank]])
                    self.O("dve", "tensor_tensor", [self.b_ps[bank], self.b_obT[ch]], [b_of], out=of[:, cs], in0=self.ps[bank][:, 0:n], in1=self.obT[:, ch, cs], op=ALU.add)
                self.O("pool", "memset", [], [b_of], of[:, NT:NTP], 0.0)
                self.head_norm_sb(of, b_of, 34 + typ, l, center=(typ == 1))
                self.O("dve", "tensor_tensor", [b_of, b_sg[hh]], [self.b_obT[ch]], out=self.obT[:, ch, :], in0=of, in1=sg[:, hh, :], op=ALU.mult)
        self.barrier()

    def head_norm_sb(self, src, b_src, gcol, l, center):
        sqh, _ = self.carve(self.R_X, 1032, [128, TB], BF16)
        rs, _ = self.carve(self.R_X, 1204, [128, TB], F32)
        for tb in range(NTB):
            sl = slice(tb * TB, (tb + 1) * TB)
            b_sq, b_rs = self.b_sqh, self.b_rs
            bank = 2 + (tb % 2)
            if center:
                self.O("act", "copy", [b_src], [b_sq], out=sqh, in_=src[:, sl])
                self.mm(self.ps[bank][:, 0:TB], self.ones_b, sqh, True, True, [self.b_cst, b_sq], [self.b_ps[bank]])
                self.O("dve", "scalar_tensor_tensor", [self.b_ps[bank], b_src], [b_src], out=src[:, sl], in0=self.ps[bank][:, 0:TB], scalar=-1.0 / 128,
                       in1=src[:, sl], op0=ALU.mult, op1=ALU.add)
            self.O("act", "activation", [b_src], [b_sq], out=sqh, in_=src[:, sl], func=AF.Square)
            self.mm(self.ps[bank][:, 0:TB], self.ones_b, sqh, True, True, [self.b_cst, b_sq], [self.b_ps[bank]])
            self.O("act", "activation", [self.b_ps[bank], self.b_eps], [b_rs], out=rs, in_=self.ps[bank][:, 0:TB], func=AF.Ln, scale=1.0 / 128, bias=self.epsb[:])
            self.O("act", "activation", [b_rs], [b_rs], out=rs, in_=rs, func=AF.Exp, scale=-0.5)
            self.O("dve", "scalar_tensor_tensor", [b_src, self.b_vecs, b_rs], [b_src], out=src[:, sl], in0=src[:, sl], scalar=self.vecs[:, l, gcol:gcol + 1],
                   in1=rs, op0=ALU.mult, op1=ALU.mult)

    def merge(self, l):
        X = self.R_X
        sgt = [self.carve(X, 1548 + 516 * i, [128, NTP], BF16)[0] for i in range(3)]
        mf, _ = self.carve(X, 3096, [128, NTP], F32)
        tf, _ = self.carve(X, 4128, [128, NTP], F32)
        b_sgt = [Buf("sgt%d" % i) for i in range(3)]
        b_mf, b_tf = Buf("mf"), Buf("tf")
        k = 0
        for blk in range(16):
            for br in range(3):
                banks = [0, 1, 2] if k % 2 == 0 else [3, 4, 5]
                k += 1
                self.dense(self.win_d[l, WB_INDEX[("gates", br * 16 + blk)]], KC, self.h_rhs, [self.b_hT], banks)
                for tb in range(NTB):
                    self.O("act", "activation", [self.b_ps[banks[tb]]], [b_sgt[br]], out=sgt[br][:, tb * TB:(tb + 1) * TB], in_=self.ps[banks[tb]][:, 0:TB], func=AF.Sigmoid)
            for br, (k0, nk, src, bsrc) in enumerate(((0, 8, self.oaT, [self.b_oaT]), (8, 4, self.obT[:, 0:4, :], self.b_obT[0:4]), (12, 4, self.obT[:, 4:8, :], self.b_obT[4:8]))):
                banks = [0, 1, 2] if k % 2 == 0 else [3, 4, 5]
                k += 1
                self.dense(self.wbr_d[l, blk][:, k0:k0 + nk, :], nk, lambda kk, tb, src=src: src[:, kk, tb * TB:(tb + 1) * TB], bsrc, banks)
                for tb in range(NTB):
                    sl = slice(tb * TB, (tb + 1) * TB)
                    if br == 0:
                        self.O("dve", "tensor_tensor", [self.b_ps[banks[tb]], b_sgt[br]], [b_mf], out=mf[:, sl], in0=self.ps[banks[tb]][:, 0:TB], in1=sgt[br][:, sl], op=ALU.mult)
                    else:
                        self.O("dve", "tensor_tensor", [self.b_ps[banks[tb]], b_sgt[br]], [b_tf], out=tf[:, sl], in0=self.ps[banks[tb]][:, 0:TB], in1=sgt[br][:, sl], op=ALU.mult)
                        if br == 1:
                            self.O("pool", "tensor_tensor", [b_mf, b_tf], [b_mf], out=mf[:, sl], in0=mf[:, sl], in1=tf[:, sl], op=ALU.add)
                        else:
                            self.O("pool", "tensor_tensor", [b_mf, b_tf], [self.b_mgT[blk]], out=self.mgT[:, blk, sl], in0=mf[:, sl], in1=tf[:, sl], op=ALU.add)

    def out_proj(self, l):
        for blk in range(16):
            banks = [0, 1, 2] if blk % 2 == 0 else [3, 4, 5]
            self.dense(self.wout_d[l, blk], KC, lambda kk, tb: self.mgT[:, kk, tb * TB:(tb + 1) * TB], self.b_mgT, banks)
            for tb in range(NTB):
                sl = slice(tb * TB, (tb + 1) * TB)
                self.O("dve", "tensor_tensor", [self.b_ps[banks[tb]], self.b_xT[blk]], [self.b_xT[blk]], out=self.xT[:, blk, sl], in0=self.ps[banks[tb]][:, 0:TB],
                       in1=self.xT[:, blk, sl], op=ALU.add)

    def ffn(self, l):
        P = self.P
        HW, SW = 516, 258
        actT, _ = self.carve(self.R_ABC, 0, [128, FC, HW], BF16)
        wfo, _ = self.carve(self.R_D, 0, [128, FC, 128], BF16)
        sg = [self.carve(self.R_X, 1548 + 258 * i, [128, SW], F32)[0] for i in range(2)]
        b_act = [Buf("act%d" % f) for f in range(FC)]
        b_wfo, b_sg = Buf("wfo"), [Buf("sgf0"), Buf("sgf1")]
        for half in range(2):
            h0 = half * HW
            for f in range(FC):
                bset = [0, 1, 2, 3] if f % 2 == 0 else [4, 5, 6, 7]
                rhs = lambda kk, tb, h0=h0: self.hT[:, kk, h0 + tb * SW:h0 + (tb + 1) * SW]
                self.dense(self.wfi_d[l, f], KC, rhs, [self.b_hT], bset[0:2], ncols=2, width=SW)
                self.dense(self.wfi_d[l, FC + f], KC, rhs, [self.b_hT], bset[2:4], ncols=2, width=SW)
                for tb in range(2):
                    self.O("act", "activation", [self.b_ps[bset[tb]]], [b_sg[tb]], out=sg[tb], in_=self.ps[bset[tb]][:, 0:SW], func=AF.Silu)
                    self.O("dve", "tensor_tensor", [self.b_ps[bset[2 + tb]], b_sg[tb]], [b_act[f]], out=actT[:, f, tb * SW:(tb + 1) * SW], in0=self.ps[bset[2 + tb]][:, 0:SW],
                           in1=sg[tb], op=ALU.mult)
            for blk in range(16):
                banks = [0, 1] if blk % 2 == 0 else [2, 3]
                for k0, nk in ((0, 16), (16, 16), (32, 12)):
                    P.dma("pool", lambda e, blk=blk, k0=k0, nk=nk: e.dma_start(out=wfo[:, k0:k0 + nk, :], in_=self.wfo_d[l, blk][:, k0:k0 + nk, :]), [], [b_wfo])
                for tb in range(2):
                    for kk in range(FC):
                        self.mm(self.ps[banks[tb]][:, 0:SW], wfo[:, kk, :], actT[:, kk, tb * SW:(tb + 1) * SW], kk == 0, kk == FC - 1,
                                [b_wfo, b_act[kk]], [self.b_ps[banks[tb]]])
                for tb in range(2):
                    sl = slice(h0 + tb * SW, h0 + (tb + 1) * SW)
                    self.O("dve", "tensor_tensor", [self.b_ps[banks[tb]], self.b_xT[blk]], [self.b_xT[blk]], out=self.xT[:, blk, sl], in0=self.ps[banks[tb]][:, 0:SW],
                           in1=self.xT[:, blk, sl], op=ALU.add)
            self.barrier()

    def setup_bias_s(self):
        self.EBs = self.nc.dram_tensor("EBs", [8, 128, 516], BF16)
        self.b_EBs = Buf("EBs")
        o = 0
        acc, o = self.carve(self.R_h, o, [128, 8, 516], F32)
        bk, o = self.carve(self.R_h, o, [128, 516], F32)
        mb, o = self.carve(self.R_h, o, [128, 516], F32)
        tb_, o = self.carve(self.R_h, o, [128, 256], F32)
        neg, o = self.carve(self.R_h, o, [128, 8], F32)
        eb = [None, None]
        eb[0], o = self.carve(self.R_h, o, [128, 516], BF16)
        eb[1], o = self.carve(self.R_h, o, [128, 516], BF16)
        b_acc, b_bk, b_tb, b_mb, b_neg = Buf("acc"), Buf("bk"), Buf("tb"), Buf("mb"), Buf("neg")
        b_eb = [Buf("eb0"), Buf("eb1")]
        self.DMA("sp", [], [b_bk], bk, self.bks_d.rearrange("p a b -> p (a b)"))
        self.DMA("sp", [], [b_tb], tb_, self.rel_d)
        for b in range(32):
            self.O("dve", "tensor_scalar", [b_bk], [b_mb], out=mb, in0=bk, scalar1=float(b), scalar2=None, op0=ALU.is_equal)
            for h in range(8):
                col = tb_[:, b * 8 + h:b * 8 + h + 1]
                if b == 0:
                    self.O("dve", "tensor_scalar", [b_mb, b_tb], [b_acc], out=acc[:, h, :], in0=mb, scalar1=col, scalar2=None, op0=ALU.mult)
                else:
                    self.O("dve", "scalar_tensor_tensor", [b_mb, b_tb, b_acc], [b_acc], out=acc[:, h, :], in0=mb, scalar=col,
                           in1=acc[:, h, :], op0=ALU.mult, op1=ALU.add)
        self.O("dve", "tensor_scalar", [b_tb], [b_neg], out=neg, in0=tb_[:, 248:256], scalar1=-1.0, scalar2=None, op0=ALU.mult)
        for h in range(8):
            self.O("act", "activation", [b_acc, b_neg], [b_eb[h % 2]], out=eb[h % 2], in_=acc[:, h, :], func=AF.Exp, bias=neg[:, h:h + 1], scale=1.0)
            self.DMA("sp", [b_eb[h % 2]], [self.b_EBs], self.EBs.ap()[h], eb[h % 2])

    def attn_sample(self, l, n_iter=16):
        P = self.P
        X, RD, RH = self.R_X, self.R_D, self.R_h
        scols = slice(NTOK, NT)
        r_ = [self.carve(X, 1548 + 512 * i, [128, 8, 64], F32)[0] for i in range(2)]
        Wb, _ = self.carve(X, 2572, [128, 64], F32)
        Z, _ = self.carve(X, 2636, [128, 16, 4], F32)
        iqs, _ = self.carve(X, 2700, [128, 16, 4], BF16)
        tiny, _ = self.carve(X, 2732, [128, 40], F32)
        idxi, _ = self.carve(X, 2772, [128, 128], I32)
        Jf, _ = self.carve(X, 2900, [128, 128], F32)
        ptf, _ = self.carve(X, 3028, [128, 2], F32)
        pti, _ = self.carve(X, 3030, [128, 1], I32)
        ebt = [self.carve(X, 3036 + 258 * i, [128, 4, 129], BF16)[0] for i in range(2)]
        kxTn, _ = self.carve(X, 3552, [128, 128], BF16)
        cms, _ = self.carve(X, 3616, [128, 4], F32)
        dg, _ = self.carve(X, 3620, [128, 8], F32)
        oas, _ = self.carve(X, 3628, [128, 1024], BF16)
        rcp, _ = self.carve(X, 4140, [128, 8], F32)
        sc, _ = self.carve(RD, 0, [128, 4, 129], F32)
        cmpj, _ = self.carve(RD, 516, [128, 4, 129], BF16)
        mS, _ = self.carve(RD, 774, [128, 4, 129], BF16)
        mEs, _ = self.carve(RD, 1032, [128, 129, 8, 4], BF16)
        kxT8 = [self.carve(RD, 3096 + 512 * i, [128, 8, 128], BF16)[0] for i in range(2)]
        kx, _ = self.carve(RH, 0, [128, 128, 64], F32)
        lo, hi, mid, part, ge, t1, t2 = [tiny[:, 4 * i:4 * i + 4] for i in range(7)]
        pmn = tiny[:, 28:36]
        b = {n: Buf(n) for n in ("Wb", "Z", "iqs", "tiny", "idx", "J", "pt", "kxTn", "cms", "dg", "oas", "rcp", "sc", "cmpj", "mS", "mEs", "kx")}
        b_r, b_ebt, b_kxT8 = [Buf("r0"), Buf("r1")], [Buf("ebt0"), Buf("ebt1")], [Buf("kxT80"), Buf("kxT81")]
        self.DMA("sp", [], [b["pt"]], pti, self.pt_d)
        P.op("pool", lambda e: e.iota(idxi, pattern=[[1, 128]], base=0, channel_multiplier=0), [], [b["idx"]])
        self.O("dve", "tensor_copy", [b["idx"]], [b["J"]], out=Jf, in_=idxi)
        self.O("dve", "tensor_copy", [b["pt"]], [b["pt"]], out=ptf[:, 0:1], in_=pti)
        self.O("dve", "tensor_scalar", [b["pt"]], [b["pt"]], out=ptf[:, 1:2], in0=ptf[:, 0:1], scalar1=128.0, scalar2=None, op0=ALU.mult)
        self.O("dve", "tensor_scalar", [b["J"], b["pt"]], [b["J"]], out=Jf, in0=Jf, scalar1=ptf[:, 1:2], scalar2=None, op0=ALU.add)
        self.O("dve", "tensor_copy", [b["J"]], [b["idx"]], out=idxi, in_=Jf)
        idxu = idxi.bitcast(U32)
        ptu = pti.bitcast(U32)
        P.dma("pool", lambda e: e.indirect_dma_start(out=kx.rearrange("p a b -> p (a b)"), out_offset=None, in_=self.cki_d[l],
                                                     in_offset=bass.IndirectOffsetOnAxis(ap=ptu[:, 0:1], axis=0)), [b["pt"]], [b["kx"]])
        self.O("dve", "tensor_copy", self.b_iqT, [b["iqs"]], out=iqs[0:64, :, :].rearrange("p (a two) t -> p a two t", two=2)[:, :, 0, :], in_=self.iqT[0:64, :, scols])
        self.DMA("sp", self.b_iqT, [b["iqs"]], iqs[0:64, :, :].rearrange("p (a two) t -> p a two t", two=2)[:, :, 1, :], self.iqT[64:128, :, scols])
        self.O("dve", "tensor_tensor", [self.b_iw, self.b_cst], [b["Z"]], out=Z[0:NS, :, :], in0=self.iw_tok[0:NS, 8, :].unsqueeze(2).broadcast_to([NS, 16, 4]),
               in1=self.ident_f[0:NS, 0:NS].unsqueeze(1).broadcast_to([NS, 16, 4]), op=ALU.mult)
        self.mm(self.ps[6][:, 0:64], self.ones_f[0:NS, :], Z[0:NS, :, :].rearrange("p a b -> p (a b)"), True, True, [self.b_cst, b["Z"]], [self.b_ps[6]])
        self.O("dve", "tensor_copy", [self.b_ps[6]], [b["Wb"]], out=Wb, in_=self.ps[6][:, 0:64])
        self.DMA("sp", [], [b["cms"]], cms, self.cms_d)
        iq2 = iqs[0:64, :, :].rearrange("p a b -> p (a b)")
        Wb3 = Wb.unsqueeze(1).broadcast_to([128, 8, 64])

        def score_group(g, ng, src_of, pbank):
            rr = r_[g % 2]
            for jj in range(ng):
                lhsT, bl = src_of(jj)
                self.mm(self.ps[pbank][:, jj * 64:(jj + 1) * 64], lhsT, iq2, True, True, [bl, b["iqs"]], [self.b_ps[pbank]])
            self.O("act", "activation", [self.b_ps[pbank]], [b_r[g % 2]], out=rr[:, 0:ng, :], in_=self.ps[pbank][:, 0:ng * 64].rearrange("p (a b) -> p a b", b=64), func=AF.Relu)
            self.O("dve", "tensor_tensor", [b_r[g % 2], b["Wb"]], [b_r[g % 2]], out=rr[:, 0:ng, :], in0=rr[:, 0:ng, :], in1=Wb3[:, 0:ng, :], op=ALU.mult)

        for g in range(16):
            j0 = g * 8
            for q in range(2):
                bank = (g % 2) * 2 + q
                for jj in range(4):
                    j = j0 + q * 4 + jj
                    self.tr(self.ps[bank][0:64, jj * 128:(jj + 1) * 128], kx[:, j, :], self.ident_f, [b["kx"], self.b_cst], [self.b_ps[bank]])
                self.O("act" if q == 0 else "dve", "copy" if q == 0 else "tensor_copy", [self.b_ps[bank]], [b_kxT8[g % 2]], out=kxT8[g % 2][0:64, q * 4:(q + 1) * 4, :],
                       in_=self.ps[bank][0:64, 0:512].rearrange("p (a b) -> p a b", a=4))
            score_group(g, 8, lambda jj, g=g: (kxT8[g % 2][0:64, jj, :], b_kxT8[g % 2]), 4 + (g % 2))
            rr = r_[g % 2]
            self.O("dve", "tensor_reduce", [b_r[g % 2]], [b["sc"]], out=sc[:, :, j0:j0 + 8].rearrange("p t j -> p j t"),
                   in_=rr.rearrange("p j (h t) -> p j t h", t=4), axis=AX.X, op=ALU.add)
        self.O("pool", "memset", [], [b["kxTn"]], kxTn, 0.0)
        ikn, _ = self.carve(X, 4148, [128, NS], BF16)
        b_ikn = Buf("ikn")
        self.O("act", "copy", [self.b_small], [b_ikn], out=ikn[64:128, :], in_=self.smallT[64:128, scols])
        self.DMA("sp", [b_ikn, b["kxTn"]], [b["kxTn"]], kxTn[0:64, 0:NS], ikn[64:128, :])
        score_group(0, 1, lambda jj: (kxTn[0:64, :], b["kxTn"]), 4)
        self.O("dve", "tensor_reduce", [b_r[0]], [b["sc"]], out=sc[:, :, 128:129].rearrange("p t j -> p j t"),
               in_=r_[0][:, 0:1, :].rearrange("p j (h t) -> p j t h", t=4), axis=AX.X, op=ALU.add)
        self.O("dve", "tensor_reduce", [b["sc"]], [b["tiny"]], out=pmn[:, 0:4], in_=sc, axis=AX.X, op=ALU.min)
        self.O("dve", "tensor_tensor", [b["sc"], b["cms"]], [b["sc"]], out=sc[:, :, 128], in0=sc[:, :, 128], in1=cms, op=ALU.add)
        self.O("dve", "tensor_reduce", [b["sc"]], [b["tiny"]], out=pmn[:, 4:8], in_=sc, axis=AX.X, op=ALU.max, negate=True)
        self.tr(self.ps[6][0:8, 0:128], pmn, self.ident_f, [b["tiny"], self.b_cst], [self.b_ps[6]])
        self.O("dve", "tensor_reduce", [self.b_ps[6]], [b["dg"]], out=dg[0:8, 0:1], in_=self.ps[6][0:8, 0:128], axis=AX.X, op=ALU.min)
        dgm, _ = self.carve(X, 4156, [128, 8], F32)
        b_dgm = Buf("dgm")
        self.O("dve", "tensor_scalar", [b["dg"], self.b_cst], [b_dgm], out=dgm[0:8, :], in0=self.ident_f[0:8, 0:8], scalar1=dg[0:8, 0:1], scalar2=None, op0=ALU.mult)
        self.mm(self.ps[6][:, 128:136], self.ones_f[0:8, :], dgm[0:8, :], True, True, [self.b_cst, b_dgm], [self.b_ps[6]])
        self.O("dve", "tensor_copy", [self.b_ps[6]], [b["tiny"]], out=lo, in_=self.ps[6][:, 128:132])
        self.O("dve", "tensor_scalar", [self.b_ps[6]], [b["tiny"]], out=hi, in0=self.ps[6][:, 132:136], scalar1=-1.0, scalar2=None, op0=ALU.mult)
        bt = [b["tiny"]]
        for it in range(n_iter):
            self.O("dve", "tensor_tensor", bt, bt, out=mid, in0=lo, in1=hi, op=ALU.add)
            self.O("dve", "tensor_scalar", bt, bt, out=mid, in0=mid, scalar1=0.5, scalar2=None, op0=ALU.mult)
            self.O("dve", "tensor_tensor", [b["sc"]] + bt, [b["cmpj"]], out=cmpj, in0=sc, in1=mid.unsqueeze(2).broadcast_to([128, 4, 129]), op=ALU.is_ge)
            self.O("dve", "tensor_reduce", [b["cmpj"]], bt, out=part, in_=cmpj, axis=AX.X, op=ALU.add)
            self.mm(self.ps[7][:, 0:4], self.ones_f, part, True, True, [self.b_cst] + bt, [self.b_ps[7]])
            self.O("dve", "tensor_scalar", [self.b_ps[7]], bt, out=ge, in0=self.ps[7][:, 0:4], scalar1=float(TOPK) - 0.5, scalar2=None, op0=ALU.is_ge)
            self.O("dve", "tensor_tensor", bt, bt, out=t1, in0=mid, in1=lo, op=ALU.subtract)
            self.O("dve", "tensor_tensor", bt, bt, out=t1, in0=t1, in1=ge, op=ALU.mult)
            self.O("dve", "tensor_tensor", bt, bt, out=t2, in0=hi, in1=mid, op=ALU.subtract)
            self.O("dve", "tensor_tensor", bt, bt, out=t2, in0=t2, in1=ge, op=ALU.mult)
            self.O("dve", "tensor_tensor", bt, bt, out=lo, in0=lo, in1=t1, op=ALU.add)
            self.O("dve", "tensor_tensor", bt, bt, out=hi, in0=mid, in1=t2, op=ALU.add)
        self.O("dve", "tensor_tensor", [b["sc"]] + bt, [b["mS"]], out=mS, in0=sc, in1=lo.unsqueeze(2).broadcast_to([128, 4, 129]), op=ALU.is_ge)
        for h in range(8):
            self.DMA("sp", [self.b_EBs], [b_ebt[h % 2]], ebt[h % 2].rearrange("p a b -> p (a b)"), self.EBs.ap()[h])
            self.O("dve", "tensor_tensor", [b["mS"], b_ebt[h % 2]], [b["mEs"]], out=mEs[:, :, h, :], in0=mS.rearrange("p t j -> p j t"),
                   in1=ebt[h % 2].rearrange("p t j -> p j t"), op=ALU.mult)
        self.barrier()
        Kj = [self.carve(RH, 1024 * i, [128, 1024], F32)[0] for i in range(2)]
        Vj = [self.carve(RH, 2048 + 1024 * i, [128, 1024], F32)[0] for i in range(2)]
        Kb, _ = self.carve(RH, 4096, [128, 1024], BF16)
        KTj = [self.carve(RH, 4608 + 512 * i, [128, 8, 128], BF16)[0] for i in range(2)]
        Va = [self.carve(RH, 5632 + 516 * i, [128, 8, 129], BF16)[0] for i in range(2)]
        pT = [self.carve(RH, 6664 + 16 * i, [128, 8, 4], BF16)[0] for i in range(2)]
        accO, _ = self.carve(RH, 6700, [128, 8, 129], F32)
        b_Kj, b_Vj, b_KTj, b_Va, b_pT = [Buf("Kj0"), Buf("Kj1")], [Buf("Vj0"), Buf("Vj1")], [Buf("KTj0"), Buf("KTj1")], [Buf("Va0"), Buf("Va1")], [Buf("pT0"), Buf("pT1")]
        b_Kb, b_acc = Buf("Kb"), Buf("accO")
        for i in range(2):
            self.O("pool", "memset", [], [b_Va[i]], Va[i][:, :, 128:129], 1.0)
        sc_ = 128.0 ** -0.5
        for j in range(129):
            s = j % 2
            if j < 128:
                P.dma("pool", lambda e, j=j, s=s: e.indirect_dma_start(out=Kj[s], out_offset=None, in_=self.ck_d[l],
                                                                       in_offset=bass.IndirectOffsetOnAxis(ap=idxu[:, j:j + 1], axis=0)), [b["idx"]], [b_Kj[s]])
                P.dma("pool", lambda e, j=j, s=s: e.indirect_dma_start(out=Vj[s], out_offset=None, in_=self.cv_d[l],
                                                                       in_offset=bass.IndirectOffsetOnAxis(ap=idxu[:, j:j + 1], axis=0)), [b["idx"]], [b_Vj[s]])
                self.O("act", "copy", [b_Kj[s]], [b_Kb], out=Kb, in_=Kj[s])
                bank = s
                for h in range(8):
                    self.tr(self.psb[bank][:, h * 128:(h + 1) * 128], Kb[:, h * 128:(h + 1) * 128], self.ident_b, [b_Kb, self.b_cst], [self.b_ps[bank]])
                self.O("dve", "tensor_copy", [self.b_ps[bank]], [b_KTj[s]], out=KTj[s], in_=self.psb[bank][:, 0:1024].rearrange("p (a b) -> p a b", a=8))
                for h in range(8):
                    self.mm(self.ps[2][:, h * 4:(h + 1) * 4], KTj[s][:, h, :], self.QT[:, h, scols], True, True, [b_KTj[s], self.b_QT[h]], [self.b_ps[2]])
                np_ = 128
                self.O("pool", "tensor_copy", [b_Vj[s]], [b_Va[s]], out=Va[s][:, :, 0:128], in_=Vj[s].rearrange("p (a b) -> p a b", a=8))
                vsrc, bv = Va[s], b_Va[s]
            else:
                for h in range(8):
                    self.mm(self.ps[2][0:NS, h * 4:(h + 1) * 4], self.KTs[:, h, :], self.QT[:, h, scols], True, True, [self.b_KTs, self.b_QT[h]], [self.b_ps[2]])
                np_ = NS
                vsrc, bv = self.Vs, self.b_Vs
            self.O("act", "activation", [self.b_ps[2]], [b_pT[s]], out=pT[s][0:np_], in_=self.ps[2][0:np_, 0:32].rearrange("p (a b) -> p a b", b=4), func=AF.Exp, scale=sc_)
            self.O("dve", "tensor_tensor", [b_pT[s], b["mEs"]], [b_pT[s]], out=pT[s][0:np_], in0=pT[s][0:np_], in1=mEs[0:np_, j, :, :], op=ALU.mult)
            for h in range(8):
                bk_, col = 3 + h // 3, (h % 3) * 129
                self.mm(self.ps[bk_][0:NS, col:col + 129], pT[s][0:np_, h, :], vsrc[0:np_, h, :], True, True, [b_pT[s], bv], [self.b_ps[bk_]])
            for gq in range(3):
                nh = 3 if gq < 2 else 2
                src = self.ps[3 + gq][0:NS, 0:nh * 129].rearrange("p (a b) -> p a b", b=129)
                dst = accO[0:NS, gq * 3:gq * 3 + nh, :]
                if j == 0:
                    self.O("dve", "tensor_copy", [self.b_ps[3 + gq]], [b_acc], out=dst, in_=src)
                else:
                    self.O("dve", "tensor_tensor", [self.b_ps[3 + gq], b_acc], [b_acc], out=dst, in0=src, in1=dst, op=ALU.add)
        self.O("dve", "reciprocal", [b_acc], [b["rcp"]], out=rcp[0:NS, :], in_=accO[0:NS, :, 128])
        self.O("dve", "tensor_tensor", [b_acc, b["rcp"]], [b["oas"]], out=oas[0:NS, :].rearrange("p (a b) -> p a b", a=8), in0=accO[0:NS, :, 0:128],
               in1=rcp[0:NS, :].unsqueeze(2).broadcast_to([NS, 8, 128]), op=ALU.mult)
        for ch in range(8):
            self.tr(self.psb[6][:, ch * 4:(ch + 1) * 4], oas[0:NS, ch * 128:(ch + 1) * 128], self.ident_b[0:NS, 0:NS], [b["oas"], self.b_cst], [self.b_ps[6]])
        self.O("act", "copy", [self.b_ps[6]], [self.b_oaT], out=self.oaT[:, :, scols], in_=self.psb[6][:, 0:32].rearrange("p (a b) -> p a b", a=8))
        self.barrier()


def _bucket(n):
    n = max(int(n), 0)
    if n < 16:
        return n
    v = 16 + int(np.float32(np.log(np.float32(n) / np.float32(16.0))) / np.float32(np.log(8.0)) * np.float32(16.0))
    return min(v, 31)


_BUCKETS = np.array([_bucket(n) for n in range(0, 20000)], np.float32)


def _const_tables(c):
    t = {}
    s_idx = np.arange(128)[:, None]
    t_idx = np.arange(128)[None, :]
    cm = np.zeros((128, 8, 128), np.float32)
    for r in range(8):
        if r > c:
            cm[:, r, :] = NEG
        elif r == c:
            cm[:, r, :] = np.where(np.arange(128)[None, :] <= np.arange(128)[:, None], 0.0, NEG)
    t["cmask"] = cm
    coef = np.zeros((128, 32), np.float32)
    for idx in range(9):
        coef[:, idx] = 1.0 if (idx - 1) == c else 0.0
        coef[:, 9 + idx] = 1.0 if idx == c else 0.0
    for r in range(8):
        coef[:, 18 + r] = 1.0 if r == c else 0.0
    t["coef"] = coef
    dist = np.arange(256)[None, :] - np.arange(128)[:, None]
    t["bkt"] = _BUCKETS[np.maximum(dist, 0)].astype(np.float32)
    cst = np.zeros((128, 4, 128), np.float32)
    cst[:, 0, :] = np.eye(128)
    cst[:, 1, :] = (s_idx <= t_idx)
    partner = np.array([(m + 32) if (m % 64) < 32 else (m - 32) for m in range(128)])
    perm = np.zeros((128, 128), np.float32)
    perm[partner, np.arange(128)] = 1.0
    cst[:, 2, :] = perm
    cst[:, 3, :] = 1.0
    t["cst"] = cst
    rt = np.zeros((9, 128, 4, 2, 128), np.float32)
    half = 32
    freqs = (np.float32(10000.0) ** (-np.arange(half, dtype=np.float32) / np.float32(half))).astype(np.float32)
    p = np.arange(128)
    dd = p % 64
    fi = freqs[dd % 32]
    sgn = np.where(dd < 32, -1.0, 1.0)
    for lt in range(9):
        if lt < 8:
            m = lt % 4
            pos = (8 * m + c) * 128 + np.arange(128)
        else:
            pos = PAST + np.arange(128)
        ang = (pos.astype(np.float32)[None, :] * fi[:, None]).astype(np.float32)
        cos = np.cos(ang).astype(np.float64)
        sin = np.sin(ang).astype(np.float64) * sgn[:, None]
        for blk in range(2):
            hh = blk * 2 + p // 64
            lg = np.log1p(-np.exp2(-5.0 - hh.astype(np.float64)))
            tt = (np.arange(128) + 1).astype(np.float64)
            gq = np.exp(lg[:, None] * tt[None, :])
            gk = np.exp(-lg[:, None] * tt[None, :]) * (64.0 ** -0.5)
            rt[lt, :, 0, blk, :] = cos * gq
            rt[lt, :, 1, blk, :] = sin * gq
            rt[lt, :, 2, blk, :] = cos * gk
            rt[lt, :, 3, blk, :] = sin * gk
    t["rtab"] = rt
    lgh = np.log1p(-np.exp2(-5.0 - np.arange(4, dtype=np.float64)))
    dret = np.zeros((64, 2, 4), np.float32)
    dret[:, 0, :] = np.exp(lgh * 128.0)[None, :]
    dret[:, 1, :] = np.exp(lgh * 4.0)[None, :]
    t["dret"] = dret
    bks = np.zeros((128, 4, 129), np.float32)
    slot = np.arange(128)[:, None, None]
    tq = np.arange(4)[None, :, None]
    jj = np.arange(128)[None, None, :]
    d_past = (PAST + tq) - (slot * 128 + jj)
    bks[:, :, 0:128] = _BUCKETS[np.maximum(d_past, 0)]
    d_new = (tq - slot)[:, :, 0]
    bks[:, :, 128] = _BUCKETS[np.clip(d_new, 0, 19999)]
    t["bkt_s"] = bks
    cms = np.full((128, 4), NEG, np.float32)
    for n in range(4):
        for tq_ in range(4):
            if n <= tq_:
                cms[n, tq_] = 0.0
    t["cmask_s"] = cms
    return t


_WCACHE = {}


def _prep_shared(inputs, use_cache):
    f = lambda a: np.asarray(a, dtype=np.float32)
    sh = {}
    w_in = f(inputs["w_in"])
    win = np.empty((DEPTH, NWB, 128, KC, 128), np.float32)
    for l in range(DEPTH):
        for j, (_n, _i, cols) in enumerate(WIN_BLOCKS):
            win[l, j] = _wblk(w_in[l], cols)
    sh["win"] = win
    ar = lambda j: np.arange(j * 128, (j + 1) * 128)
    sh["wbr"] = np.stack([np.stack([_wblk(f(inputs["w_branch"])[l], ar(j)) for j in range(16)]) for l in range(DEPTH)])
    sh["wout"] = np.stack([np.stack([_wblk(f(inputs["w_out"])[l], ar(j)) for j in range(16)]) for l in range(DEPTH)])
    sh["wfi"] = np.stack([np.stack([_wblk(f(inputs["w_ffn_in"])[l], ar(j)) for j in range(2 * FC)]) for l in range(DEPTH)])
    sh["wfo"] = np.stack([np.stack([_wblk(f(inputs["w_ffn_out"])[l], ar(j)) for j in range(16)]) for l in range(DEPTH)])
    vecs = np.zeros((128, DEPTH, 36), np.float32)
    for l in range(DEPTH):
        vecs[:, l, 0:16] = _pvec(f(inputs["norm_mix"])[l])
        vecs[:, l, 16:32] = _pvec(f(inputs["norm_ffn"])[l])
        vecs[:, l, 32] = f(inputs["a_q_norm"])[l]
        vecs[:, l, 33] = f(inputs["a_k_norm"])[l]
        vecs[:, l, 34] = f(inputs["gla_norm"])[l]
        vecs[:, l, 35] = f(inputs["ret_norm"])[l]
    sh["vecs"] = vecs
    sh["gla_wa"] = np.ascontiguousarray(f(inputs["gla_wa"]))
    sh["gla_ba"] = np.ascontiguousarray(f(inputs["gla_ba"]).reshape(DEPTH, 1, 256))
    sh["rel_bc"] = np.ascontiguousarray(np.broadcast_to(f(inputs["rel_table"]).reshape(1, 256), (128, 256)))
    if use_cache:
        ck = f(inputs["cache_k"]).reshape(DEPTH, NPOOL * 128, 1024)
        cv = f(inputs["cache_v"]).reshape(DEPTH, NPOOL * 128, 1024)
        cki = f(inputs["cache_kidx"]).reshape(DEPTH, NPOOL, 128 * 64)
        for l in range(DEPTH):
            sh["cache_k%d" % l] = ck[l]
            sh["cache_v%d" % l] = cv[l]
            sh["cache_kidx%d" % l] = cki[l]
    return sh


def _prep_core(inputs, c, sh, use_cache):
    xp = np.asarray(inputs["x_prompt"], np.float32)
    xs = np.asarray(inputs["x_sample"], np.float32)
    X = np.zeros((NTP, D), np.float32)
    for b in range(BATCH):
        for m in range(4):
            lt = b * 4 + m
            g = 8 * m + c
            X[lt * 128:(lt + 1) * 128] = xp[b, g * 128:(g + 1) * 128]
    X[NTOK:NT] = xs[c]
    d = dict(sh)
    d["xT"] = np.ascontiguousarray(X.T.reshape(KC, 128, NTP).transpose(1, 0, 2))
    ct = _const_tables(c)
    for k in ("cmask", "coef", "bkt", "rtab", "dret", "cst"):
        d[k] = ct[k]
    if use_cache:
        d["bkt_s"] = ct["bkt_s"]
        d["cmask_s"] = ct["cmask_s"]
        d["ptab"] = np.ascontiguousarray(np.asarray(inputs["page_table"], np.int32)[c].reshape(128, 1))
    d["s_gla"] = np.ascontiguousarray(np.asarray(inputs["state_gla"], np.float32)[:, c])
    d["s_ret"] = np.ascontiguousarray(np.asarray(inputs["state_ret"], np.float32)[:, c])
    return d


def _assemble(res):
    y_p = np.zeros((BATCH, SEQ, D), np.float32)
    y_s = np.zeros((NCORES, NS, D), np.float32)
    k_p = np.zeros((DEPTH, BATCH, SEQ, 8, 128), np.float32)
    v_p = np.zeros((DEPTH, BATCH, SEQ, 8, 128), np.float32)
    ki_p = np.zeros((DEPTH, BATCH, SEQ, 64), np.float32)
    k_s = np.zeros((DEPTH, NCORES, NS, 8, 128), np.float32)
    v_s = np.zeros((DEPTH, NCORES, NS, 8, 128), np.float32)
    ki_s = np.zeros((DEPTH, NCORES, NS, 64), np.float32)
    gla_s = np.zeros((DEPTH, NCORES, 4, 64, 128), np.float32)
    ret_s = np.zeros((DEPTH, NCORES, 4, 64, 128), np.float32)
    for c in range(NCORES):
        r = res[c]
        Y = r["yT"].transpose(2, 1, 0).reshape(NTP, D)
        kT = r["kT"].transpose(0, 3, 2, 1)
        vT = r["vT"].transpose(0, 3, 2, 1)
        kiT = r["kiT"].transpose(0, 2, 1)
        for b in range(BATCH):
            for m in range(4):
                lt = b * 4 + m
                g = 8 * m + c
                y_p[b, g * 128:(g + 1) * 128] = Y[lt * 128:(lt + 1) * 128]
                k_p[:, b, g * 128:(g + 1) * 128] = kT[:, lt * 128:(lt + 1) * 128]
                v_p[:, b, g * 128:(g + 1) * 128] = vT[:, lt * 128:(lt + 1) * 128]
                ki_p[:, b, g * 128:(g + 1) * 128] = kiT[:, lt * 128:(lt + 1) * 128]
        y_s[c] = Y[NTOK:NT]
        k_s[:, c] = kT[:, NTOK:NT]
        v_s[:, c] = vT[:, NTOK:NT]
        ki_s[:, c] = kiT[:, NTOK:NT]
        gla_s[:, c] = r["gla_s"]
        ret_s[:, c] = r["ret_s"]
    gla_p = np.ascontiguousarray(res[0]["gla_p"])
    ret_p = np.ascontiguousarray(res[0]["ret_p"])
    return (y_p, y_s, k_p, v_p, ki_p, gla_p, ret_p, k_s, v_s, ki_s, gla_s, ret_s)


def build(stages="all", use_cache=True, use_state=None):
    B = Builder(stages, use_cache, use_state)
    B.setup()
    st = stages
    B.setup_bias()
    B.barrier()
    if use_cache:
        B.setup_bias_s()
        B.barrier()
    for l in range(DEPTH):
        B.rmsnorm_x(l, 0)
        B.barrier()
        B.phase_kv(l)
        B.phase_q(l)
        if st == "kvq":
            break
        B.barrier()
        if st != "linonly":
            B.attn_prompt(l)
            if use_cache:
                B.attn_sample(l)
        if st == "attn":
            dbg, _ = B.carve(B.R_h, 0, [128, 8, NTP], F32)
            b_dbg = Buf("dbg")
            B.O("dve", "tensor_copy", [B.b_oaT], [b_dbg], out=dbg, in_=B.oaT)
            B.DMA("sp", [b_dbg], [], B.yT_o[:, 0:8, :], dbg, is_output=True)
            break
        if st.startswith("lin"):
            pass
        B.rmsnorm_x(l, 0)
        B.barrier()
        B.lin_attn(l, 0)
        B.lin_attn(l, 1)
        if st.startswith("lin"):
            dbg, _ = B.carve(B.R_h, 0, [128, 8, NTP], F32)
            b_dbg = Buf("dbg")
            B.O("dve", "tensor_copy", B.b_obT, [b_dbg], out=dbg, in_=B.obT)
            B.DMA("sp", [b_dbg], [], B.yT_o[:, 8:16, :], dbg, is_output=True)
            break
        B.merge(l)
        B.barrier()
        B.out_proj(l)
        B.barrier()
        B.rmsnorm_x(l, 1)
        B.barrier()
        B.ffn(l)
    if st in ("all", "nosample"):
        B.DMA("sp", B.b_xT, [], B.yT_o, B.xT[:], is_output=True)
    B.P.finish()
    B.P.emit()
    return B


def run(inputs, stages="all", use_cache=True, trace=False, use_state=None):
    B = build(stages, use_cache, use_state)
    sh = _prep_shared(inputs, use_cache)
    in_maps = [_prep_core(inputs, c, sh, use_cache) for c in range(NCORES)]
    for m in in_maps:
        for k in list(m.keys()):
            if k not in B.din:
                del m[k]
    r = run_bass_kernel_spmd(B.nc, in_maps, core_ids=list(range(NCORES)), trace=trace)
    return r, B


def kernel(**inputs):
    r, B = run(inputs, "all", True)
    return _assemble(r.results)
```

```python
import numpy as np
import ml_dtypes
import concourse.bass as bass
import concourse.mybir as mybir
from concourse.bass_utils import run_bass_kernel_spmd

F32 = mybir.dt.float32
BF16 = mybir.dt.bfloat16
I32 = mybir.dt.int32
U32 = mybir.dt.uint32
AF = mybir.ActivationFunctionType
ALU = mybir.AluOpType
AX = mybir.AxisListType

NCORES = 8
D = 2048
KC = 16
NTILE = 8
NTOK = 1024
NS = 4
NT = NTOK + NS
NTP = 1032
TB = 344
NTB = 3
DEPTH = 2
SEQ = 4096
BATCH = 2
PAST = 16384
NPAGE = 128
NPOOL = 1280
EPS = 1e-6
DFF = 5632
FC = 44
TOPK = 256
NEG = -1.0e30

_sizes = (1024, 1024, 1024, 1024, 64, 16, 256, 256, 512, 16, 512, 256, 256, 512, 512, 6144)
_names = ("aq", "ak", "av", "iq", "ik", "iw", "gq", "gk", "gv", "ga", "gg", "rq", "rk", "rv", "rg", "gates")
OFF = {}
_o = 0
for _n, _s in zip(_names, _sizes):
    OFF[_n] = _o
    _o += _s
IN_TOTAL = _o


def _win_blocks():
    blocks = []
    def add(name, off, n):
        for i in range(n // 128):
            blocks.append((name, i, np.arange(off + i * 128, off + (i + 1) * 128)))
    add("aq", OFF["aq"], 1024)
    add("ak", OFF["ak"], 1024)
    add("av", OFF["av"], 1024)
    add("iq", OFF["iq"], 1024)
    small = -np.ones(128, np.int64)
    small[0:16] = np.arange(OFF["ga"], OFF["ga"] + 16)
    small[32:48] = np.arange(OFF["iw"], OFF["iw"] + 16)
    small[64:128] = np.arange(OFF["ik"], OFF["ik"] + 64)
    blocks.append(("small", 0, small))
    add("gq", OFF["gq"], 256)
    add("gk", OFF["gk"], 256)
    add("gv", OFF["gv"], 512)
    add("gg", OFF["gg"], 512)
    add("rq", OFF["rq"], 256)
    add("rk", OFF["rk"], 256)
    add("rv", OFF["rv"], 512)
    add("rg", OFF["rg"], 512)
    add("gates", OFF["gates"], 6144)
    return blocks


WIN_BLOCKS = _win_blocks()
NWB = len(WIN_BLOCKS)
WB_INDEX = {}
for _i, (_n, _j, _c) in enumerate(WIN_BLOCKS):
    WB_INDEX[(_n, _j)] = _i


class Buf:
    __slots__ = ("name", "w", "r")

    def __init__(self, name):
        self.name = name
        self.w = None
        self.r = {}


ENGS = ("pe", "act", "dve", "pool", "sp")


class Prog:
    def __init__(self, nc, n_dma_sems=40):
        self.nc = nc
        self.q = {e: [] for e in ENGS}
        self.cnt = {}
        self.water = {e: {} for e in ENGS}
        self.semh = {}
        for e in ("pe", "act", "dve", "pool"):
            self.semh[e] = nc.alloc_semaphore("s_" + e)
            self.cnt[e] = 0
        self.dsems = {}
        self.dnext = {}
        for q, n in (("sp", 24), ("act", 12), ("pool", 24)):
            self.dsems[q] = []
            self.dnext[q] = 0
            for i in range(n):
                k = "d%s%d" % (q, i)
                self.semh[k] = nc.alloc_semaphore("s_" + k)
                self.cnt[k] = 0
                self.dsems[q].append(k)
        self.out_toks = []
        self.n_inst = 0

    def _need(self, eng, toks):
        best = {}
        for t in toks:
            if t is None:
                continue
            k, v = t
            if k == "pe" and eng == "pe":
                continue
            if self.water[eng].get(k, 0) >= v:
                continue
            if best.get(k, 0) < v:
                best[k] = v
        for k, v in best.items():
            self.q[eng].append(("wait", k, v))
            self.water[eng][k] = v
            self.n_inst += 1

    def _deps(self, reads, writes):
        toks = []
        for b in reads:
            toks.append(b.w)
        for b in writes:
            toks.append(b.w)
            for k, v in b.r.items():
                toks.append((k, v))
        return toks

    def _mark(self, tok, reads, writes):
        k, v = tok
        for b in reads:
            if b.r.get(k, 0) < v:
                b.r[k] = v
        for b in writes:
            b.w = tok
            b.r = {}

    def op(self, eng, fn, reads=(), writes=()):
        self._need(eng, self._deps(reads, writes))
        self.cnt[eng] += 1
        tok = (eng, self.cnt[eng])
        self.q[eng].append(("op", fn, eng, 1))
        self.n_inst += 1
        self._mark(tok, reads, writes)
        return tok

    def dma(self, queue, fn, reads=(), writes=(), is_output=False):
        k = self.dsems[queue][self.dnext[queue]]
        self.dnext[queue] = (self.dnext[queue] + 1) % len(self.dsems[queue])
        toks = self._deps(reads, writes)
        if self.cnt[k] > 0:
            toks.append((k, self.cnt[k]))
        self._need(queue, toks)
        self.cnt[k] += 16
        tok = (k, self.cnt[k])
        self.q[queue].append(("op", fn, k, 16))
        self.n_inst += 1
        self._mark(tok, reads, writes)
        if is_output:
            self.out_toks.append(tok)
        return tok

    def cc(self, fn, reads=(), writes=()):
        k = "cc%d" % len([s for s in self.semh if s.startswith("cc")])
        self.semh[k] = self.nc.alloc_semaphore("s_" + k)
        self.cnt[k] = 0
        self._need("pool", self._deps(reads, writes))
        self.cnt[k] += 1
        tok = (k, 1)
        self.q["pool"].append(("op", fn, k, 1))
        self.n_inst += 1
        self._mark(tok, reads, writes)
        return tok

    def barrier(self):
        toks = [(k, v) for k, v in self.cnt.items() if v > 0]
        for e in ENGS:
            self._need(e, toks)

    def finish(self):
        best = {}
        for k, v in self.out_toks:
            best[k] = max(best.get(k, 0), v)
        for k, v in best.items():
            self.q["sp"].append(("wait", k, v))

    def emit(self):
        nc = self.nc
        engmap = {"pe": "tensor", "act": "scalar", "dve": "vector", "pool": "gpsimd", "sp": "sync"}
        with nc.Block() as block:
            for e in ENGS:
                items = self.q[e]
                semh = self.semh

                def body(eng, items=items):
                    for it in items:
                        if it[0] == "wait":
                            eng.wait_ge(semh[it[1]], it[2])
                        else:
                            it[1](eng).then_inc(semh[it[2]], it[3])

                getattr(block, engmap[e])(body)


def _wblk(w, cols):
    K = w.shape[0]
    sel = np.zeros((K, 128), np.float32)
    m = cols >= 0
    sel[:, m] = w[:, cols[m]]
    return np.ascontiguousarray(sel.reshape(K // 128, 128, 128).transpose(1, 0, 2))


def _pvec(v):
    return np.ascontiguousarray(v.reshape(-1, 128).T)


XW = 6000


class Builder:
    def __init__(self, stages="all", use_cache=True, use_state=None):
        self.stages = stages
        self.use_cache = use_cache
        self.use_state = use_cache if use_state is None else use_state
        nc = bass.Bass("TRN2", target_bir_lowering=False)
        self.nc = nc
        self.P = Prog(nc)
        self.din = {}
        self.dout = {}
        self._decl()
        self._alloc()

    def inp(self, name, shape, dt=F32):
        t = self.nc.dram_tensor(name, list(shape), dt, kind="ExternalInput")
        self.din[name] = (list(shape), dt)
        return t.ap()

    def outp(self, name, shape, dt=F32):
        t = self.nc.dram_tensor(name, list(shape), dt, kind="ExternalOutput")
        self.dout[name] = (list(shape), dt)
        return t.ap()

    def _decl(self):
        nc = self.nc
        self.xT_d = self.inp("xT", [128, KC, NTP])
        self.win_d = self.inp("win", [DEPTH, NWB, 128, KC, 128])
        self.wbr_d = self.inp("wbr", [DEPTH, 16, 128, KC, 128])
        self.wout_d = self.inp("wout", [DEPTH, 16, 128, KC, 128])
        self.wfi_d = self.inp("wfi", [DEPTH, 2 * FC, 128, KC, 128])
        self.wfo_d = self.inp("wfo", [DEPTH, 16, 128, FC, 128])
        self.vec_d = self.inp("vecs", [128, DEPTH, 36])
        self.wa_d = self.inp("gla_wa", [DEPTH, 16, 256])
        self.ba_d = self.inp("gla_ba", [DEPTH, 1, 256])
        self.rel_d = self.inp("rel_bc", [128, 256])
        self.cm_d = self.inp("cmask", [128, 8, 128])
        self.coef_d = self.inp("coef", [128, 32])
        self.bk_d = self.inp("bkt", [128, 256])
        self.rtab_d = self.inp("rtab", [9, 128, 4, 2, 128])
        self.dret_d = self.inp("dret", [64, 2, 4])
        self.cst_d = self.inp("cst", [128, 4, 128])
        if self.use_cache:
            self.ck_d = [self.inp("cache_k%d" % l, [NPOOL * 128, 1024]) for l in range(DEPTH)]
            self.cv_d = [self.inp("cache_v%d" % l, [NPOOL * 128, 1024]) for l in range(DEPTH)]
            self.cki_d = [self.inp("cache_kidx%d" % l, [NPOOL, 128 * 64]) for l in range(DEPTH)]
            self.pt_d = self.inp("ptab", [128, 1], I32)
            self.bks_d = self.inp("bkt_s", [128, 4, 129])
            self.cms_d = self.inp("cmask_s", [128, 4])
        if self.use_state:
            self.sgla_d = self.inp("s_gla", [DEPTH, 4, 64, 128])
            self.sret_d = self.inp("s_ret", [DEPTH, 4, 64, 128])
        self.yT_o = self.outp("yT", [128, KC, NTP])
        self.kT_o = self.outp("kT", [DEPTH, 128, 8, NTP])
        self.vT_o = self.outp("vT", [DEPTH, 128, 8, NTP])
        self.kiT_o = self.outp("kiT", [DEPTH, 64, NTP])
        self.glap_o = self.outp("gla_p", [DEPTH, BATCH, 4, 64, 128])
        self.retp_o = self.outp("ret_p", [DEPTH, BATCH, 4, 64, 128])
        self.glas_o = self.outp("gla_s", [DEPTH, 4, 64, 128])
        self.rets_o = self.outp("ret_s", [DEPTH, 4, 64, 128])
        dt_ = nc.dram_tensor
        self.payK = [dt_("payK%d" % l, [1024, 1024], BF16) for l in range(DEPTH)]
        self.payV = [dt_("payV%d" % l, [1024, 1024], BF16) for l in range(DEPTH)]
        self.payI = [dt_("payI%d" % l, [64, 1024], BF16) for l in range(DEPTH)]
        self.gK = [dt_("gK%d" % l, [8 * 1024, 1024], BF16) for l in range(DEPTH)]
        self.gV = [dt_("gV%d" % l, [8 * 1024, 1024], BF16) for l in range(DEPTH)]
        self.gI = [dt_("gI%d" % l, [8 * 64, 1024], BF16) for l in range(DEPTH)]
        self.payU = [[dt_("payU%d_%d" % (l, t), [8 * 4 * 64, 128], F32) for t in range(2)] for l in range(DEPTH)]
        self.payD = [[dt_("payD%d_%d" % (l, t), [64, 32], F32) for t in range(2)] for l in range(DEPTH)]
        self.gU = [[dt_("gU%d_%d" % (l, t), [8 * 2048, 128], F32) for t in range(2)] for l in range(DEPTH)]
        self.gD = [[dt_("gD%d_%d" % (l, t), [8 * 64, 32], F32) for t in range(2)] for l in range(DEPTH)]
        self.Gd = dt_("Gd", [8, 128, 9 * 128], BF16)
        self.b_Gd = Buf("Gd")

    def _alloc(self):
        nc = self.nc
        sb = nc.alloc_sbuf_tensor
        self.xT = sb("xT_s", [128, KC, NTP], F32)
        self.b_xT = [Buf("xT%d" % c) for c in range(KC)]
        self.ABCD = sb("abcd", [128, 32, NTP], BF16)
        self.hT = sb("hT_s", [128, KC, NTP], BF16)
        self.b_hT = Buf("hT")
        self.X = sb("X_s", [128, XW], F32)
        f32v = lambda ap: ap.rearrange("p a b -> p (a b)").bitcast(F32)
        self.R_h = f32v(self.hT[:])
        self.R_A = f32v(self.ABCD[:, 0:8, :])
        self.R_B = f32v(self.ABCD[:, 8:16, :])
        self.R_AB = f32v(self.ABCD[:, 0:16, :])
        self.R_ABC = f32v(self.ABCD[:, 0:24, :])
        self.R_C = f32v(self.ABCD[:, 16:24, :])
        self.R_D = f32v(self.ABCD[:, 24:32, :])
        self.R_X = self.X[:]
        self.QT = self.ABCD[:, 0:8, :]
        self.iqT = self.ABCD[:, 8:16, :]
        self.oaT = self.ABCD[:, 16:24, :]
        self.obT = self.ABCD[:, 24:32, :]
        self.mgT = self.ABCD[:, 0:16, :]
        self.b_QT = [Buf("QT%d" % i) for i in range(8)]
        self.b_iqT = [Buf("iqT%d" % i) for i in range(8)]
        self.b_oaT = Buf("oaT")
        self.b_obT = [Buf("obT%d" % i) for i in range(8)]
        self.b_mgT = [Buf("mgT%d" % i) for i in range(16)]
        self.cst = sb("cst_s", [128, 4, 128], F32)
        self.cstb = sb("cstb", [128, 4, 128], BF16)
        self.b_cst = Buf("cst")
        self.vecs = sb("vecs_s", [128, DEPTH, 36], F32)
        self.b_vecs = Buf("vecs")
        self.coef = sb("coef_sb", [128, 32], F32)
        self.b_coef = Buf("coef")
        self.epsb = sb("epsb", [128, 1], F32)
        self.pw = sb("pw", [128, 16], F32)
        self.b_pw = Buf("pw")
        self.b_eps = Buf("eps")
        self.wbuf = [sb("wbuf%d" % i, [128, KC, 128], BF16) for i in range(3)]
        self.b_wbuf = [Buf("wbuf%d" % i) for i in range(3)]
        self.wnext = 0
        self.ps = [nc.alloc_psum_tensor("ps%d" % i, [128, 512], F32) for i in range(8)]
        self.psb = [p[:, 0:512].bitcast(BF16) for p in self.ps]
        self.b_ps = [Buf("ps%d" % i) for i in range(8)]
        self.iw_tok = sb("iw_tok", [128, 9, 16], F32)
        self.iw_abs = sb("iw_abs", [128, 9, 16], F32)
        self.iw_sgn = sb("iw_sgn", [128, 9, 16], F32)
        self.b_iw = Buf("iw")
        self.KTs = sb("KTs", [128, 8, NS], BF16)
        self.Vs = sb("Vs", [NS, 8, 129], BF16)
        self.b_KTs = Buf("KTs")
        self.b_Vs = Buf("Vs")
        self.b_sqh, self.b_rs = Buf("sqh"), Buf("rs")
        self.smallT, _ = self.carve(self.R_X, 0, [128, NTP], F32)
        self.b_small = Buf("smallT")

    def carve(self, base, off, shape, dt):
        n = int(np.prod(shape[1:]))
        words = n if dt in (F32, I32, U32) else (n + 1) // 2
        ap = base[:, off:off + words]
        if dt != F32:
            ap = ap.bitcast(dt)[:, 0:n]
        if len(shape) == 3:
            ap = ap.rearrange("p (a b) -> p a b", a=shape[1])
        elif len(shape) == 4:
            ap = ap.rearrange("p (a b c) -> p a b c", a=shape[1], b=shape[2])
        return ap, off + words

    def barrier(self):
        self.P.barrier()

    def O(self, eng, name, reads, writes, *args, **kw):
        return self.P.op(eng, lambda e: getattr(e, name)(*args, **kw), reads, writes)

    def DMA(self, q, reads, writes, out, in_, is_output=False):
        return self.P.dma(q, lambda e: e.dma_start(out=out, in_=in_), reads, writes, is_output=is_output)

    def mm(self, out, lhsT, rhs, start, stop, reads, writes):
        return self.P.op("pe", lambda e: e.matmul(out, lhsT=lhsT, rhs=rhs, start=start, stop=stop), reads, writes)

    def tr(self, out, in_, ident, reads, writes):
        return self.P.op("pe", lambda e: e.transpose(out, in_, ident), reads, writes)

    def next_w(self):
        i = self.wnext
        self.wnext = (self.wnext + 1) % len(self.wbuf)
        return self.wbuf[i], self.b_wbuf[i]

    def setup(self):
        self.DMA("sp", [], self.b_xT, self.xT[:], self.xT_d)
        self.DMA("sp", [], [self.b_cst], self.cst[:], self.cst_d)
        self.DMA("sp", [], [self.b_vecs], self.vecs[:], self.vec_d)
        self.DMA("sp", [], [self.b_coef], self.coef[:], self.coef_d)
        self.O("dve", "tensor_copy", [self.b_cst], [self.b_cst], out=self.cstb[:], in_=self.cst[:])
        self.O("pool", "memset", [], [self.b_eps], self.epsb[:], EPS)
        self.O("pool", "memset", [], [self.b_Vs], self.Vs[:], 1.0)
        self.O("pool", "memset", [], [self.b_iw], self.iw_tok[:], 0.0)
        for k in range(16):
            self.O("pool", "memset", [], [self.b_pw], self.pw[:, k:k + 1], 2.0 ** -(k + 1))
        self.ident_f = self.cst[:, 0, :]
        self.tri_f = self.cst[:, 1, :]
        self.perm_f = self.cst[:, 2, :]
        self.ones_f = self.cst[:, 3, :]
        self.ident_b = self.cstb[:, 0, :]
        self.tri_b = self.cstb[:, 1, :]
        self.ones_b = self.cstb[:, 3, :]

    def rmsnorm_x(self, l, which):
        sq, _ = self.carve(self.R_AB, 0, [128, KC, NTP], BF16)
        rstd, _ = self.carve(self.R_X, 1548, [128, NTB, TB], F32)
        b_sq = Buf("sq")
        b_rstd = Buf("rstd")
        self.O("act", "activation", self.b_xT, [b_sq], out=sq, in_=self.xT[:], func=AF.Square)
        for tb in range(NTB):
            for c in range(KC):
                self.mm(self.ps[tb][:, 0:TB], self.ones_b, sq[:, c, tb * TB:(tb + 1) * TB], c == 0, c == KC - 1,
                        [self.b_cst, b_sq], [self.b_ps[tb]])
        for tb in range(NTB):
            self.O("act", "activation", [self.b_ps[tb], self.b_eps], [b_rstd], out=rstd[:, tb, :], in_=self.ps[tb][:, 0:TB],
                   func=AF.Ln, scale=1.0 / D, bias=self.epsb[:])
        self.O("act", "activation", [b_rstd], [b_rstd], out=rstd, in_=rstd, func=AF.Exp, scale=-0.5)
        r2 = rstd.rearrange("p a b -> p (a b)")
        for c in range(KC):
            self.O("dve", "scalar_tensor_tensor", [self.b_xT[c], self.b_vecs, b_rstd], [self.b_hT],
                   out=self.hT[:, c, :], in0=self.xT[:, c, :], scalar=self.vecs[:, l, which * 16 + c:which * 16 + c + 1],
                   in1=r2, op0=ALU.mult, op1=ALU.mult)

    def dense(self, w_ap, nk, rhs_of, rhs_bufs, banks, ncols=NTB, width=TB, wtile=None):
        if wtile is None:
            wb, b_wb = self.next_w()
        else:
            wb, b_wb = wtile
        self.P.dma("pool", lambda e: e.dma_start(out=wb[:, 0:nk, :], in_=w_ap), [], [b_wb])
        for tb in range(ncols):
            for k in range(nk):
                self.mm(self.ps[banks[tb]][:, 0:width], wb[:, k, :], rhs_of(k, tb), k == 0, k == nk - 1,
                        [b_wb] + rhs_bufs, [self.b_ps[banks[tb]]])

    def h_rhs(self, k, tb):
        return self.hT[:, k, tb * TB:(tb + 1) * TB]

    def head_rms(self, banks, gcol, out_ap, b_out, l, statbanks):
        sqh, _ = self.carve(self.R_X, 1032, [128, TB], BF16)
        rs, _ = self.carve(self.R_X, 1204, [128, TB], F32)
        for tb in range(NTB):
            b_sq, b_rs = self.b_sqh, self.b_rs
            sbk = statbanks[tb % len(statbanks)]
            self.O("act", "activation", [self.b_ps[banks[tb]]], [b_sq], out=sqh, in_=self.ps[banks[tb]][:, 0:TB], func=AF.Square)
            self.mm(self.ps[sbk][:, 0:TB], self.ones_b, sqh, True, True, [self.b_cst, b_sq], [self.b_ps[sbk]])
            self.O("act", "activation", [self.b_ps[sbk], self.b_eps], [b_rs], out=rs, in_=self.ps[sbk][:, 0:TB],
                   func=AF.Ln, scale=1.0 / 128, bias=self.epsb[:])
            self.O("act", "activation", [b_rs], [b_rs], out=rs, in_=rs, func=AF.Exp, scale=-0.5)
            self.O("dve", "scalar_tensor_tensor", [self.b_ps[banks[tb]], self.b_vecs, b_rs], [b_out],
                   out=out_ap[:, tb * TB:(tb + 1) * TB], in0=self.ps[banks[tb]][:, 0:TB], scalar=self.vecs[:, l, gcol:gcol + 1],
                   in1=rs, op0=ALU.mult, op1=ALU.mult)

    def evac(self, banks, out_ap, b_out, engs=("dve", "act", "dve")):
        for tb in range(NTB):
            e = engs[tb]
            o = out_ap[:, tb * TB:(tb + 1) * TB]
            i = self.ps[banks[tb]][:, 0:TB]
            if e == "act":
                self.O("act", "copy", [self.b_ps[banks[tb]]], [b_out], out=o, in_=i)
            else:
                self.O(e, "tensor_copy", [self.b_ps[banks[tb]]], [b_out], out=o, in_=i)

    def phase_kv(self, l):
        P = self.P
        knf, knb, vf, vb, vtok = [], [], [], [], []
        o = 0
        for i in range(2):
            a, o = self.carve(self.R_C, o, [128, NTP], F32); knf.append(a)
        for i in range(2):
            a, o = self.carve(self.R_C, o, [128, NTP], BF16); knb.append(a)
        o = 0
        for i in range(2):
            a, o = self.carve(self.R_D, o, [128, NTP], F32); vf.append(a)
        for i in range(2):
            a, o = self.carve(self.R_D, o, [128, NTP], BF16); vb.append(a)
        for i in range(2):
            a, o = self.carve(self.R_D, o, [128, 4, 128], BF16); vtok.append(a)
        b_knf = [Buf("knf%d" % i) for i in range(2)]
        b_knb = [Buf("knb%d" % i) for i in range(2)]
        b_vf = [Buf("vf%d" % i) for i in range(2)]
        b_vb = [Buf("vb%d" % i) for i in range(2)]
        b_vtok = [Buf("vtok%d" % i) for i in range(2)]
        self.b_payK = [Buf("payK%d" % h) for h in range(8)]
        self.b_payV = [Buf("payV%d" % i) for i in range(16)]
        self.b_payI = Buf("payI")
        payVv = self.payV[l].ap().rearrange("(t p) c -> p t c", p=128)
        for h in range(8):
            s = h % 2
            banks = [0, 1, 2] if s == 0 else [3, 4, 5]
            self.dense(self.win_d[l, WB_INDEX[("ak", h)]], KC, self.h_rhs, [self.b_hT], banks)
            self.head_rms(banks, 33, knf[s], b_knf[s], l, [6, 7])
            self.DMA("sp", [b_knf[s]], [], self.kT_o[l, :, h, :], knf[s], is_output=True)
            self.O("act", "copy", [b_knf[s]], [b_knb[s]], out=knb[s], in_=knf[s])
            self.DMA("sp", [b_knb[s]], [self.b_payK[h]], self.payK[l].ap()[h * 128:(h + 1) * 128, :], knb[s][:, 0:NTOK])
            self.O("pool", "tensor_copy", [b_knb[s]], [self.b_KTs], out=self.KTs[:, h, :], in_=knb[s][:, NTOK:NT])
        for h in range(8):
            s = h % 2
            banks = [0, 1, 2] if s == 0 else [3, 4, 5]
            self.dense(self.win_d[l, WB_INDEX[("av", h)]], KC, self.h_rhs, [self.b_hT], banks)
            self.evac(banks, vf[s], b_vf[s])
            self.DMA("sp", [b_vf[s]], [], self.vT_o[l, :, h, :], vf[s], is_output=True)
            self.O("pool", "tensor_copy", [b_vf[s]], [b_vb[s]], out=vb[s], in_=vf[s])
            for g in range(2):
                pb = 6 + g
                for jj in range(4):
                    lt = g * 4 + jj
                    self.tr(self.psb[pb][:, jj * 128:(jj + 1) * 128], vb[s][:, lt * 128:(lt + 1) * 128], self.ident_b,
                            [b_vb[s], self.b_cst], [self.b_ps[pb]])
                self.O("dve", "tensor_copy", [self.b_ps[pb]], [b_vtok[g]], out=vtok[g],
                       in_=self.psb[pb][:, 0:512].rearrange("p (a b) -> p a b", a=4))
                self.DMA("sp", [b_vtok[g]], [self.b_payV[h * 2 + g]], payVv[:, g * 4:(g + 1) * 4, h * 128:(h + 1) * 128], vtok[g])
            self.tr(self.psb[6][0:NS, 512:640], vb[s][:, NTOK:NT], self.ident_b, [b_vb[s], self.b_cst], [self.b_ps[6]])
            self.O("dve", "tensor_copy", [self.b_ps[6]], [self.b_Vs], out=self.Vs[:, h, 0:128], in_=self.psb[6][0:NS, 512:640])
        banks = [0, 1, 2]
        self.dense(self.win_d[l, WB_INDEX[("small", 0)]], KC, self.h_rhs, [self.b_hT], banks)
        self.evac(banks, self.smallT, self.b_small)
        self.DMA("sp", [self.b_small], [], self.kiT_o[l], self.smallT[64:128, :], is_output=True)
        ikb, _ = self.carve(self.R_C, 3096, [128, NTP], BF16)
        b_ikb = Buf("ikb")
        P.op("act", lambda e: e.copy(out=ikb[64:128, :], in_=self.smallT[64:128, :]), [self.b_small], [b_ikb])
        self.DMA("sp", [b_ikb], [self.b_payI], self.payI[l].ap(), ikb[64:128, 0:NTOK])
        for lt in range(9):
            n = 128 if lt < 8 else NS
            self.tr(self.ps[7][0:n, lt * 16:(lt + 1) * 16], self.smallT[32:48, lt * 128:lt * 128 + n], self.ident_f[32:48, 32:48],
                    [self.b_small, self.b_cst], [self.b_ps[7]])
        self.O("dve", "tensor_copy", [self.b_ps[7]], [self.b_iw], out=self.iw_tok[:, 0:8, :],
               in_=self.ps[7][:, 0:128].rearrange("p (a b) -> p a b", a=8))
        self.O("dve", "tensor_copy", [self.b_ps[7]], [self.b_iw], out=self.iw_tok[0:NS, 8, :], in_=self.ps[7][0:NS, 128:144])
        self.O("act", "activation", [self.b_iw], [self.b_iw], out=self.iw_abs[:], in_=self.iw_tok[:], func=AF.Abs)
        self.O("act", "activation", [self.b_iw], [self.b_iw], out=self.iw_sgn[:], in_=self.iw_tok[:], func=AF.Sign)
        self.b_gK, self.b_gV, self.b_gI = Buf("gK"), Buf("gV"), Buf("gI")
        rg = [list(range(NCORES))]
        for pay, g, br, bw in ((self.payK[l], self.gK[l], self.b_payK, self.b_gK), (self.payV[l], self.gV[l], self.b_payV, self.b_gV),
                               (self.payI[l], self.gI[l], [self.b_payI], self.b_gI)):
            P.cc(lambda e, pay=pay, g=g: e.collective_compute("AllGather", ALU.bypass, replica_groups=rg,
                                                              ins=[pay.ap().opt()], outs=[g.ap().opt()]), br, [bw])

    def phase_q(self, l):
        for h in range(8):
            banks = [0, 1, 2] if h % 2 == 0 else [3, 4, 5]
            self.dense(self.win_d[l, WB_INDEX[("aq", h)]], KC, self.h_rhs, [self.b_hT], banks)
            self.head_rms(banks, 32, self.QT[:, h, :], self.b_QT[h], l, [6, 7])
        for i in range(8):
            banks = [0, 1, 2] if i % 2 == 0 else [3, 4, 5]
            self.dense(self.win_d[l, WB_INDEX[("iq", i)]], KC, self.h_rhs, [self.b_hT], banks)
            self.evac(banks, self.iqT[:, i, :], self.b_iqT[i], engs=("dve", "act", "pool") if False else ("dve", "act", "dve"))

    def setup_bias(self):
        o = 0
        acc, o = self.carve(self.R_h, o, [128, 8, 256], F32)
        bk, o = self.carve(self.R_h, o, [128, 256], F32)
        tb_, o = self.carve(self.R_h, o, [128, 256], F32)
        mb, o = self.carve(self.R_h, o, [128, 256], F32)
        neg, o = self.carve(self.R_h, o, [128, 8], F32)
        tmp, o = self.carve(self.R_h, o, [128, 128], F32)
        gs = []
        for i in range(2):
            a, o = self.carve(self.R_h, o, [128, 9, 128], BF16)
            gs.append(a)
        b_acc, b_bk, b_tb, b_mb, b_neg, b_tmp = Buf("acc"), Buf("bk"), Buf("tb"), Buf("mb"), Buf("neg"), Buf("tmp")
        b_gs = [Buf("gs0"), Buf("gs1")]
        self.DMA("sp", [], [b_bk], bk, self.bk_d)
        self.DMA("sp", [], [b_tb], tb_, self.rel_d)
        for b in range(32):
            self.O("dve", "tensor_scalar", [b_bk], [b_mb], out=mb, in0=bk, scalar1=float(b), scalar2=None, op0=ALU.is_equal)
            for h in range(8):
                col = tb_[:, b * 8 + h:b * 8 + h + 1]
                if b == 0:
                    self.O("dve", "tensor_scalar", [b_mb, b_tb], [b_acc], out=acc[:, h, :], in0=mb, scalar1=col, scalar2=None, op0=ALU.mult)
                else:
                    self.O("dve", "scalar_tensor_tensor", [b_mb, b_tb, b_acc], [b_acc], out=acc[:, h, :], in0=mb, scalar=col,
                           in1=acc[:, h, :], op0=ALU.mult, op1=ALU.add)
        self.O("dve", "tensor_scalar", [b_tb], [b_neg], out=neg, in0=tb_[:, 248:256], scalar1=-1.0, scalar2=None, op0=ALU.mult)
        for h in range(8):
            self.O("act", "activation", [b_acc, b_neg], [b_acc], out=acc[:, h, :], in_=acc[:, h, :], func=AF.Exp, bias=neg[:, h:h + 1], scale=1.0)
        self.O("dve", "tensor_scalar", [b_acc], [b_acc], out=acc, in0=acc, scalar1=-1.0, scalar2=None, op0=ALU.add)
        for h in range(8):
            g = gs[h % 2]
            for idx in range(9):
                self.O("dve", "tensor_scalar", [b_acc, self.b_coef], [b_tmp], out=tmp, in0=acc[:, h, 0:128], scalar1=self.coef[:, idx:idx + 1],
                       scalar2=1.0, op0=ALU.mult, op1=ALU.add)
                self.O("dve", "scalar_tensor_tensor", [b_acc, self.b_coef, b_tmp], [b_gs[h % 2]], out=g[:, idx, :], in0=acc[:, h, 128:256],
                       scalar=self.coef[:, 9 + idx:10 + idx], in1=tmp, op0=ALU.mult, op1=ALU.add)
            self.DMA("sp", [b_gs[h % 2]], [self.b_Gd], self.Gd.ap()[h], g.rearrange("p a b -> p (a b)"))

    def attn_prompt(self, l, n_iter=16):
        P = self.P
        gIv = self.gI[l].ap().rearrange("(r d) t -> d r t", d=64)
        gKv = self.gK[l].ap().rearrange("(r hd) t -> hd r t", hd=1024)
        gVv = self.gV[l].ap().rearrange("(r t p) c -> p r t c", r=8, t=8, p=128)
        cm, _ = self.carve(self.R_X, 1548, [128, 8 * 128], F32)
        b_cm = Buf("cm")
        self.DMA("sp", [], [b_cm], cm, self.cm_d.rearrange("p a b -> p (a b)"))
        maskT, _ = self.carve(self.R_D, 0, [128, 32, 128], BF16)
        for b in range(BATCH):
            for m in range(4):
                lt = b * 4 + m
                cols = slice(lt * 128, (lt + 1) * 128)
                nkt = 8 * m + 8
                L = nkt * 128
                sc, _ = self.carve(self.R_h, 0, [128, 4096], F32)
                junk, _ = self.carve(self.R_h, 4096, [128, 4096], BF16)
                ik, _ = self.carve(self.R_h, 6144, [128, 32, 128], BF16)
                tmp = [self.carve(self.R_X, 2580 + 512 * i, [128, 512], F32)[0] for i in range(2)]
                tiny, _ = self.carve(self.R_X, 3604, [128, 8], F32)
                lo, hi, mid, cnt, ge, t1, t2 = [tiny[:, i:i + 1] for i in range(7)]
                b_sc = [Buf("sc%d" % i) for i in range(8)]
                b_junk, b_ik, b_tiny = Buf("junk"), Buf("ik"), Buf("tiny")
                b_tmp = [Buf("tmp0"), Buf("tmp1")]
                b_maskT = Buf("maskT")
                for mm in range(m + 1):
                    for dup in range(2):
                        self.DMA("sp" if dup == 0 else "act", [self.b_gI], [b_ik], ik[dup * 64:(dup + 1) * 64, mm * 8:(mm + 1) * 8, :],
                                 gIv[:, :, (4 * b + mm) * 128:(4 * b + mm + 1) * 128])
                k = 0
                for h in range(16):
                    for kb in range(nkt // 4):
                        bank = k % 4
                        tb_ = tmp[k % 2]
                        k += 1
                        hp = (h % 2) * 64
                        self.mm(self.ps[bank][:, 0:512], self.iqT[hp:hp + 64, h // 2, cols], ik[hp:hp + 64, kb * 4:(kb + 1) * 4, :], True, True,
                                [self.b_iqT[h // 2], b_ik], [self.b_ps[bank]])
                        self.O("act", "activation", [self.b_ps[bank], self.b_iw], [b_tmp[k % 2]], out=tb_, in_=self.ps[bank][:, 0:512],
                               func=AF.Relu, scale=self.iw_abs[:, lt, h:h + 1])
                        dst = sc[:, kb * 512:(kb + 1) * 512]
                        if h == 0:
                            self.O("dve", "tensor_scalar", [b_tmp[k % 2], self.b_iw], [b_sc[kb]], out=dst, in0=tb_, scalar1=self.iw_sgn[:, lt, h:h + 1],
                                   scalar2=None, op0=ALU.mult)
                        else:
                            self.O("dve", "scalar_tensor_tensor", [b_tmp[k % 2], self.b_iw, b_sc[kb]], [b_sc[kb]], out=dst, in0=tb_,
                                   scalar=self.iw_sgn[:, lt, h:h + 1], in1=dst, op0=ALU.mult, op1=ALU.add)
                allsc = b_sc[0:nkt // 4]
                self.O("dve", "tensor_reduce", allsc, [b_tiny], out=lo, in_=sc[:, 0:L], axis=AX.X, op=ALU.min)
                last = sc[:, L - 1024:L]
                self.O("dve", "tensor_tensor", allsc + [b_cm], allsc, out=last, in0=last, in1=cm, op=ALU.add)
                self.O("dve", "tensor_reduce", allsc, [b_tiny], out=hi, in_=sc[:, 0:L], axis=AX.X, op=ALU.max)
                Wt, _ = self.carve(self.R_X, 3612, [128, 16], F32)
                self.O("dve", "tensor_tensor", [b_tiny], [b_tiny], out=t1, in0=hi, in1=lo, op=ALU.subtract)
                self.O("dve", "tensor_scalar", [b_tiny, self.b_pw], [b_tiny], out=Wt[:, 0:n_iter], in0=self.pw[:, 0:n_iter], scalar1=t1, scalar2=None, op0=ALU.mult)
                self.O("dve", "tensor_tensor", [b_tiny], [b_tiny], out=mid, in0=lo, in1=Wt[:, 0:1], op=ALU.add)
                for it in range(n_iter):
                    self.O("dve", "tensor_scalar", allsc + [b_tiny], [b_junk, b_tiny], out=junk[:, 0:L], in0=sc[:, 0:L], scalar1=mid, scalar2=None,
                           op0=ALU.is_ge, op1=ALU.add, accum_out=cnt)
                    if it < n_iter - 1:
                        self.O("dve", "tensor_scalar", [b_tiny], [b_tiny], out=ge, in0=cnt, scalar1=float(TOPK) - 0.5, scalar2=-0.5, op0=ALU.is_ge, op1=ALU.add)
                        self.O("dve", "scalar_tensor_tensor", [b_tiny], [b_tiny], out=mid, in0=ge, scalar=Wt[:, it:it + 1], in1=mid, op0=ALU.mult, op1=ALU.add)
                    else:
                        self.O("dve", "tensor_scalar", [b_tiny], [b_tiny], out=ge, in0=cnt, scalar1=float(TOPK) - 0.5, scalar2=-1.0, op0=ALU.is_ge, op1=ALU.add)
                        self.O("dve", "scalar_tensor_tensor", [b_tiny], [b_tiny], out=lo, in0=ge, scalar=Wt[:, it:it + 1], in1=mid, op0=ALU.mult, op1=ALU.add)
                self.O("dve", "tensor_scalar", allsc + [b_tiny], [b_junk], out=junk[:, 0:L], in0=sc[:, 0:L], scalar1=lo, scalar2=None, op0=ALU.is_ge)
                for g in range(nkt // 8):
                    bank = 4 + (g % 2)
                    for jj in range(8):
                        j = g * 8 + jj
                        self.tr(self.psb[bank][:, jj * 128:(jj + 1) * 128], junk[:, j * 128:(j + 1) * 128], self.ident_b, [b_junk, self.b_cst], [self.b_ps[bank]])
                    src = self.psb[bank][:, 0:1024].rearrange("p (a b) -> p a b", a=8)
                    if g % 2 == 0:
                        self.O("act", "copy", [self.b_ps[bank]], [b_maskT], out=maskT[:, g * 8:(g + 1) * 8, :], in_=src)
                    else:
                        self.O("pool", "tensor_copy", [self.b_ps[bank]], [b_maskT], out=maskT[:, g * 8:(g + 1) * 8, :], in_=src) if False else \
                            self.O("dve", "tensor_copy", [self.b_ps[bank]], [b_maskT], out=maskT[:, g * 8:(g + 1) * 8, :], in_=src)
                self.barrier()
                KTb = [self.carve(self.R_h, 2048 * i, [128, 32, 128], BF16)[0] for i in range(2)]
                Vb = [self.carve(self.R_h, 4096 + 2064 * i, [128, 32, 129], BF16)[0] for i in range(2)]
                pt = [self.carve(self.R_D, 2048 + 256 * i, [128, 4, 128], BF16)[0] for i in range(2)]
                oa, _ = self.carve(self.R_D, 2560, [128, 1024], BF16)
                mE, _ = self.carve(self.R_D, 3072, [128, 12, 128], BF16)
                Gb = [self.carve(self.R_X, 2580 + 576 * i, [128, 9, 128], BF16)[0] for i in range(2)]
                rec, _ = self.carve(self.R_X, 3740, [128, 2], F32)
                b_KT, b_V, b_pt, b_G = [Buf("KT0"), Buf("KT1")], [Buf("V0"), Buf("V1")], [Buf("pt0"), Buf("pt1")], [Buf("G0"), Buf("G1")]
                b_oa, b_mE, b_rec = Buf("oa"), Buf("mE"), Buf("rec")
                for i in range(2):
                    self.O("pool", "memset", [], [b_V[i]], Vb[i][:, :, 128:129], 1.0)
                for h in range(8):
                    s = h % 2
                    for mm in range(m + 1):
                        self.DMA("sp", [self.b_gK], [b_KT[s]], KTb[s][:, mm * 8:(mm + 1) * 8, :],
                                 gKv[h * 128:(h + 1) * 128, :, (4 * b + mm) * 128:(4 * b + mm + 1) * 128])
                        self.DMA("act", [self.b_gV], [b_V[s]], Vb[s][:, mm * 8:(mm + 1) * 8, 0:128], gVv[:, :, 4 * b + mm, h * 128:(h + 1) * 128])
                    self.DMA("sp", [self.b_Gd], [b_G[s]], Gb[s].rearrange("p a b -> p (a b)"), self.Gd.ap()[h])
                    if m >= 1:
                        self.O("pool", "tensor_copy", [b_maskT], [b_mE], out=mE[:, 0:3, :], in_=maskT[:, 8 * m - 4:8 * m - 1, :])
                        self.O("pool", "tensor_tensor", [b_maskT, b_G[s]], [b_mE], out=mE[:, 3:12, :], in0=maskT[:, 8 * m - 1:8 * m + 8, :], in1=Gb[s], op=ALU.mult)
                    else:
                        self.O("pool", "tensor_tensor", [b_maskT, b_G[s]], [b_mE], out=mE[:, 4:12, :], in0=maskT[:, 0:8, :], in1=Gb[s][:, 1:9, :], op=ALU.mult)
                    pob = 2 + s
                    ngr = nkt // 4
                    for kg in range(ngr):
                        bank = kg % 2
                        for jj in range(4):
                            self.mm(self.ps[bank][:, jj * 128:(jj + 1) * 128], KTb[s][:, kg * 4 + jj, :], self.QT[:, h, cols], True, True,
                                    [b_KT[s], self.b_QT[h]], [self.b_ps[bank]])
                        p_ = pt[kg % 2]
                        self.O("act", "activation", [self.b_ps[bank]], [b_pt[kg % 2]], out=p_, in_=self.ps[bank][:, 0:512].rearrange("p (a b) -> p a b", a=4),
                               func=AF.Exp, scale=128.0 ** -0.5)
                        if kg < 2 * m - 1:
                            mk, bm = maskT[:, kg * 4:(kg + 1) * 4, :], b_maskT
                        else:
                            o4 = (kg - (2 * m - 1)) * 4
                            mk, bm = mE[:, o4:o4 + 4, :], b_mE
                        self.O("dve", "tensor_tensor", [b_pt[kg % 2], bm], [b_pt[kg % 2]], out=p_, in0=p_, in1=mk, op=ALU.mult)
                        for jj in range(4):
                            j = kg * 4 + jj
                            self.mm(self.ps[pob][:, 0:129], p_[:, jj, :], Vb[s][:, j, :], j == 0, j == nkt - 1, [b_pt[kg % 2], b_V[s]], [self.b_ps[pob]])
                    self.O("dve", "reciprocal", [self.b_ps[pob]], [b_rec], out=rec[:, s:s + 1], in_=self.ps[pob][:, 128:129])
                    self.O("dve", "tensor_scalar", [self.b_ps[pob], b_rec], [b_oa], out=oa[:, h * 128:(h + 1) * 128], in0=self.ps[pob][:, 0:128],
                           scalar1=rec[:, s:s + 1], scalar2=None, op0=ALU.mult)
                for ch in range(8):
                    self.tr(self.psb[4][:, ch * 128:(ch + 1) * 128], oa[:, ch * 128:(ch + 1) * 128], self.ident_b, [b_oa, self.b_cst], [self.b_ps[4]])
                self.O("act", "copy", [self.b_ps[4]], [self.b_oaT], out=self.oaT[:, :, cols], in_=self.psb[4][:, 0:1024].rearrange("p (a b) -> p a b", a=8))
                self.barrier()

    def lin_attn(self, l, typ):
        P = self.P
        names = ("gq", "gk", "gv", "gg") if typ == 0 else ("rq", "rk", "rv", "rg")
        o = 0
        qf, o = self.carve(self.R_AB, o, [128, 2, NTP], F32)
        kf, o = self.carve(self.R_AB, o, [128, 2, NTP], F32)
        vb, o = self.carve(self.R_AB, o, [128, 4, NTP], BF16)
        sg, o = self.carve(self.R_AB, o, [128, 4, NTP], BF16)
        b_qf = [Buf("qf%d" % i) for i in range(2)]
        b_kf = [Buf("kf%d" % i) for i in range(2)]
        b_vb = [Buf("vb%d" % i) for i in range(4)]
        b_sg = [Buf("sg%d" % i) for i in range(4)]
        X = self.R_X
        qt, _ = self.carve(X, 1548, [128, 2, NTP], BF16)
        b_qt = [Buf("qt0"), Buf("qt1")]
        T1 = 2580
        la, _ = self.carve(X, T1, [128, 256], F32)
        ebq, _ = self.carve(X, T1 + 256, [128, 2, 128], F32)
        ebk, _ = self.carve(X, T1 + 512, [128, 2, 128], F32)
        rtab, _ = self.carve(X, T1, [128, 4, 2, 128], F32)
        kt, _ = self.carve(X, 3604, [128, 2, 128], BF16)
        AT = [self.carve(X, 3732 + 64 * i, [128, 128], BF16)[0] for i in range(2)]
        vtok = [self.carve(X, 3860 + 64 * i, [128, 128], BF16)[0] for i in range(2)]
        ktok = [self.carve(X, 3988 + 32 * i, [128, 64], BF16)[0] for i in range(2)]
        dvec, _ = self.carve(X, 4052, [128, 4], F32)
        tmpf, _ = self.carve(X, 4056, [128, 128], F32)
        Ust, _ = self.carve(X, 4184, [128, 4, 128], F32)
        gab, _ = self.carve(X, 4696, [128, NTP], BF16)
        wab, _ = self.carve(X, 5212, [128, 256], BF16)
        bab, _ = self.carve(X, 5340, [128, 256], BF16)
        Us_s, _ = self.carve(X, 5468, [128, 4, 128], F32)
        dv_s, _ = self.carve(X, 5980, [128, 4], F32)
        dret, _ = self.carve(X, 5984, [128, 2, 4], F32)
        b_la, b_eb, b_rtab, b_kt = Buf("la"), Buf("eb"), Buf("rtab"), Buf("kt")
        b_AT, b_vtok, b_ktok = [Buf("AT0"), Buf("AT1")], [Buf("vt0"), Buf("vt1")], [Buf("kk0"), Buf("kk1")]
        b_dvec, b_tmpf, b_Ust, b_gab, b_wab, b_Us_s, b_dret = Buf("dvec"), Buf("tmpf"), Buf("Ust"), Buf("gab"), Buf("wab"), Buf("Us_s"), Buf("dret")
        for i in range(2):
            banks = [0, 1, 2] if i % 2 == 0 else [3, 4, 5]
            self.dense(self.win_d[l, WB_INDEX[(names[0], i)]], KC, self.h_rhs, [self.b_hT], banks)
            self.evac(banks, qf[:, i, :], b_qf[i])
        for i in range(2):
            banks = [0, 1, 2] if i % 2 == 0 else [3, 4, 5]
            self.dense(self.win_d[l, WB_INDEX[(names[1], i)]], KC, self.h_rhs, [self.b_hT], banks)
            self.evac(banks, kf[:, i, :], b_kf[i])
        for i in range(4):
            banks = [0, 1, 2] if i % 2 == 0 else [3, 4, 5]
            self.dense(self.win_d[l, WB_INDEX[(names[2], i)]], KC, self.h_rhs, [self.b_hT], banks)
            self.evac(banks, vb[:, i, :], b_vb[i])
        for i in range(4):
            banks = [0, 1, 2] if i % 2 == 0 else [3, 4, 5]
            self.dense(self.win_d[l, WB_INDEX[(names[3], i)]], KC, self.h_rhs, [self.b_hT], banks)
            for tb in range(NTB):
                self.O("act", "activation", [self.b_ps[banks[tb]]], [b_sg[i]], out=sg[:, i, tb * TB:(tb + 1) * TB], in_=self.ps[banks[tb]][:, 0:TB], func=AF.Silu)
        if typ == 0:
            self.O("act", "copy", [self.b_small], [b_gab], out=gab[0:16, :], in_=self.smallT[0:16, :])
            P.dma("pool", lambda e: e.dma_start(out=wab[0:16, :], in_=self.wa_d[l]), [], [b_wab])
            P.dma("pool", lambda e: e.dma_start(out=bab[0:1, :], in_=self.ba_d[l]), [], [b_wab])
        else:
            self.DMA("sp", [], [b_dret], dret[0:64, :, :], self.dret_d)
        payUv = self.payU[l][typ].ap().rearrange("(t h k) v -> k t h v", t=8, h=4, k=64)
        b_payU = [Buf("payU%d" % i) for i in range(8)]
        b_payD = [Buf("payD%d" % i) for i in range(8)]
        for lt in range(9):
            n = 128 if lt < 8 else NS
            c0 = lt * 128
            cs = slice(c0, c0 + n)
            if typ == 0:
                self.mm(self.ps[6][0:n, 0:256], gab[0:16, cs], wab[0:16, :], True, False, [b_gab, b_wab], [self.b_ps[6]])
                self.mm(self.ps[6][0:n, 0:256], self.ones_b[0:1, 0:n], bab[0:1, :], False, True, [self.b_cst, b_wab], [self.b_ps[6]])
                self.O("act", "activation", [self.b_ps[6]], [b_la], out=la[0:n, :], in_=self.ps[6][0:n, 0:256], func=AF.Exp, scale=-1.0)
                self.O("act", "activation", [b_la], [b_la], out=la[0:n, :], in_=la[0:n, :], func=AF.Ln, bias=1.0, scale=1.0)
                for i in range(2):
                    self.mm(self.ps[7][:, i * 128:i * 128 + n], la[0:n, i * 128:(i + 1) * 128], self.tri_f[0:n, 0:n], True, True,
                            [b_la, self.b_cst], [self.b_ps[7]])
                for hh in range(4):
                    self.mm(self.ps[7][0:64, 256 + 2 * hh:256 + 2 * hh + 1], la[0:n, hh * 64:(hh + 1) * 64], self.ones_f[0:n, 0:1], True, True,
                            [b_la, self.b_cst], [self.b_ps[7]])
                for i in range(2):
                    self.O("act", "activation", [self.b_ps[7]], [b_eb], out=ebq[:, i, 0:n], in_=self.ps[7][:, i * 128:i * 128 + n], func=AF.Exp, scale=-1.0 / 16)
                    self.O("act", "activation", [self.b_ps[7]], [b_eb], out=ebk[:, i, 0:n], in_=self.ps[7][:, i * 128:i * 128 + n], func=AF.Exp, scale=1.0 / 16)
                    self.O("dve", "scalar_tensor_tensor", [b_qf[i], b_eb], [b_qt[i]], out=qt[:, i, cs], in0=qf[:, i, cs], scalar=0.125, in1=ebq[:, i, 0:n],
                           op0=ALU.mult, op1=ALU.mult)
                    self.O("dve", "tensor_tensor", [b_kf[i], b_eb], [b_kt], out=kt[:, i, 0:n], in0=kf[:, i, cs], in1=ebk[:, i, 0:n], op=ALU.mult)
                self.O("act", "activation", [self.b_ps[7]], [b_dvec], out=dvec[0:64, :], in_=self.ps[7][0:64, 256:264].rearrange("p (a b) -> p a b", b=2)[:, :, 0],
                       func=AF.Exp, scale=-1.0 / 16)
            else:
                self.DMA("sp", [], [b_rtab], rtab, self.rtab_d[lt])
                for i in range(2):
                    for src, bsrc, which, dst, bdst, dcs in ((qf, b_qf[i], 0, qt, b_qt[i], cs), (kf, b_kf[i], 2, kt, b_kt, slice(0, n))):
                        self.mm(self.ps[7][:, 0:n], self.perm_f, src[:, i, cs], True, True, [self.b_cst, bsrc], [self.b_ps[7]])
                        self.O("dve", "tensor_tensor", [bsrc, b_rtab], [b_tmpf], out=tmpf[:, 0:n], in0=src[:, i, cs], in1=rtab[:, which, i, 0:n], op=ALU.mult)
                        self.O("dve", "tensor_tensor", [self.b_ps[7], b_rtab], [self.b_ps[7]], out=self.ps[7][:, 128:128 + n], in0=self.ps[7][:, 0:n],
                               in1=rtab[:, which + 1, i, 0:n], op=ALU.mult)
                        self.O("dve", "tensor_tensor", [self.b_ps[7], b_tmpf], [bdst], out=dst[:, i, dcs], in0=self.ps[7][:, 128:128 + n], in1=tmpf[:, 0:n], op=ALU.add)
                self.O("dve", "tensor_copy", [b_dret], [b_dvec], out=dvec[0:64, :], in_=dret[0:64, 0 if lt < 8 else 1, :])
            for hh in range(4):
                i = hh // 2
                hp = (hh % 2) * 64
                s = hh % 2
                self.mm(self.ps[s][0:n, 0:n], kt[hp:hp + 64, i, 0:n], qt[hp:hp + 64, i, cs], True, True, [b_kt, b_qt[i]], [self.b_ps[s]])
                self.O("dve", "tensor_tensor", [self.b_ps[s], self.b_cst], [b_AT[s]], out=AT[s][0:n, 0:n], in0=self.ps[s][0:n, 0:n], in1=self.tri_f[0:n, 0:n], op=ALU.mult)
                self.tr(self.psb[2][0:n, s * 128:(s + 1) * 128], vb[:, hh, cs], self.ident_b, [b_vb[hh], self.b_cst], [self.b_ps[2]])
                self.O("act", "copy", [self.b_ps[2]], [b_vtok[s]], out=vtok[s][0:n, :], in_=self.psb[2][0:n, s * 128:(s + 1) * 128])
                self.tr(self.psb[3][0:n, s * 64:(s + 1) * 64], kt[hp:hp + 64, i, 0:n], self.ident_b[hp:hp + 64, hp:hp + 64], [b_kt, self.b_cst], [self.b_ps[3]])
                self.O("act", "copy", [self.b_ps[3]], [b_ktok[s]], out=ktok[s][0:n, :], in_=self.psb[3][0:n, s * 64:(s + 1) * 64])
                self.mm(self.ps[4 + s][:, 0:n], vtok[s][0:n, :], AT[s][0:n, 0:n], True, True, [b_vtok[s], b_AT[s]], [self.b_ps[4 + s]])
                self.O("act", "copy", [self.b_ps[4 + s]], [self.b_obT[typ * 4 + hh]], out=self.obT[:, typ * 4 + hh, cs], in_=self.ps[4 + s][:, 0:n])
                self.mm(self.ps[6][0:64, 256 + s * 128:256 + (s + 1) * 128], ktok[s][0:n, :], vtok[s][0:n, :], True, True, [b_ktok[s], b_vtok[s]], [self.b_ps[6]])
                dstU = Ust if lt < 8 else Us_s
                self.O("dve", "tensor_scalar", [self.b_ps[6], b_dvec], [b_Ust if lt < 8 else b_Us_s], out=dstU[0:64, hh, :],
                       in0=self.ps[6][0:64, 256 + s * 128:256 + (s + 1) * 128], scalar1=dvec[0:64, hh:hh + 1], scalar2=None, op0=ALU.mult)
            if lt < 8:
                self.DMA("sp", [b_Ust], [b_payU[lt]], payUv[:, lt, :, :], Ust[0:64, :, :])
                self.DMA("sp", [b_dvec], [b_payD[lt]], self.payD[l][typ].ap()[:, lt * 4:(lt + 1) * 4], dvec[0:64, :])
            else:
                self.O("dve", "tensor_copy", [b_dvec], [b_Us_s], out=dv_s[0:64, :], in_=dvec[0:64, :])
        b_gU, b_gD = Buf("gU"), Buf("gD")
        rg = [list(range(NCORES))]
        for pay, g, br, bw in ((self.payU[l][typ], self.gU[l][typ], b_payU, b_gU), (self.payD[l][typ], self.gD[l][typ], b_payD, b_gD)):
            P.cc(lambda e, pay=pay, g=g: e.collective_compute("AllGather", ALU.bypass, replica_groups=rg,
                                                              ins=[pay.ap().opt()], outs=[g.ap().opt()]), br, [bw])
        self.barrier()
        SU, _ = self.carve(self.R_AB, 0, [128, 32, 128], F32)
        SDall, _ = self.carve(X, T1, [128, 8, 32], F32)
        SD, _ = self.carve(X, T1 + 256, [128, 32], F32)
        S, _ = self.carve(X, T1 + 288, [128, 128], F32)
        Sin, _ = self.carve(X, T1 + 416, [128, 4, 128], F32)
        SinB, _ = self.carve(X, T1 + 928, [128, 9, 128], BF16)
        S0h, _ = self.carve(X, T1 + 1504, [128, 128], F32)
        S0p, _ = self.carve(X, T1 + 1632, [128, 128], F32)
        b_S0p = Buf("S0p")
        of, _ = self.carve(self.R_AB, 4128, [128, NTP], F32)
        b_SU, b_SDall, b_SD, b_S, b_Sin, b_SinB, b_S0h, b_of = Buf("SU"), Buf("SDall"), Buf("SD"), Buf("S"), Buf("Sin"), Buf("SinB"), Buf("S0h"), Buf("of")
        gUv = self.gU[l][typ].ap().rearrange("(r t h k) v -> k r t h v", r=8, t=8, h=4, k=64)
        gDv = self.gD[l][typ].ap().rearrange("(r k) c -> k r c", k=64)
        for half in range(2):
            self.DMA("sp", [b_gD], [b_SDall], SDall[half * 64:(half + 1) * 64, :, :], gDv)
        st_in = (self.sgla_d if typ == 0 else self.sret_d) if self.use_state else None
        st_out_p = self.glap_o if typ == 0 else self.retp_o
        st_out_s = self.glas_o if typ == 0 else self.rets_o
        SDv = SDall.rearrange("p r (t h) -> p t r h", h=4)
        for pr in range(2):
            for b in range(BATCH):
                for mm in range(4):
                    for half in range(2):
                        self.DMA("sp" if half == 0 else "act", [b_gU], [b_SU], SU[half * 64:(half + 1) * 64, mm * 8:(mm + 1) * 8, :],
                                 gUv[:, :, 4 * b + mm, 2 * pr + half, :])
                for half in range(2):
                    hs = slice(half * 64, (half + 1) * 64)
                    self.O("dve", "tensor_copy", [b_SDall], [b_SD], out=SD[hs, :].rearrange("p (t r) -> p t r", r=8),
                           in_=SDv[hs, 4 * b:4 * b + 4, :, 2 * pr + half])
                self.O("pool", "memset", [], [b_S], S, 0.0)
                self.O("pool", "memset", [], [b_Sin], Sin, 0.0)
                for j in range(32):
                    mm, r = j // 8, j % 8
                    self.O("dve", "scalar_tensor_tensor", [b_S, self.b_coef, b_Sin], [b_Sin], out=Sin[:, mm, :], in0=S, scalar=self.coef[:, 18 + r:19 + r],
                           in1=Sin[:, mm, :], op0=ALU.mult, op1=ALU.add)
                    self.O("dve", "scalar_tensor_tensor", [b_S, b_SD, b_SU], [b_S], out=S, in0=S, scalar=SD[:, j:j + 1], in1=SU[:, j, :],
                           op0=ALU.mult, op1=ALU.add)
                for half in range(2):
                    self.DMA("sp", [b_S], [], st_out_p[l, b, 2 * pr + half], S[half * 64:(half + 1) * 64, :], is_output=True)
                self.O("act", "copy", [b_Sin], [b_SinB], out=SinB[:, 4 * b:4 * b + 4, :], in_=Sin)
            if self.use_state:
                for half in range(2):
                    hh = 2 * pr + half
                    hs_ = slice(half * 64, (half + 1) * 64)
                    self.DMA("act", [], [b_S0p], S0p[hs_, :], st_in[l, hh])
                    P.op("act", lambda e, hs_=hs_: e.copy(out=SinB[hs_, 8, :], in_=S0p[hs_, :]), [b_S0p], [b_SinB])
                    self.DMA("sp", [], [b_S0h], S0h[0:64, :], st_in[l, hh])
                    self.O("dve", "scalar_tensor_tensor", [b_S0h, b_Us_s], [b_S0h], out=S0h[0:64, :], in0=S0h[0:64, :], scalar=dv_s[0:64, hh:hh + 1],
                           in1=Us_s[0:64, hh, :], op0=ALU.mult, op1=ALU.add)
                    self.DMA("sp", [b_S0h], [], st_out_s[l, hh], S0h[0:64, :], is_output=True)
            else:
                self.O("pool", "memset", [], [b_SinB], SinB[:, 8, :], 0.0)
            for half in range(2):
                hh = 2 * pr + half
                hp = half * 64
                ch = typ * 4 + hh
                for lt in range(9):
                    n = 128 if lt < 8 else NS
                    cs = slice(lt * 128, lt * 128 + n)
                    bank = lt % 2
                    self.mm(self.ps[bank][:, 0:n], SinB[hp:hp + 64, lt, :], qt[hp:hp + 64, pr, cs], True, True, [b_SinB, b_qt[pr]], [self.b_ps[bank]])
                    self.O("dve", "tensor_tensor", [self.b_ps[bank], self.b_obT[ch]], [b_of], out=of[:, cs], in0=self.ps[bank][:, 0:n], in1=self.obT[:, ch, cs], op=ALU.add)
                self.O("pool", "memset", [], [b_of], of[:, NT:NTP], 0.0)
                self.head_norm_sb(of, b_of, 34 + typ, l, center=(typ == 1))
                self.O("dve", "tensor_tensor", [b_of, b_sg[hh]], [self.b_obT[ch]], out=self.obT[:, ch, :], in0=of, in1=sg[:, hh, :], op=ALU.mult)
        self.barrier()

    def head_norm_sb(self, src, b_src, gcol, l, center):
        sqh, _ = self.carve(self.R_X, 1032, [128, TB], BF16)
        rs, _ = self.carve(self.R_X, 1204, [128, TB], F32)
        for tb in range(NTB):
            sl = slice(tb * TB, (tb + 1) * TB)
            b_sq, b_rs = self.b_sqh, self.b_rs
            bank = 2 + (tb % 2)
            if center:
                self.O("act", "copy", [b_src], [b_sq], out=sqh, in_=src[:, sl])
                self.mm(self.ps[bank][:, 0:TB], self.ones_b, sqh, True, True, [self.b_cst, b_sq], [self.b_ps[bank]])
                self.O("dve", "scalar_tensor_tensor", [self.b_ps[bank], b_src], [b_src], out=src[:, sl], in0=self.ps[bank][:, 0:TB], scalar=-1.0 / 128,
                       in1=src[:, sl], op0=ALU.mult, op1=ALU.add)
            self.O("act", "activation", [b_src], [b_sq], out=sqh, in_=src[:, sl], func=AF.Square)
            self.mm(self.ps[bank][:, 0:TB], self.ones_b, sqh, True, True, [self.b_cst, b_sq], [self.b_ps[bank]])
            self.O("act", "activation", [self.b_ps[bank], self.b_eps], [b_rs], out=rs, in_=self.ps[bank][:, 0:TB], func=AF.Ln, scale=1.0 / 128, bias=self.epsb[:])
            self.O("act", "activation", [b_rs], [b_rs], out=rs, in_=rs, func=AF.Exp, scale=-0.5)
            self.O("dve", "scalar_tensor_tensor", [b_src, self.b_vecs, b_rs], [b_src], out=src[:, sl], in0=src[:, sl], scalar=self.vecs[:, l, gcol:gcol + 1],
                   in1=rs, op0=ALU.mult, op1=ALU.mult)

    def merge(self, l):
        X = self.R_X
        sgt = [self.carve(X, 1548 + 516 * i, [128, NTP], BF16)[0] for i in range(3)]
        mf, _ = self.carve(X, 3096, [128, NTP], F32)
        tf, _ = self.carve(X, 4128, [128, NTP], F32)
        b_sgt = [Buf("sgt%d" % i) for i in range(3)]
        b_mf, b_tf = Buf("mf"), Buf("tf")
        k = 0
        for blk in range(16):
            for br in range(3):
                banks = [0, 1, 2] if k % 2 == 0 else [3, 4, 5]
                k += 1
                self.dense(self.win_d[l, WB_INDEX[("gates", br * 16 + blk)]], KC, self.h_rhs, [self.b_hT], banks)
                for tb in range(NTB):
                    self.O("act", "activation", [self.b_ps[banks[tb]]], [b_sgt[br]], out=sgt[br][:, tb * TB:(tb + 1) * TB], in_=self.ps[banks[tb]][:, 0:TB], func=AF.Sigmoid)
            for br, (k0, nk, src, bsrc) in enumerate(((0, 8, self.oaT, [self.b_oaT]), (8, 4, self.obT[:, 0:4, :], self.b_obT[0:4]), (12, 4, self.obT[:, 4:8, :], self.b_obT[4:8]))):
                banks = [0, 1, 2] if k % 2 == 0 else [3, 4, 5]
                k += 1
                self.dense(self.wbr_d[l, blk][:, k0:k0 + nk, :], nk, lambda kk, tb, src=src: src[:, kk, tb * TB:(tb + 1) * TB], bsrc, banks)
                for tb in range(NTB):
                    sl = slice(tb * TB, (tb + 1) * TB)
                    if br == 0:
                        self.O("dve", "tensor_tensor", [self.b_ps[banks[tb]], b_sgt[br]], [b_mf], out=mf[:, sl], in0=self.ps[banks[tb]][:, 0:TB], in1=sgt[br][:, sl], op=ALU.mult)
                    else:
                        self.O("dve", "tensor_tensor", [self.b_ps[banks[tb]], b_sgt[br]], [b_tf], out=tf[:, sl], in0=self.ps[banks[tb]][:, 0:TB], in1=sgt[br][:, sl], op=ALU.mult)
                        if br == 1:
                            self.O("pool", "tensor_tensor", [b_mf, b_tf], [b_mf], out=mf[:, sl], in0=mf[:, sl], in1=tf[:, sl], op=ALU.add)
                        else:
                            self.O("pool", "tensor_tensor", [b_mf, b_tf], [self.b_mgT[blk]], out=self.mgT[:, blk, sl], in0=mf[:, sl], in1=tf[:, sl], op=ALU.add)

    def out_proj(self, l):
        for blk in range(16):
            banks = [0, 1, 2] if blk % 2 == 0 else [3, 4, 5]
            self.dense(self.wout_d[l, blk], KC, lambda kk, tb: self.mgT[:, kk, tb * TB:(tb + 1) * TB], self.b_mgT, banks)
            for tb in range(NTB):
                sl = slice(tb * TB, (tb + 1) * TB)
                self.O("dve", "tensor_tensor", [self.b_ps[banks[tb]], self.b_xT[blk]], [self.b_xT[blk]], out=self.xT[:, blk, sl], in0=self.ps[banks[tb]][:, 0:TB],
                       in1=self.xT[:, blk, sl], op=ALU.add)

    def ffn(self, l):
        P = self.P
        HW, SW = 516, 258
        actT, _ = self.carve(self.R_ABC, 0, [128, FC, HW], BF16)
        wfo, _ = self.carve(self.R_D, 0, [128, FC, 128], BF16)
        sg = [self.carve(self.R_X, 1548 + 258 * i, [128, SW], F32)[0] for i in range(2)]
        b_act = [Buf("act%d" % f) for f in range(FC)]
        b_wfo, b_sg = Buf("wfo"), [Buf("sgf0"), Buf("sgf1")]
        for half in range(2):
            h0 = half * HW
            for f in range(FC):
                bset = [0, 1, 2, 3] if f % 2 == 0 else [4, 5, 6, 7]
                rhs = lambda kk, tb, h0=h0: self.hT[:, kk, h0 + tb * SW:h0 + (tb + 1) * SW]
                self.dense(self.wfi_d[l, f], KC, rhs, [self.b_hT], bset[0:2], ncols=2, width=SW)
                self.dense(self.wfi_d[l, FC + f], KC, rhs, [self.b_hT], bset[2:4], ncols=2, width=SW)
                for tb in range(2):
                    self.O("act", "activation", [self.b_ps[bset[tb]]], [b_sg[tb]], out=sg[tb], in_=self.ps[bset[tb]][:, 0:SW], func=AF.Silu)
                    self.O("dve", "tensor_tensor", [self.b_ps[bset[2 + tb]], b_sg[tb]], [b_act[f]], out=actT[:, f, tb * SW:(tb + 1) * SW], in0=self.ps[bset[2 + tb]][:, 0:SW],
                           in1=sg[tb], op=ALU.mult)
            for blk in range(16):
                banks = [0, 1] if blk % 2 == 0 else [2, 3]
                for k0, nk in ((0, 16), (16, 16), (32, 12)):
                    P.dma("pool", lambda e, blk=blk, k0=k0, nk=nk: e.dma_start(out=wfo[:, k0:k0 + nk, :], in_=self.wfo_d[l, blk][:, k0:k0 + nk, :]), [], [b_wfo])
                for tb in range(2):
                    for kk in range(FC):
                        self.mm(self.ps[banks[tb]][:, 0:SW], wfo[:, kk, :], actT[:, kk, tb * SW:(tb + 1) * SW], kk == 0, kk == FC - 1,
                                [b_wfo, b_act[kk]], [self.b_ps[banks[tb]]])
                for tb in range(2):
                    sl = slice(h0 + tb * SW, h0 + (tb + 1) * SW)
                    self.O("dve", "tensor_tensor", [self.b_ps[banks[tb]], self.b_xT[blk]], [self.b_xT[blk]], out=self.xT[:, blk, sl], in0=self.ps[banks[tb]][:, 0:SW],
                           in1=self.xT[:, blk, sl], op=ALU.add)
            self.barrier()

    def setup_bias_s(self):
        self.EBs = self.nc.dram_tensor("EBs", [8, 128, 516], BF16)
        self.b_EBs = Buf("EBs")
        o = 0
        acc, o = self.carve(self.R_h, o, [128, 8, 516], F32)
        bk, o = self.carve(self.R_h, o, [128, 516], F32)
        mb, o = self.carve(self.R_h, o, [128, 516], F32)
        tb_, o = self.carve(self.R_h, o, [128, 256], F32)
        neg, o = self.carve(self.R_h, o, [128, 8], F32)
        eb = [None, None]
        eb[0], o = self.carve(self.R_h, o, [128, 516], BF16)
        eb[1], o = self.carve(self.R_h, o, [128, 516], BF16)
        b_acc, b_bk, b_tb, b_mb, b_neg = Buf("acc"), Buf("bk"), Buf("tb"), Buf("mb"), Buf("neg")
        b_eb = [Buf("eb0"), Buf("eb1")]
        self.DMA("sp", [], [b_bk], bk, self.bks_d.rearrange("p a b -> p (a b)"))
        self.DMA("sp", [], [b_tb], tb_, self.rel_d)
        for b in range(32):
            self.O("dve", "tensor_scalar", [b_bk], [b_mb], out=mb, in0=bk, scalar1=float(b), scalar2=None, op0=ALU.is_equal)
            for h in range(8):
                col = tb_[:, b * 8 + h:b * 8 + h + 1]
                if b == 0:
                    self.O("dve", "tensor_scalar", [b_mb, b_tb], [b_acc], out=acc[:, h, :], in0=mb, scalar1=col, scalar2=None, op0=ALU.mult)
                else:
                    self.O("dve", "scalar_tensor_tensor", [b_mb, b_tb, b_acc], [b_acc], out=acc[:, h, :], in0=mb, scalar=col,
                           in1=acc[:, h, :], op0=ALU.mult, op1=ALU.add)
        self.O("dve", "tensor_scalar", [b_tb], [b_neg], out=neg, in0=tb_[:, 248:256], scalar1=-1.0, scalar2=None, op0=ALU.mult)
        for h in range(8):
            self.O("act", "activation", [b_acc, b_neg], [b_eb[h % 2]], out=eb[h % 2], in_=acc[:, h, :], func=AF.Exp, bias=neg[:, h:h + 1], scale=1.0)
            self.DMA("sp", [b_eb[h % 2]], [self.b_EBs], self.EBs.ap()[h], eb[h % 2])

    def attn_sample(self, l, n_iter=16):
        P = self.P
        X, RD, RH = self.R_X, self.R_D, self.R_h
        scols = slice(NTOK, NT)
        r_ = [self.carve(X, 1548 + 512 * i, [128, 8, 64], F32)[0] for i in range(2)]
        Wb, _ = self.carve(X, 2572, [128, 64], F32)
        Z, _ = self.carve(X, 2636, [128, 16, 4], F32)
        iqs, _ = self.carve(X, 2700, [128, 16, 4], BF16)
        tiny, _ = self.carve(X, 2732, [128, 40], F32)
        idxi, _ = self.carve(X, 2772, [128, 128], I32)
        Jf, _ = self.carve(X, 2900, [128, 128], F32)
        ptf, _ = self.carve(X, 3028, [128, 2], F32)
        pti, _ = self.carve(X, 3030, [128, 1], I32)
        ebt = [self.carve(X, 3036 + 258 * i, [128, 4, 129], BF16)[0] for i in range(2)]
        kxTn, _ = self.carve(X, 3552, [128, 128], BF16)
        cms, _ = self.carve(X, 3616, [128, 4], F32)
        dg, _ = self.carve(X, 3620, [128, 8], F32)
        oas, _ = self.carve(X, 3628, [128, 1024], BF16)
        rcp, _ = self.carve(X, 4140, [128, 8], F32)
        sc, _ = self.carve(RD, 0, [128, 4, 129], F32)
        cmpj, _ = self.carve(RD, 516, [128, 4, 129], BF16)
        mS, _ = self.carve(RD, 774, [128, 4, 129], BF16)
        mEs, _ = self.carve(RD, 1032, [128, 129, 8, 4], BF16)
        kxT8 = [self.carve(RD, 3096 + 512 * i, [128, 8, 128], BF16)[0] for i in range(2)]
        kx, _ = self.carve(RH, 0, [128, 128, 64], F32)
        lo, hi, mid, part, ge, t1, t2 = [tiny[:, 4 * i:4 * i + 4] for i in range(7)]
        pmn = tiny[:, 28:36]
        b = {n: Buf(n) for n in ("Wb", "Z", "iqs", "tiny", "idx", "J", "pt", "kxTn", "cms", "dg", "oas", "rcp", "sc", "cmpj", "mS", "mEs", "kx")}
        b_r, b_ebt, b_kxT8 = [Buf("r0"), Buf("r1")], [Buf("ebt0"), Buf("ebt1")], [Buf("kxT80"), Buf("kxT81")]
        self.DMA("sp", [], [b["pt"]], pti, self.pt_d)
        P.op("pool", lambda e: e.iota(idxi, pattern=[[1, 128]], base=0, channel_multiplier=0), [], [b["idx"]])
        self.O("dve", "tensor_copy", [b["idx"]], [b["J"]], out=Jf, in_=idxi)
        self.O("dve", "tensor_copy", [b["pt"]], [b["pt"]], out=ptf[:, 0:1], in_=pti)
        self.O("dve", "tensor_scalar", [b["pt"]], [b["pt"]], out=ptf[:, 1:2], in0=ptf[:, 0:1], scalar1=128.0, scalar2=None, op0=ALU.mult)
        self.O("dve", "tensor_scalar", [b["J"], b["pt"]], [b["J"]], out=Jf, in0=Jf, scalar1=ptf[:, 1:2], scalar2=None, op0=ALU.add)
        self.O("dve", "tensor_copy", [b["J"]], [b["idx"]], out=idxi, in_=Jf)
        idxu = idxi.bitcast(U32)
        ptu = pti.bitcast(U32)
        P.dma("pool", lambda e: e.indirect_dma_start(out=kx.rearrange("p a b -> p (a b)"), out_offset=None, in_=self.cki_d[l],
                                                     in_offset=bass.IndirectOffsetOnAxis(ap=ptu[:, 0:1], axis=0)), [b["pt"]], [b["kx"]])
        self.O("dve", "tensor_copy", self.b_iqT, [b["iqs"]], out=iqs[0:64, :, :].rearrange("p (a two) t -> p a two t", two=2)[:, :, 0, :], in_=self.iqT[0:64, :, scols])
        self.DMA("sp", self.b_iqT, [b["iqs"]], iqs[0:64, :, :].rearrange("p (a two) t -> p a two t", two=2)[:, :, 1, :], self.iqT[64:128, :, scols])
        self.O("dve", "tensor_tensor", [self.b_iw, self.b_cst], [b["Z"]], out=Z[0:NS, :, :], in0=self.iw_tok[0:NS, 8, :].unsqueeze(2).broadcast_to([NS, 16, 4]),
               in1=self.ident_f[0:NS, 0:NS].unsqueeze(1).broadcast_to([NS, 16, 4]), op=ALU.mult)
        self.mm(self.ps[6][:, 0:64], self.ones_f[0:NS, :], Z[0:NS, :, :].rearrange("p a b -> p (a b)"), True, True, [self.b_cst, b["Z"]], [self.b_ps[6]])
        self.O("dve", "tensor_copy", [self.b_ps[6]], [b["Wb"]], out=Wb, in_=self.ps[6][:, 0:64])
        self.DMA("sp", [], [b["cms"]], cms, self.cms_d)
        iq2 = iqs[0:64, :, :].rearrange("p a b -> p (a b)")
        Wb3 = Wb.unsqueeze(1).broadcast_to([128, 8, 64])

        def score_group(g, ng, src_of, pbank):
            rr = r_[g % 2]
            for jj in range(ng):
                lhsT, bl = src_of(jj)
                self.mm(self.ps[pbank][:, jj * 64:(jj + 1) * 64], lhsT, iq2, True, True, [bl, b["iqs"]], [self.b_ps[pbank]])
            self.O("act", "activation", [self.b_ps[pbank]], [b_r[g % 2]], out=rr[:, 0:ng, :], in_=self.ps[pbank][:, 0:ng * 64].rearrange("p (a b) -> p a b", b=64), func=AF.Relu)
            self.O("dve", "tensor_tensor", [b_r[g % 2], b["Wb"]], [b_r[g % 2]], out=rr[:, 0:ng, :], in0=rr[:, 0:ng, :], in1=Wb3[:, 0:ng, :], op=ALU.mult)

        for g in range(16):
            j0 = g * 8
            for q in range(2):
                bank = (g % 2) * 2 + q
                for jj in range(4):
                    j = j0 + q * 4 + jj
                    self.tr(self.ps[bank][0:64, jj * 128:(jj + 1) * 128], kx[:, j, :], self.ident_f, [b["kx"], self.b_cst], [self.b_ps[bank]])
                self.O("act" if q == 0 else "dve", "copy" if q == 0 else "tensor_copy", [self.b_ps[bank]], [b_kxT8[g % 2]], out=kxT8[g % 2][0:64, q * 4:(q + 1) * 4, :],
                       in_=self.ps[bank][0:64, 0:512].rearrange("p (a b) -> p a b", a=4))
            score_group(g, 8, lambda jj, g=g: (kxT8[g % 2][0:64, jj, :], b_kxT8[g % 2]), 4 + (g % 2))
            rr = r_[g % 2]
            self.O("dve", "tensor_reduce", [b_r[g % 2]], [b["sc"]], out=sc[:, :, j0:j0 + 8].rearrange("p t j -> p j t"),
                   in_=rr.rearrange("p j (h t) -> p j t h", t=4), axis=AX.X, op=ALU.add)
        self.O("pool", "memset", [], [b["kxTn"]], kxTn, 0.0)
        ikn, _ = self.carve(X, 4148, [128, NS], BF16)
        b_ikn = Buf("ikn")
        self.O("act", "copy", [self.b_small], [b_ikn], out=ikn[64:128, :], in_=self.smallT[64:128, scols])
        self.DMA("sp", [b_ikn, b["kxTn"]], [b["kxTn"]], kxTn[0:64, 0:NS], ikn[64:128, :])
        score_group(0, 1, lambda jj: (kxTn[0:64, :], b["kxTn"]), 4)
        self.O("dve", "tensor_reduce", [b_r[0]], [b["sc"]], out=sc[:, :, 128:129].rearrange("p t j -> p j t"),
               in_=r_[0][:, 0:1, :].rearrange("p j (h t) -> p j t h", t=4), axis=AX.X, op=ALU.add)
        self.O("dve", "tensor_reduce", [b["sc"]], [b["tiny"]], out=pmn[:, 0:4], in_=sc, axis=AX.X, op=ALU.min)
        self.O("dve", "tensor_tensor", [b["sc"], b["cms"]], [b["sc"]], out=sc[:, :, 128], in0=sc[:, :, 128], in1=cms, op=ALU.add)
        self.O("dve", "tensor_reduce", [b["sc"]], [b["tiny"]], out=pmn[:, 4:8], in_=sc, axis=AX.X, op=ALU.max, negate=True)
        self.tr(self.ps[6][0:8, 0:128], pmn, self.ident_f, [b["tiny"], self.b_cst], [self.b_ps[6]])
        self.O("dve", "tensor_reduce", [self.b_ps[6]], [b["dg"]], out=dg[0:8, 0:1], in_=self.ps[6][0:8, 0:128], axis=AX.X, op=ALU.min)
        dgm, _ = self.carve(X, 4156, [128, 8], F32)
        b_dgm = Buf("dgm")
        self.O("dve", "tensor_scalar", [b["dg"], self.b_cst], [b_dgm], out=dgm[0:8, :], in0=self.ident_f[0:8, 0:8], scalar1=dg[0:8, 0:1], scalar2=None, op0=ALU.mult)
        self.mm(self.ps[6][:, 128:136], self.ones_f[0:8, :], dgm[0:8, :], True, True, [self.b_cst, b_dgm], [self.b_ps[6]])
        self.O("dve", "tensor_copy", [self.b_ps[6]], [b["tiny"]], out=lo, in_=self.ps[6][:, 128:132])
        self.O("dve", "tensor_scalar", [self.b_ps[6]], [b["tiny"]], out=hi, in0=self.ps[6][:, 132:136], scalar1=-1.0, scalar2=None, op0=ALU.mult)
        bt = [b["tiny"]]
        for it in range(n_iter):
            self.O("dve", "tensor_tensor", bt, bt, out=mid, in0=lo, in1=hi, op=ALU.add)
            self.O("dve", "tensor_scalar", bt, bt, out=mid, in0=mid, scalar1=0.5, scalar2=None, op0=ALU.mult)
            self.O("dve", "tensor_tensor", [b["sc"]] + bt, [b["cmpj"]], out=cmpj, in0=sc, in1=mid.unsqueeze(2).broadcast_to([128, 4, 129]), op=ALU.is_ge)
            self.O("dve", "tensor_reduce", [b["cmpj"]], bt, out=part, in_=cmpj, axis=AX.X, op=ALU.add)
            self.mm(self.ps[7][:, 0:4], self.ones_f, part, True, True, [self.b_cst] + bt, [self.b_ps[7]])
            self.O("dve", "tensor_scalar", [self.b_ps[7]], bt, out=ge, in0=self.ps[7][:, 0:4], scalar1=float(TOPK) - 0.5, scalar2=None, op0=ALU.is_ge)
            self.O("dve", "tensor_tensor", bt, bt, out=t1, in0=mid, in1=lo, op=ALU.subtract)
            self.O("dve", "tensor_tensor", bt, bt, out=t1, in0=t1, in1=ge, op=ALU.mult)
            self.O("dve", "tensor_tensor", bt, bt, out=t2, in0=hi, in1=mid, op=ALU.subtract)
            self.O("dve", "tensor_tensor", bt, bt, out=t2, in0=t2, in1=ge, op=ALU.mult)
            self.O("dve", "tensor_tensor", bt, bt, out=lo, in0=lo, in1=t1, op=ALU.add)
            self.O("dve", "tensor_tensor", bt, bt, out=hi, in0=mid, in1=t2, op=ALU.add)
        self.O("dve", "tensor_tensor", [b["sc"]] + bt, [b["mS"]], out=mS, in0=sc, in1=lo.unsqueeze(2).broadcast_to([128, 4, 129]), op=ALU.is_ge)
        for h in range(8):
            self.DMA("sp", [self.b_EBs], [b_ebt[h % 2]], ebt[h % 2].rearrange("p a b -> p (a b)"), self.EBs.ap()[h])
            self.O("dve", "tensor_tensor", [b["mS"], b_ebt[h % 2]], [b["mEs"]], out=mEs[:, :, h, :], in0=mS.rearrange("p t j -> p j t"),
                   in1=ebt[h % 2].rearrange("p t j -> p j t"), op=ALU.mult)
        self.barrier()
        Kj = [self.carve(RH, 1024 * i, [128, 1024], F32)[0] for i in range(2)]
        Vj = [self.carve(RH, 2048 + 1024 * i, [128, 1024], F32)[0] for i in range(2)]
        Kb, _ = self.carve(RH, 4096, [128, 1024], BF16)
        KTj = [self.carve(RH, 4608 + 512 * i, [128, 8, 128], BF16)[0] for i in range(2)]
        Va = [self.carve(RH, 5632 + 516 * i, [128, 8, 129], BF16)[0] for i in range(2)]
        pT = [self.carve(RH, 6664 + 16 * i, [128, 8, 4], BF16)[0] for i in range(2)]
        accO, _ = self.carve(RH, 6700, [128, 8, 129], F32)
        b_Kj, b_Vj, b_KTj, b_Va, b_pT = [Buf("Kj0"), Buf("Kj1")], [Buf("Vj0"), Buf("Vj1")], [Buf("KTj0"), Buf("KTj1")], [Buf("Va0"), Buf("Va1")], [Buf("pT0"), Buf("pT1")]
        b_Kb, b_acc = Buf("Kb"), Buf("accO")
        for i in range(2):
            self.O("pool", "memset", [], [b_Va[i]], Va[i][:, :, 128:129], 1.0)
        sc_ = 128.0 ** -0.5
        for j in range(129):
            s = j % 2
            if j < 128:
                P.dma("pool", lambda e, j=j, s=s: e.indirect_dma_start(out=Kj[s], out_offset=None, in_=self.ck_d[l],
                                                                       in_offset=bass.IndirectOffsetOnAxis(ap=idxu[:, j:j + 1], axis=0)), [b["idx"]], [b_Kj[s]])
                P.dma("pool", lambda e, j=j, s=s: e.indirect_dma_start(out=Vj[s], out_offset=None, in_=self.cv_d[l],
                                                                       in_offset=bass.IndirectOffsetOnAxis(ap=idxu[:, j:j + 1], axis=0)), [b["idx"]], [b_Vj[s]])
                self.O("act", "copy", [b_Kj[s]], [b_Kb], out=Kb, in_=Kj[s])
                bank = s
                for h in range(8):
                    self.tr(self.psb[bank][:, h * 128:(h + 1) * 128], Kb[:, h * 128:(h + 1) * 128], self.ident_b, [b_Kb, self.b_cst], [self.b_ps[bank]])
                self.O("dve", "tensor_copy", [self.b_ps[bank]], [b_KTj[s]], out=KTj[s], in_=self.psb[bank][:, 0:1024].rearrange("p (a b) -> p a b", a=8))
                for h in range(8):
                    self.mm(self.ps[2][:, h * 4:(h + 1) * 4], KTj[s][:, h, :], self.QT[:, h, scols], True, True, [b_KTj[s], self.b_QT[h]], [self.b_ps[2]])
                np_ = 128
                self.O("pool", "tensor_copy", [b_Vj[s]], [b_Va[s]], out=Va[s][:, :, 0:128], in_=Vj[s].rearrange("p (a b) -> p a b", a=8))
                vsrc, bv = Va[s], b_Va[s]
            else:
                for h in range(8):
                    self.mm(self.ps[2][0:NS, h * 4:(h + 1) * 4], self.KTs[:, h, :], self.QT[:, h, scols], True, True, [self.b_KTs, self.b_QT[h]], [self.b_ps[2]])
                np_ = NS
                vsrc, bv = self.Vs, self.b_Vs
            self.O("act", "activation", [self.b_ps[2]], [b_pT[s]], out=pT[s][0:np_], in_=self.ps[2][0:np_, 0:32].rearrange("p (a b) -> p a b", b=4), func=AF.Exp, scale=sc_)
            self.O("dve", "tensor_tensor", [b_pT[s], b["mEs"]], [b_pT[s]], out=pT[s][0:np_], in0=pT[s][0:np_], in1=mEs[0:np_, j, :, :], op=ALU.mult)
            for h in range(8):
                bk_, col = 3 + h // 3, (h % 3) * 129
                self.mm(self.ps[bk_][0:NS, col:col + 129], pT[s][0:np_, h, :], vsrc[0:np_, h, :], True, True, [b_pT[s], bv], [self.b_ps[bk_]])
            for gq in range(3):
                nh = 3 if gq < 2 else 2
                src = self.ps[3 + gq][0:NS, 0:nh * 129].rearrange("p (a b) -> p a b", b=129)
                dst = accO[0:NS, gq * 3:gq * 3 + nh, :]
                if j == 0:
                    self.O("dve", "tensor_copy", [self.b_ps[3 + gq]], [b_acc], out=dst, in_=src)
                else:
                    self.O("dve", "tensor_tensor", [self.b_ps[3 + gq], b_acc], [b_acc], out=dst, in0=src, in1=dst, op=ALU.add)
        self.O("dve", "reciprocal", [b_acc], [b["rcp"]], out=rcp[0:NS, :], in_=accO[0:NS, :, 128])
        self.O("dve", "tensor_tensor", [b_acc, b["rcp"]], [b["oas"]], out=oas[0:NS, :].rearrange("p (a b) -> p a b", a=8), in0=accO[0:NS, :, 0:128],
               in1=rcp[0:NS, :].unsqueeze(2).broadcast_to([NS, 8, 128]), op=ALU.mult)
        for ch in range(8):
            self.tr(self.psb[6][:, ch * 4:(ch + 1) * 4], oas[0:NS, ch * 128:(ch + 1) * 128], self.ident_b[0:NS, 0:NS], [b["oas"], self.b_cst], [self.b_ps[6]])
        self.O("act", "copy", [self.b_ps[6]], [self.b_oaT], out=self.oaT[:, :, scols], in_=self.psb[6][:, 0:32].rearrange("p (a b) -> p a b", a=8))
        self.barrier()


def _bucket(n):
    n = max(int(n), 0)
    if n < 16:
        return n
    v = 16 + int(np.float32(np.log(np.float32(n) / np.float32(16.0))) / np.float32(np.log(8.0)) * np.float32(16.0))
    return min(v, 31)


_BUCKETS = np.array([_bucket(n) for n in range(0, 20000)], np.float32)


def _const_tables(c):
    t = {}
    s_idx = np.arange(128)[:, None]
    t_idx = np.arange(128)[None, :]
    cm = np.zeros((128, 8, 128), np.float32)
    for r in range(8):
        if r > c:
            cm[:, r, :] = NEG
        elif r == c:
            cm[:, r, :] = np.where(np.arange(128)[None, :] <= np.arange(128)[:, None], 0.0, NEG)
    t["cmask"] = cm
    coef = np.zeros((128, 32), np.float32)
    for idx in range(9):
        coef[:, idx] = 1.0 if (idx - 1) == c else 0.0
        coef[:, 9 + idx] = 1.0 if idx == c else 0.0
    for r in range(8):
        coef[:, 18 + r] = 1.0 if r == c else 0.0
    t["coef"] = coef
    dist = np.arange(256)[None, :] - np.arange(128)[:, None]
    t["bkt"] = _BUCKETS[np.maximum(dist, 0)].astype(np.float32)
    cst = np.zeros((128, 4, 128), np.float32)
    cst[:, 0, :] = np.eye(128)
    cst[:, 1, :] = (s_idx <= t_idx)
    partner = np.array([(m + 32) if (m % 64) < 32 else (m - 32) for m in range(128)])
    perm = np.zeros((128, 128), np.float32)
    perm[partner, np.arange(128)] = 1.0
    cst[:, 2, :] = perm
    cst[:, 3, :] = 1.0
    t["cst"] = cst
    rt = np.zeros((9, 128, 4, 2, 128), np.float32)
    half = 32
    freqs = (np.float32(10000.0) ** (-np.arange(half, dtype=np.float32) / np.float32(half))).astype(np.float32)
    p = np.arange(128)
    dd = p % 64
    fi = freqs[dd % 32]
    sgn = np.where(dd < 32, -1.0, 1.0)
    for lt in range(9):
        if lt < 8:
            m = lt % 4
            pos = (8 * m + c) * 128 + np.arange(128)
        else:
            pos = PAST + np.arange(128)
        ang = (pos.astype(np.float32)[None, :] * fi[:, None]).astype(np.float32)
        cos = np.cos(ang).astype(np.float64)
        sin = np.sin(ang).astype(np.float64) * sgn[:, None]
        for blk in range(2):
            hh = blk * 2 + p // 64
            lg = np.log1p(-np.exp2(-5.0 - hh.astype(np.float64)))
            tt = (np.arange(128) + 1).astype(np.float64)
            gq = np.exp(lg[:, None] * tt[None, :])
            gk = np.exp(-lg[:, None] * tt[None, :]) * (64.0 ** -0.5)
            rt[lt, :, 0, blk, :] = cos * gq
            rt[lt, :, 1, blk, :] = sin * gq
            rt[lt, :, 2, blk, :] = cos * gk
            rt[lt, :, 3, blk, :] = sin * gk
    t["rtab"] = rt
    lgh = np.log1p(-np.exp2(-5.0 - np.arange(4, dtype=np.float64)))
    dret = np.zeros((64, 2, 4), np.float32)
    dret[:, 0, :] = np.exp(lgh * 128.0)[None, :]
    dret[:, 1, :] = np.exp(lgh * 4.0)[None, :]
    t["dret"] = dret
    bks = np.zeros((128, 4, 129), np.float32)
    slot = np.arange(128)[:, None, None]
    tq = np.arange(4)[None, :, None]
    jj = np.arange(128)[None, None, :]
    d_past = (PAST + tq) - (slot * 128 + jj)
    bks[:, :, 0:128] = _BUCKETS[np.maximum(d_past, 0)]
    d_new = (tq - slot)[:, :, 0]
    bks[:, :, 128] = _BUCKETS[np.clip(d_new, 0, 19999)]
    t["bkt_s"] = bks
    cms = np.full((128, 4), NEG, np.float32)
    for n in range(4):
        for tq_ in range(4):
            if n <= tq_:
                cms[n, tq_] = 0.0
    t["cmask_s"] = cms
    return t


_WCACHE = {}


def _prep_shared(inputs, use_cache):
    f = lambda a: np.asarray(a, dtype=np.float32)
    sh = {}
    w_in = f(inputs["w_in"])
    win = np.empty((DEPTH, NWB, 128, KC, 128), np.float32)
    for l in range(DEPTH):
        for j, (_n, _i, cols) in enumerate(WIN_BLOCKS):
            win[l, j] = _wblk(w_in[l], cols)
    sh["win"] = win
    ar = lambda j: np.arange(j * 128, (j + 1) * 128)
    sh["wbr"] = np.stack([np.stack([_wblk(f(inputs["w_branch"])[l], ar(j)) for j in range(16)]) for l in range(DEPTH)])
    sh["wout"] = np.stack([np.stack([_wblk(f(inputs["w_out"])[l], ar(j)) for j in range(16)]) for l in range(DEPTH)])
    sh["wfi"] = np.stack([np.stack([_wblk(f(inputs["w_ffn_in"])[l], ar(j)) for j in range(2 * FC)]) for l in range(DEPTH)])
    sh["wfo"] = np.stack([np.stack([_wblk(f(inputs["w_ffn_out"])[l], ar(j)) for j in range(16)]) for l in range(DEPTH)])
    vecs = np.zeros((128, DEPTH, 36), np.float32)
    for l in range(DEPTH):
        vecs[:, l, 0:16] = _pvec(f(inputs["norm_mix"])[l])
        vecs[:, l, 16:32] = _pvec(f(inputs["norm_ffn"])[l])
        vecs[:, l, 32] = f(inputs["a_q_norm"])[l]
        vecs[:, l, 33] = f(inputs["a_k_norm"])[l]
        vecs[:, l, 34] = f(inputs["gla_norm"])[l]
        vecs[:, l, 35] = f(inputs["ret_norm"])[l]
    sh["vecs"] = vecs
    sh["gla_wa"] = np.ascontiguousarray(f(inputs["gla_wa"]))
    sh["gla_ba"] = np.ascontiguousarray(f(inputs["gla_ba"]).reshape(DEPTH, 1, 256))
    sh["rel_bc"] = np.ascontiguousarray(np.broadcast_to(f(inputs["rel_table"]).reshape(1, 256), (128, 256)))
    if use_cache:
        ck = f(inputs["cache_k"]).reshape(DEPTH, NPOOL * 128, 1024)
        cv = f(inputs["cache_v"]).reshape(DEPTH, NPOOL * 128, 1024)
        cki = f(inputs["cache_kidx"]).reshape(DEPTH, NPOOL, 128 * 64)
        for l in range(DEPTH):
            sh["cache_k%d" % l] = ck[l]
            sh["cache_v%d" % l] = cv[l]
            sh["cache_kidx%d" % l] = cki[l]
    return sh


def _prep_core(inputs, c, sh, use_cache):
    xp = np.asarray(inputs["x_prompt"], np.float32)
    xs = np.asarray(inputs["x_sample"], np.float32)
    X = np.zeros((NTP, D), np.float32)
    for b in range(BATCH):
        for m in range(4):
            lt = b * 4 + m
            g = 8 * m + c
            X[lt * 128:(lt + 1) * 128] = xp[b, g * 128:(g + 1) * 128]
    X[NTOK:NT] = xs[c]
    d = dict(sh)
    d["xT"] = np.ascontiguousarray(X.T.reshape(KC, 128, NTP).transpose(1, 0, 2))
    ct = _const_tables(c)
    for k in ("cmask", "coef", "bkt", "rtab", "dret", "cst"):
        d[k] = ct[k]
    if use_cache:
        d["bkt_s"] = ct["bkt_s"]
        d["cmask_s"] = ct["cmask_s"]
        d["ptab"] = np.ascontiguousarray(np.asarray(inputs["page_table"], np.int32)[c].reshape(128, 1))
    d["s_gla"] = np.ascontiguousarray(np.asarray(inputs["state_gla"], np.float32)[:, c])
    d["s_ret"] = np.ascontiguousarray(np.asarray(inputs["state_ret"], np.float32)[:, c])
    return d


def _assemble(res):
    y_p = np.zeros((BATCH, SEQ, D), np.float32)
    y_s = np.zeros((NCORES, NS, D), np.float32)
    k_p = np.zeros((DEPTH, BATCH, SEQ, 8, 128), np.float32)
    v_p = np.zeros((DEPTH, BATCH, SEQ, 8, 128), np.float32)
    ki_p = np.zeros((DEPTH, BATCH, SEQ, 64), np.float32)
    k_s = np.zeros((DEPTH, NCORES, NS, 8, 128), np.float32)
    v_s = np.zeros((DEPTH, NCORES, NS, 8, 128), np.float32)
    ki_s = np.zeros((DEPTH, NCORES, NS, 64), np.float32)
    gla_s = np.zeros((DEPTH, NCORES, 4, 64, 128), np.float32)
    ret_s = np.zeros((DEPTH, NCORES, 4, 64, 128), np.float32)
    for c in range(NCORES):
        r = res[c]
        Y = r["yT"].transpose(2, 1, 0).reshape(NTP, D)
        kT = r["kT"].transpose(0, 3, 2, 1)
        vT = r["vT"].transpose(0, 3, 2, 1)
        kiT = r["kiT"].transpose(0, 2, 1)
        for b in range(BATCH):
            for m in range(4):
                lt = b * 4 + m
                g = 8 * m + c
                y_p[b, g * 128:(g + 1) * 128] = Y[lt * 128:(lt + 1) * 128]
                k_p[:, b, g * 128:(g + 1) * 128] = kT[:, lt * 128:(lt + 1) * 128]
                v_p[:, b, g * 128:(g + 1) * 128] = vT[:, lt * 128:(lt + 1) * 128]
                ki_p[:, b, g * 128:(g + 1) * 128] = kiT[:, lt * 128:(lt + 1) * 128]
        y_s[c] = Y[NTOK:NT]
        k_s[:, c] = kT[:, NTOK:NT]
        v_s[:, c] = vT[:, NTOK:NT]
        ki_s[:, c] = kiT[:, NTOK:NT]
        gla_s[:, c] = r["gla_s"]
        ret_s[:, c] = r["ret_s"]
    gla_p = np.ascontiguousarray(res[0]["gla_p"])
    ret_p = np.ascontiguousarray(res[0]["ret_p"])
    return (y_p, y_s, k_p, v_p, ki_p, gla_p, ret_p, k_s, v_s, ki_s, gla_s, ret_s)


def build(stages="all", use_cache=True, use_state=None):
    B = Builder(stages, use_cache, use_state)
    B.setup()
    st = stages
    B.setup_bias()
    B.barrier()
    if use_cache:
        B.setup_bias_s()
        B.barrier()
    for l in range(DEPTH):
        B.rmsnorm_x(l, 0)
        B.barrier()
        B.phase_kv(l)
        B.phase_q(l)
        if st == "kvq":
            break
        B.barrier()
        if st != "linonly":
            B.attn_prompt(l)
            if use_cache:
                B.attn_sample(l)
        if st == "attn":
            dbg, _ = B.carve(B.R_h, 0, [128, 8, NTP], F32)
            b_dbg = Buf("dbg")
            B.O("dve", "tensor_copy", [B.b_oaT], [b_dbg], out=dbg, in_=B.oaT)
            B.DMA("sp", [b_dbg], [], B.yT_o[:, 0:8, :], dbg, is_output=True)
            break
        if st.startswith("lin"):
            pass
        B.rmsnorm_x(l, 0)
        B.barrier()
        B.lin_attn(l, 0)
        B.lin_attn(l, 1)
        if st.startswith("lin"):
            dbg, _ = B.carve(B.R_h, 0, [128, 8, NTP], F32)
            b_dbg = Buf("dbg")
            B.O("dve", "tensor_copy", B.b_obT, [b_dbg], out=dbg, in_=B.obT)
            B.DMA("sp", [b_dbg], [], B.yT_o[:, 8:16, :], dbg, is_output=True)
            break
        B.merge(l)
        B.barrier()
        B.out_proj(l)
        B.barrier()
        B.rmsnorm_x(l, 1)
        B.barrier()
        B.ffn(l)
    if st in ("all", "nosample"):
        B.DMA("sp", B.b_xT, [], B.yT_o, B.xT[:], is_output=True)
    B.P.finish()
    B.P.emit()
    return B


def run(inputs, stages="all", use_cache=True, trace=False, use_state=None):
    B = build(stages, use_cache, use_state)
    sh = _prep_shared(inputs, use_cache)
    in_maps = [_prep_core(inputs, c, sh, use_cache) for c in range(NCORES)]
    for m in in_maps:
        for k in list(m.keys()):
            if k not in B.din:
                del m[k]
    r = run_bass_kernel_spmd(B.nc, in_maps, core_ids=list(range(NCORES)), trace=trace)
    return r, B


def kernel(**inputs):
    r, B = run(inputs, "all", True)
    return _assemble(r.results)
```
